# Optimizing a Trainium2 kernel written in Bass

```python
import math
import jax, jax.numpy as jnp
from jax import lax
import numpy as np

D_MODEL = 1024
BATCH = 16
SEQ = 2048
DEPTH = 1

GMLP_WIDTH = 512
GMLP_GROUPS = 4
GMLP_GROUP_DIM = GMLP_WIDTH // GMLP_GROUPS
CHUNK = 128

ATTN_PATTERNS = ((128, 1), (512, 4), (2048, 16))
N_ATTN_GROUPS = len(ATTN_PATTERNS)
HEADS_PER_GROUP = 4
HEAD_DIM = 128
ATTN_GROUP_WIDTH = HEADS_PER_GROUP * HEAD_DIM
ATTN_WIDTH = N_ATTN_GROUPS * ATTN_GROUP_WIDTH
ROPE_DIM = HEAD_DIM // 4
ROPE_THETA = 500000.0

N_BRANCHES = 2
D_FF = 4 * D_MODEL
D_IN = 2 * GMLP_WIDTH + 3 * ATTN_WIDTH + N_BRANCHES * D_MODEL
NORM_EPS = 1e-6
MASK_VALUE = -1e30

kernel_name = "hybrid_gmlp_dilated_attn_encoder_block"


def rms_norm(x, gain):
    xf = x.astype(jnp.float32)
    y = xf * lax.rsqrt(jnp.mean(xf * xf, axis=-1, keepdims=True) + NORM_EPS)
    return (y * gain.astype(jnp.float32)).astype(x.dtype)


def layer_norm(x, gain, bias):
    xf = x.astype(jnp.float32)
    mu = jnp.mean(xf, axis=-1, keepdims=True)
    var = jnp.mean(jnp.square(xf - mu), axis=-1, keepdims=True)
    y = (xf - mu) * lax.rsqrt(var + NORM_EPS)
    return (y * gain.astype(jnp.float32) + bias.astype(jnp.float32)).astype(x.dtype)


def partial_rope(t):
    S = t.shape[1]
    half = ROPE_DIM // 2
    inv_freq = ROPE_THETA ** (-jnp.arange(0, ROPE_DIM, 2, dtype=jnp.float32) / ROPE_DIM)
    ang = jnp.arange(S, dtype=jnp.float32)[:, None] * inv_freq[None, :]
    cos = jnp.cos(ang)[None, :, None, :]
    sin = jnp.sin(ang)[None, :, None, :]
    tf = t.astype(jnp.float32)
    x1, x2, rest = tf[..., :half], tf[..., half:ROPE_DIM], tf[..., ROPE_DIM:]
    out = jnp.concatenate([x1 * cos - x2 * sin, x2 * cos + x1 * sin, rest], axis=-1)
    return out.astype(t.dtype)


def dilated_window_attention(q, k, v, dilation, radius):
    B, S, H, Dh = q.shape
    L = S // dilation
    nb = -(-L // radius)
    Lp = nb * radius

    def strided(t):
        return t.astype(jnp.float32).reshape(B, L, dilation, H, Dh).transpose(0, 2, 1, 3, 4)

    qs, ks, vs = strided(q), strided(k), strided(v)
    qb = jnp.pad(qs, ((0, 0), (0, 0), (0, Lp - L), (0, 0), (0, 0))).reshape(B, dilation, nb, radius, H, Dh)

    def windows(t):
        tb = jnp.pad(t, ((0, 0), (0, 0), (radius, Lp - L + radius), (0, 0), (0, 0)))
        tb = tb.reshape(B, dilation, nb + 2, radius, H, Dh)
        return jnp.concatenate([tb[:, :, :-2], tb[:, :, 1:-1], tb[:, :, 2:]], axis=3)

    kw, vw = windows(ks), windows(vs)
    blk = jnp.arange(nb)[:, None, None]
    qpos = blk * radius + jnp.arange(radius)[None, :, None]
    kpos = blk * radius - radius + jnp.arange(3 * radius)[None, None, :]
    valid = (jnp.abs(qpos - kpos) <= radius) & (kpos >= 0) & (kpos < L)

    scale = 1.0 / math.sqrt(Dh)
    s = jnp.einsum('bdnqhe,bdnkhe->bdnhqk', qb, kw) * scale
    s = jnp.where(valid[None, None, :, None], s, MASK_VALUE)
    m = jnp.max(s, axis=-1, keepdims=True)
    p = jnp.exp(s - m)
    denom = jnp.sum(p, axis=-1, keepdims=True)
    o = jnp.einsum('bdnhqk,bdnkhe->bdnhqe', p, vw) / denom
    lse = (m + jnp.log(denom))[..., 0]

    o = o.transpose(0, 1, 2, 4, 3, 5).reshape(B, dilation, Lp, H, Dh)[:, :, :L]
    o = o.transpose(0, 2, 1, 3, 4).reshape(B, S, H, Dh)
    lse = lse.transpose(0, 1, 2, 4, 3).reshape(B, dilation, Lp, H)[:, :, :L]
    lse = lse.transpose(0, 2, 1, 3).reshape(B, S, H)
    return o, lse


def gmlp_spatial_gating(z, ln_gain, ln_bias, w_spatial, b_spatial):
    B, S, _ = z.shape
    u, v = z[..., :GMLP_WIDTH], z[..., GMLP_WIDTH:]
    v = layer_norm(v, ln_gain, ln_bias)
    v = v.reshape(B, S // CHUNK, CHUNK, GMLP_GROUPS, GMLP_GROUP_DIM)
    sv = jnp.einsum('bcsge,gts->bctge', v, w_spatial) + b_spatial.T[None, None, :, :, None]
    return u * sv.reshape(B, S, GMLP_WIDTH)


def setup_inputs(seed: int = 0) -> dict:
    key = jax.random.key(seed)
    ks = jax.random.split(key, 17)
    f32 = jnp.float32

    def nrm(k, shape, scale):
        return jax.random.normal(k, shape, f32) * scale

    def gain(k, shape):
        return 1.0 + 0.05 * jax.random.normal(k, shape, f32)

    return {
        "x": jax.random.normal(ks[0], (BATCH, SEQ, D_MODEL), f32),
        "norm_mix_pre": gain(ks[1], (DEPTH, D_MODEL)),
        "w_in": nrm(ks[2], (DEPTH, D_MODEL, D_IN), D_MODEL ** -0.5),
        "b_gate": nrm(ks[3], (DEPTH, N_BRANCHES * D_MODEL), 0.02),
        "ln_v_gain": gain(ks[4], (DEPTH, GMLP_WIDTH)),
        "ln_v_bias": nrm(ks[5], (DEPTH, GMLP_WIDTH), 0.02),
        "w_spatial": nrm(ks[6], (DEPTH, GMLP_GROUPS, CHUNK, CHUNK), CHUNK ** -0.5),
        "b_spatial": gain(ks[7], (DEPTH, GMLP_GROUPS, CHUNK)),
        "w_branch_a": nrm(ks[8], (DEPTH, GMLP_WIDTH, D_MODEL), GMLP_WIDTH ** -0.5),
        "w_branch_b": nrm(ks[9], (DEPTH, ATTN_GROUP_WIDTH, D_MODEL), ATTN_GROUP_WIDTH ** -0.5),
        "w_out": nrm(ks[10], (DEPTH, D_MODEL, D_MODEL), D_MODEL ** -0.5),
        "norm_mix_post": gain(ks[11], (DEPTH, D_MODEL)),
        "norm_mlp_pre": gain(ks[12], (DEPTH, D_MODEL)),
        "w_up": nrm(ks[13], (DEPTH, D_MODEL, D_FF), D_MODEL ** -0.5),
        "w_down": nrm(ks[14], (DEPTH, D_FF, D_MODEL), D_FF ** -0.5),
        "norm_mlp_post": gain(ks[15], (DEPTH, D_MODEL)),
    }


def reference(x, norm_mix_pre, w_in, b_gate, ln_v_gain, ln_v_bias, w_spatial, b_spatial,
              w_branch_a, w_branch_b, w_out, norm_mix_post, norm_mlp_pre, w_up, w_down,
              norm_mlp_post):
    B, S, D = x.shape
    split_points = list(np.cumsum([GMLP_WIDTH * 2, ATTN_WIDTH, ATTN_WIDTH, ATTN_WIDTH]))
    h = x
    for l in range(DEPTH):
        n = rms_norm(h, norm_mix_pre[l])
        proj = jnp.einsum('bsd,de->bse', n, w_in[l])
        z_gmlp, q, k, v, gates = jnp.split(proj, split_points, axis=-1)

        y_a = gmlp_spatial_gating(jax.nn.gelu(z_gmlp), ln_v_gain[l], ln_v_bias[l],
                                  w_spatial[l], b_spatial[l])

        n_heads = N_ATTN_GROUPS * HEADS_PER_GROUP
        q = partial_rope(q.reshape(B, S, n_heads, HEAD_DIM))
        k = partial_rope(k.reshape(B, S, n_heads, HEAD_DIM))
        v = v.reshape(B, S, n_heads, HEAD_DIM)
        outs, lses = [], []
        for g, (window, dilation) in enumerate(ATTN_PATTERNS):
            sl = slice(g * HEADS_PER_GROUP, (g + 1) * HEADS_PER_GROUP)
            o_g, lse_g = dilated_window_attention(q[:, :, sl], k[:, :, sl], v[:, :, sl],
                                                  dilation, window // (2 * dilation))
            outs.append(o_g)
            lses.append(lse_g)
        o_all = jnp.stack(outs, axis=0)
        w_mix = jax.nn.softmax(jnp.stack(lses, axis=0), axis=0)
        y_b = jnp.sum(w_mix[..., None] * o_all, axis=0).reshape(B, S, ATTN_GROUP_WIDTH).astype(h.dtype)

        g_all = jax.nn.sigmoid((gates + b_gate[l]).astype(jnp.float32)).astype(h.dtype)
        g_a, g_b = g_all[..., :D_MODEL], g_all[..., D_MODEL:]
        merged = (g_a * jnp.einsum('bse,ed->bsd', y_a, w_branch_a[l])
                  + g_b * jnp.einsum('bse,ed->bsd', y_b, w_branch_b[l]))
        mix_out = jnp.einsum('bsd,de->bse', merged, w_out[l])
        h = h + rms_norm(mix_out, norm_mix_post[l])

        n2 = rms_norm(h, norm_mlp_pre[l])
        hid = jnp.square(jax.nn.relu(jnp.einsum('bsd,df->bsf', n2, w_up[l])))
        mlp_out = jnp.einsum('bsf,fd->bsd', hid, w_down[l])
        h = h + rms_norm(mlp_out, norm_mlp_post[l])
    return h
```

```python
import math
import numpy as np
import concourse.bass as bass
import concourse.mybir as mybir
from concourse.bass_utils import run_bass_kernel_spmd

F32 = mybir.dt.float32
BF16 = mybir.dt.bfloat16
AF = mybir.ActivationFunctionType
ALU = mybir.AluOpType

S = 2048
D = 1024
KC = 8
PAD = 256
SP = S + 2 * PAD
EPS = 1e-6
NCORES = 8
SEQ_PER_CORE = 2
SCALE = 1.0 / math.sqrt(128.0)
SAME_ENGINE_SYNC = True
import os as _os
SKIP = set(_os.environ.get('KSKIP', '').split(','))
POOL_DMA_MAX_OUTSTANDING = 2


class Reg:
    __slots__ = ("name", "w", "r", "dsem", "dcnt", "alias", "excl")

    def __init__(self, name):
        self.name = name
        self.excl = False
        self.w = None
        self.r = []
        self.dsem = None
        self.dcnt = 0
        self.alias = []


class Op:
    __slots__ = ("eng", "fn", "waits", "inc", "count", "dreg", "dval")

    def __init__(self, eng, fn):
        self.eng = eng
        self.fn = fn
        self.waits = []
        self.inc = False
        self.count = None
        self.dreg = None
        self.dval = None


class Prog:
    ENGS = ("pe", "act", "dve", "pool", "sp")

    def __init__(self, nc, stack):
        self.nc = nc
        self.stack = stack
        self.ops = {e: [] for e in self.ENGS}
        self.sems = {e: stack.enter_context(nc.semaphore("s_" + e)) for e in ("pe", "act", "dve", "pool")}
        self.nreg = 0

    def reg(self, name, alias=()):
        r = Reg(name)
        r.alias = list(alias)
        return r

    def _dep(self, o, tok):
        if tok is None:
            return
        if tok[0] == "op":
            p = tok[1]
            if p.eng == o.eng and (p.eng in ("pe", "sp") or not SAME_ENGINE_SYNC):
                return
            p.inc = True
        o.waits.append(tok)

    def _deps(self, o, reads, writes):
        for r in reads:
            for a in r.alias:
                self._dep(o, a.w)
            self._dep(o, r.w)
            if r.excl:
                for t in r.r:
                    if t[0] == "op" and t[1].eng != o.eng:
                        self._dep(o, t)
        for w in writes:
            for a in w.alias:
                self._dep(o, a.w)
                for t in a.r:
                    self._dep(o, t)
            w.alias = []
            self._dep(o, w.w)
            for t in w.r:
                self._dep(o, t)

    def _mark(self, tok, reads, writes):
        for r in reads:
            key = tok[1].eng if tok[0] == "op" else ("d", id(tok[1]))
            r.r = [t for t in r.r if (t[1].eng if t[0] == "op" else ("d", id(t[1]))) != key]
            r.r.append(tok)
        for w in writes:
            w.w = tok
            w.r = []

    def op(self, eng, fn, reads=(), writes=()):
        o = Op(eng, fn)
        self._deps(o, reads, writes)
        self.ops[eng].append(o)
        self._mark(("op", o), reads, writes)
        return o

    def dma(self, eng, out, in_, reads=(), writes=(), semreg=None):
        if semreg.dsem is None:
            self.nreg += 1
            semreg.dsem = self.stack.enter_context(self.nc.semaphore("d%d_%s" % (self.nreg, semreg.name)))
        o = Op(eng, lambda e: e.dma_start(out=out, in_=in_))
        self._deps(o, reads, writes)
        if eng == "pool" and POOL_DMA_MAX_OUTSTANDING:
            hist = self.__dict__.setdefault("pool_dma_hist", [])
            if len(hist) >= POOL_DMA_MAX_OUTSTANDING:
                self._dep(o, hist[-POOL_DMA_MAX_OUTSTANDING])
            hist.append(("dma", semreg, 16 * (semreg.dcnt + 1)))
        semreg.dcnt += 1
        o.dreg = semreg
        o.dval = 16 * semreg.dcnt
        self.ops[eng].append(o)
        self._mark(("dma", semreg, o.dval), reads, writes)
        return o

    def resolve(self):
        for e in self.ENGS:
            c = 0
            for o in self.ops[e]:
                if o.inc:
                    c += 1
                    o.count = c
            self.maxcount = getattr(self, "maxcount", {})
            self.maxcount[e] = c

    def emit_engine(self, e, h):
        waited = {}
        for o in self.ops[e]:
            for tok in o.waits:
                if tok[0] == "op":
                    sem, val = self.sems[tok[1].eng], tok[1].count
                else:
                    sem, val = tok[1].dsem, tok[2]
                k = id(sem)
                if waited.get(k, 0) >= val:
                    continue
                waited[k] = val
                h.wait_ge(sem, val)
            if o.fn is None:
                continue
            ins = o.fn(h)
            if o.dreg is not None:
                ins.then_inc(o.dreg.dsem, 16)
            elif o.inc:
                ins.then_inc(self.sems[e], 1)

    def final_wait(self, eng, regs):
        o = Op(eng, None)
        for r in regs:
            self._dep(o, r.w)
            for t in r.r:
                self._dep(o, t)
        self.ops[eng].append(o)
        return o


def build(nseq=SEQ_PER_CORE, debug=None, phases=(1, 2), upto=99):
    from contextlib import ExitStack

    nc = bass.Bass("TRN2", target_bir_lowering=False)
    stack = ExitStack()

    def din(name, shape, dt=F32):
        return nc.dram_tensor(name, list(shape), dt, kind="ExternalInput").ap()

    x = din("x", [nseq, S, D])
    w_in = din("w_in", [D, 7680])
    w_sp = din("w_spatial", [4, 128, 128])
    w_a = din("w_branch_a", [512, D])
    w_b = din("w_branch_b", [512, D])
    w_out = din("w_out", [D, D])
    w_up = din("w_up", [D, 4096])
    w_down = din("w_down", [4096, D])
    c_ident = din("c_ident", [128, 128])
    c_masks = din("c_masks", [128, 640])
    c_cos = din("c_cos", [128, 256])
    c_sin = din("c_sin", [128, 256])
    c_gpre = din("c_gpre", [128, 8])
    c_gpre2 = din("c_gpre2", [128, 8])
    c_lng = din("c_lng", [128, 4])
    c_bgate = din("c_bgate", [128, 16])
    c_lnb = din("c_lnb", [1, 512])
    c_bsp = din("c_bsp", [1, 512])
    c_gpost = din("c_gpost", [128, D])
    c_gpost2 = din("c_gpost2", [128, D])
    out = nc.dram_tensor("out", [nseq, S, D], F32, kind="ExternalOutput").ap()
    wup_bf = nc.dram_tensor("wup_bf", [D, 4096], BF16, kind="Internal").ap()
    wdn_bf = nc.dram_tensor("wdn_bf", [4096, D], BF16, kind="Internal").ap()
    dbg = {}
    if debug:
        for name, shape in debug.items():
            dbg[name] = nc.dram_tensor(name, list(shape), F32, kind="ExternalOutput").ap()

    ARENA_F = 53200
    arena = stack.enter_context(nc.sbuf_tensor("arena", [128, ARENA_F], F32))
    psum = [stack.enter_context(nc.psum_tensor("ps%d" % i, [128, 1024], F32)) for i in range(4)]
    P = Prog(nc, stack)

    class Arena:
        def __init__(self):
            self.top = 0
            self.hist = []
            self.peak = 0

        def alloc(self, name, nwords):
            nwords = (nwords + 7) // 8 * 8
            st = self.top
            self.top += nwords
            assert self.top <= ARENA_F, "SBUF arena overflow at %s: %d" % (name, self.top)
            self.peak = max(self.peak, self.top)
            return st

        def reg(self, name, st, nwords):
            en = st + nwords
            al = [r for (a, b, r) in self.hist if a < en and st < b]
            r = P.reg(name, al)
            self.hist.append((st, en, r))
            return r

        def mark(self):
            return self.top

        def release(self, m):
            self.top = m

    A = Arena()

    class Buf:
        def __init__(self, name, nwords, nregs=1):
            self.st = A.alloc(name, nwords)
            self.n = nwords
            self.f = arena[:, self.st:self.st + nwords]
            self.b = arena[:, self.st:self.st + nwords].bitcast(BF16)
            self.reg = A.reg(name, self.st, nwords)

    def dump(name, src_ap, reg, n, isbf=False):
        if not (debug and name in dbg):
            return
        if isbf:
            stg = Buf("stg_" + name, n)
            P.op("dve", lambda e, stg=stg: e.tensor_copy(out=stg.f, in_=src_ap), reads=[reg], writes=[stg.reg])
            P.dma("sp", dbg[name], stg.f, reads=[stg.reg], writes=[], semreg=stg.reg)
            out_regs.append(stg.reg)
        else:
            P.dma("sp", dbg[name], src_ap, reads=[reg], writes=[], semreg=reg)
            out_regs.append(reg)

    out_regs = []
    bank_f = []
    bank_reg = []
    for i in range(8):
        bank_f.append(psum[i // 2][:, (i % 2) * 512:(i % 2) * 512 + 512])
        bank_reg.append(P.reg("bank%d" % i))
        bank_reg[-1].excl = True
    bank_ctr = {}
    ALLB = (0, 1, 2, 3, 4, 5, 6, 7)
    bank_pool = [ALLB]

    def next_bank(pool=None):
        pool = pool or bank_pool[0]
        c = bank_ctr.get(pool, 0)
        bank_ctr[pool] = c + 1
        return pool[c % len(pool)]

    def next_pair(pool=None):
        pool = pool or (0, 2, 4, 6)
        key = ("pair",) + pool
        c = bank_ctr.get(key, 0)
        bank_ctr[key] = c + 1
        return pool[c % len(pool)]

    identF = Buf("identF", 128)
    gpre = Buf("gpre", 8)
    gpre2 = Buf("gpre2", 8)
    lng = Buf("lng", 8)
    hbg = Buf("hbg", 16)
    gpost2 = Buf("gpost2", 1024)
    neghalf = Buf("neghalf", 8)
    epsT = Buf("epsT", 8)
    gpost = Buf("gpost", 1024)
    mP1 = A.mark()
    identB = Buf("identB", 64)
    masks = Buf("masks", 320)
    onesB = Buf("onesB", 64)
    cosT = Buf("cosT", 256)
    sinT = Buf("sinT", 256)
    Cg = Buf("Cg", 512)
    WsT = Buf("WsT", 256)
    constR = P.reg("constdma")

    def sq(eng):
        return eng

    P.dma("sp", identF.f, c_ident, writes=[identF.reg], semreg=identF.reg)
    P.dma("sp", cosT.f, c_cos, writes=[cosT.reg], semreg=cosT.reg)
    P.dma("sp", sinT.f, c_sin, writes=[sinT.reg], semreg=sinT.reg)
    P.dma("sp", gpre.f, c_gpre, writes=[gpre.reg], semreg=gpre.reg)
    P.dma("sp", gpre2.f, c_gpre2, writes=[gpre2.reg], semreg=gpre2.reg)
    P.dma("sp", lng.f[:, 0:4], c_lng, writes=[lng.reg], semreg=lng.reg)
    P.dma("sp", hbg.f, c_bgate, writes=[hbg.reg], semreg=hbg.reg)
    P.dma("sp", gpost.f, c_gpost, writes=[gpost.reg], semreg=gpost.reg)
    P.dma("sp", gpost2.f, c_gpost2, writes=[gpost2.reg], semreg=gpost2.reg)
    P.dma("pool", masks.b, c_masks, writes=[masks.reg], semreg=masks.reg)

    P.op("dve", lambda e: e.tensor_scalar(out=hbg.f, in0=hbg.f, scalar1=0.5, scalar2=None, op0=ALU.mult),
         reads=[hbg.reg], writes=[hbg.reg])
    P.op("dve", lambda e: e.memset(onesB.b, 1.0), writes=[onesB.reg])
    P.op("dve", lambda e: e.memset(neghalf.f, -0.5), writes=[neghalf.reg])
    P.op("dve", lambda e: e.memset(epsT.f[:, 0:1], EPS), writes=[epsT.reg])
    P.op("dve", lambda e: e.memset(epsT.f[:, 1:2], 4.0 * EPS), writes=[epsT.reg])
    P.op("dve", lambda e: e.tensor_copy(out=identB.b, in_=identF.f), reads=[identF.reg], writes=[identB.reg])

    def rstd_op(dst, src, n, mul, reads, writes, post=1.0, mode="pow"):
        if mode == "pow":
            P.op("dve", lambda e: e.tensor_scalar(out=dst, in0=src, scalar1=mul, scalar2=EPS, op0=ALU.mult, op1=ALU.add),
                 reads=reads, writes=writes)
            P.op("pool", lambda e: e.tensor_tensor(out=dst, in0=dst, in1=neghalf.f[:, 0:n], op=ALU.pow),
                 reads=writes + [neghalf.reg], writes=writes)
            if post != 1.0:
                P.op("dve", lambda e: e.tensor_scalar(out=dst, in0=dst, scalar1=post, scalar2=None, op0=ALU.mult),
                     reads=writes, writes=writes)
        else:
            assert post in (1.0, 0.5)
            bcol = 0 if post == 1.0 else 1
            P.op("act", lambda e: e.activation(out=dst, in_=src, func=AF.Sqrt, scale=mul / (post * post),
                                               bias=epsT.f[:, bcol:bcol + 1]),
                 reads=reads + [epsT.reg], writes=writes)
            P.op("dve", lambda e: e.reciprocal(out=dst, in_=dst), reads=writes, writes=writes)

    m0 = A.mark()
    wsp = Buf("wsp", 512)
    wsTf = Buf("wsTf", 512)
    rows = Buf("rows", 512 * 3 + 128)
    onesF = Buf("onesF", 8)
    P.dma("sp", wsp.f.rearrange("p (g s) -> p g s", g=4), w_sp.rearrange("g t s -> t g s"),
          writes=[wsp.reg], semreg=wsp.reg)
    P.dma("sp", rows.f[0:1, 512:1024], c_lnb, writes=[rows.reg], semreg=rows.reg)
    P.dma("sp", rows.f[0:1, 1024:1536], c_bsp, writes=[rows.reg], semreg=rows.reg)
    P.op("dve", lambda e: e.memset(rows.f[0:1, 1536:1664], 1.0), reads=[], writes=[rows.reg])
    P.op("dve", lambda e: e.memset(onesF.f, 1.0), writes=[onesF.reg])
    bk = next_bank()

    def _tr_ws(e, bk=bk):
        ins = None
        for g in range(4):
            ins = e.transpose(bank_f[bk][:, g * 128:(g + 1) * 128], wsp.f[:, g * 128:(g + 1) * 128], identF.f)
        return ins
    P.op("pe", _tr_ws, reads=[wsp.reg, identF.reg], writes=[bank_reg[bk]])
    P.op("act", lambda e, bk=bk: e.copy(out=wsTf.f, in_=bank_f[bk]), reads=[bank_reg[bk]], writes=[wsTf.reg])
    P.op("dve", lambda e, bk=bk: e.tensor_copy(out=WsT.b, in_=bank_f[bk]), reads=[bank_reg[bk]], writes=[WsT.reg])
    bk = next_bank()

    def _rs(e, bk=bk):
        ins = None
        for g in range(4):
            ins = e.matmul(bank_f[bk][0:1, g * 128:(g + 1) * 128], onesF.f[:, 0:1], wsTf.f[:, g * 128:(g + 1) * 128],
                           start=True, stop=True)
        return ins
    P.op("pe", _rs, reads=[wsTf.reg, onesF.reg], writes=[bank_reg[bk]])
    P.op("act", lambda e, bk=bk: e.copy(out=rows.f[0:1, 0:512], in_=bank_f[bk][0:1, :]),
         reads=[bank_reg[bk]], writes=[rows.reg])
    bk = next_bank()

    def _cg(e, bk=bk):
        ins = None
        for g in range(4):
            sl = slice(g * 128, (g + 1) * 128)
            e.matmul(bank_f[bk][:, sl], rows.f[0:1, 512 + g * 128:512 + (g + 1) * 128], rows.f[0:1, sl],
                     start=True, stop=False)
            ins = e.matmul(bank_f[bk][:, sl], rows.f[0:1, 1536:1664], rows.f[0:1, 1024 + g * 128:1024 + (g + 1) * 128],
                           start=False, stop=True)
        return ins
    P.op("pe", _cg, reads=[rows.reg], writes=[bank_reg[bk]])
    P.op("act", lambda e, bk=bk: e.copy(out=Cg.f, in_=bank_f[bk]), reads=[bank_reg[bk]], writes=[Cg.reg])
    dump("dbg_Cg", Cg.f, Cg.reg, 512)
    dump("dbg_wsTf", wsTf.f, wsTf.reg, 512)
    dump("dbg_rows", rows.f[0:1, :], rows.reg, 1664)
    dump("dbg_WsT", WsT.b, WsT.reg, 512, isbf=True)
    A.release(m0)

    def wload(buf_ap, src_ap, reg, reads=()):
        P.dma("pool", buf_ap, src_ap, reads=list(reads), writes=[reg], semreg=reg)

    nT = Buf("nT", KC * SP // 2)
    nTv = nT.b.rearrange("p (k t) -> p k t", k=KC)
    yaT = Buf("yaT", 4 * S // 2)
    yaTv = yaT.b.rearrange("p (k t) -> p k t", k=4)
    ybT = Buf("ybT", 4 * S // 2)
    ybTv = ybT.b.rearrange("p (k t) -> p k t", k=4)
    junk = Buf("junk", D // 2)
    Wu = Buf("Wu", KC * 512 // 2)
    Wuv = Wu.b.rearrange("p (k n) -> p k n", k=KC)
    Wv = Buf("Wv", KC * 512 // 2)
    Wvv = Wv.b.rearrange("p (k n) -> p k n", k=KC)
    ssb = [Buf("ss%d" % i, 8) for i in range(2)]
    P.op("pool", lambda e: e.memset(nTv[:, :, 0:PAD], 0.0), writes=[nT.reg])
    P.op("pool", lambda e: e.memset(nTv[:, :, PAD + S:SP], 0.0), writes=[nT.reg])
    xq_f = [yaT.f, ybT.f]
    xq_reg = [yaT.reg, ybT.reg]

    def stage1_dma(s, q):
        xv = xq_f[q % 2].rearrange("p (t d) -> p t d", t=4)
        P.dma("sp", xv, x[s, q * 512:(q + 1) * 512, :].rearrange("(t p) d -> p t d", p=128),
              writes=[xq_reg[q % 2]], semreg=xq_reg[q % 2])

    def stage1_front(s, q):
        xf = xq_f[q % 2]
        xreg = xq_reg[q % 2]
        ss = ssb[q % 2]
        xv = xf.rearrange("p (t d) -> p t d", t=4)
        for tt in range(4):
            P.op("act", lambda e, xv=xv, tt=tt, ss=ss: e.activation(out=junk.b, in_=xv[:, tt, :], func=AF.Square,
                                                                     accum_out=ss.f[:, tt:tt + 1]),
                 reads=[xreg], writes=[junk.reg, ss.reg])
        rstd_op(ss.f[:, 0:4], ss.f[:, 0:4], 4, 1.0 / D, [ss.reg], [ss.reg], mode="sqrt")
        for tt in range(4):
            P.op("dve", lambda e, xv=xv, tt=tt, ss=ss: e.tensor_scalar(out=xv[:, tt, :], in0=xv[:, tt, :],
                                                                        scalar1=ss.f[:, tt:tt + 1], scalar2=None,
                                                                        op0=ALU.mult),
                 reads=[xreg, ss.reg], writes=[xreg])

    def stage1_back(s, q):
        xf = xq_f[q % 2]
        xreg = xq_reg[q % 2]
        xv = xf.rearrange("p (t d) -> p t d", t=4)
        for kc in range(KC):
            bk = next_bank((4, 5, 6, 7))

            def _tr(e, bk=bk, xv=xv, kc=kc):
                ins = None
                for tt in range(4):
                    ins = e.transpose(bank_f[bk][:, tt * 128:(tt + 1) * 128], xv[:, tt, kc * 128:(kc + 1) * 128],
                                      identF.f)
                return ins
            P.op("pe", _tr, reads=[xreg, identF.reg], writes=[bank_reg[bk]])
            dst = nTv[:, kc, PAD + q * 512:PAD + (q + 1) * 512]
            if kc % 2 == 0:
                P.op("act", lambda e, bk=bk, dst=dst, kc=kc: e.activation(out=dst, in_=bank_f[bk], func=AF.Identity,
                                                                           scale=gpre.f[:, kc:kc + 1]),
                     reads=[bank_reg[bk], gpre.reg], writes=[nT.reg])
            else:
                P.op("dve", lambda e, bk=bk, dst=dst, kc=kc: e.tensor_scalar(out=dst, in0=bank_f[bk],
                                                                              scalar1=gpre.f[:, kc:kc + 1],
                                                                              scalar2=None, op0=ALU.mult),
                     reads=[bank_reg[bk], gpre.reg], writes=[nT.reg])

    def stage1_quad(s, q, after_dma=None, dma=True):
        if dma:
            stage1_dma(s, q)
        if after_dma is not None:
            after_dma(xq_reg[q % 2])
        stage1_front(s, q)
        stage1_back(s, q)

    def phase1(s):
        mS = A.mark()

        mQ = A.mark()
        Wqk = [Buf("Wqk%d" % i, KC * 768 // 2) for i in range(2)]
        Wv3 = [Buf("Wv3_%d" % i, KC * 384 // 2) for i in range(2)]
        mW = A.mark()

        def slot_w_thunks(j):
            b = j % 2
            th = []
            src = w_in[:, 1024:4096].rearrange("(k p) (m c) -> p k m c", p=128, c=512)
            dstw = Wqk[b].b.rearrange("p (k m c) -> p k m c", k=KC, m=6)
            for m_ in range(6):
                th.append(lambda m_=m_, src=src, dstw=dstw: wload(dstw[:, :, m_, :], src[:, :, m_, j * 128:(j + 1) * 128],
                                                                  Wqk[b].reg))
            src2 = w_in[:, 4096:5632].rearrange("(k p) (m c) -> p k m c", p=128, c=512)
            dstw2 = Wv3[b].b.rearrange("p (k m c) -> p k m c", k=KC, m=3)
            for m_ in range(3):
                th.append(lambda m_=m_, src2=src2, dstw2=dstw2: wload(dstw2[:, :, m_, :],
                                                                      src2[:, :, m_, j * 128:(j + 1) * 128], Wv3[b].reg))
            return th

        def _wuv(xreg):
            wload(Wuv, w_in[:, 0:512].rearrange("(k p) n -> p k n", p=128), Wu.reg, reads=[xreg])
            wload(Wvv, w_in[:, 512:1024].rearrange("(k p) n -> p k n", p=128), Wv.reg)
        if s == 0:
            for q in range(4):
                stage1_quad(0, q, after_dma=_wuv if q == 0 else None)
        m1 = A.mark()
        if debug and "dbg_nT" in dbg and s == 0:
            stg = Buf("dbgstg", KC * SP)
            P.op("dve", lambda e, stg=stg: e.tensor_copy(out=stg.f, in_=nT.b), reads=[nT.reg], writes=[stg.reg])
            P.dma("sp", dbg["dbg_nT"], stg.f, reads=[stg.reg], writes=[], semreg=stg.reg)
            out_regs.append(stg.reg)
            A.release(m1)

        if upto < 2:
            return
        m2 = A.mark()
        uT = Buf("uT", 4 * S // 2)
        uTv = uT.b.rearrange("p (k t) -> p k t", k=4)
        NB2 = 4
        vg = [Buf("vg%d" % i, 512) for i in range(NB2)]
        vn = [Buf("vn%d" % i, 256) for i in range(NB2)]
        tmpg = [Buf("tmpg%d" % i, 512) for i in range(NB2)]
        st6 = [Buf("st6_%d" % i, 8) for i in range(NB2)]
        mv = [Buf("mv%d" % i, 8) for i in range(NB2)]

        def u_group(q, c):
            bk = next_bank((3, 4, 5))

            def _mm(e, bk=bk, q=q, c=c):
                ins = None
                for kc in range(KC):
                    ins = e.matmul(bank_f[bk], Wuv[:, kc, c * 128:(c + 1) * 128],
                                   nTv[:, kc, PAD + q * 512:PAD + (q + 1) * 512], start=(kc == 0), stop=(kc == KC - 1))
                return ins
            P.op("pe", _mm, reads=[Wu.reg, nT.reg], writes=[bank_reg[bk]])
            P.op("act", lambda e, bk=bk, q=q, c=c: e.activation(out=uTv[:, c, q * 512:(q + 1) * 512], in_=bank_f[bk],
                                                                 func=AF.Gelu_apprx_tanh),
                 reads=[bank_reg[bk]], writes=[uT.reg])

        def v_front(t):
            bk = next_bank((0, 1, 2))
            i2 = t % NB2

            def _mm(e, bk=bk, t=t):
                ins = None
                for kc in range(KC):
                    ins = e.matmul(bank_f[bk], nTv[:, kc, PAD + t * 128:PAD + (t + 1) * 128], Wvv[:, kc, :],
                                   start=(kc == 0), stop=(kc == KC - 1))
                return ins
            P.op("pe", _mm, reads=[Wv.reg, nT.reg], writes=[bank_reg[bk]])
            P.op("act", lambda e, bk=bk, i2=i2: e.activation(out=vg[i2].f, in_=bank_f[bk], func=AF.Gelu_apprx_tanh),
                 reads=[bank_reg[bk]], writes=[vg[i2].reg])
            P.op("dve", lambda e, i2=i2: e.bn_stats(out=st6[i2].f[:, 0:6], in_=vg[i2].f), reads=[vg[i2].reg],
                 writes=[st6[i2].reg])
            P.op("dve", lambda e, i2=i2: e.bn_aggr(out=mv[i2].f[:, 0:2], in_=st6[i2].f[:, 0:6]), reads=[st6[i2].reg],
                 writes=[mv[i2].reg])
            rstd_op(mv[i2].f[:, 1:2], mv[i2].f[:, 1:2], 1, 1.0, [mv[i2].reg], [mv[i2].reg])
            P.op("dve", lambda e, i2=i2: e.tensor_scalar(out=vn[i2].b, in0=vg[i2].f, scalar1=mv[i2].f[:, 0:1],
                                                          scalar2=mv[i2].f[:, 1:2], op0=ALU.subtract, op1=ALU.mult),
                 reads=[vg[i2].reg, mv[i2].reg], writes=[vn[i2].reg])

        def v_back(t):
            i2 = t % NB2
            bk2 = next_bank((6, 7))

            def _sp(e, bk2=bk2, i2=i2):
                ins = None
                for g in range(4):
                    sl = slice(g * 128, (g + 1) * 128)
                    ins = e.matmul(bank_f[bk2][:, sl], vn[i2].b[:, sl], WsT.b[:, sl], start=True, stop=True)
                return ins
            P.op("pe", _sp, reads=[vn[i2].reg, WsT.reg], writes=[bank_reg[bk2]])
            for g in range(4):
                sl = slice(g * 128, (g + 1) * 128)
                P.op("dve", lambda e, bk2=bk2, i2=i2, g=g, sl=sl: e.scalar_tensor_tensor(
                    out=tmpg[i2].f[:, sl], in0=bank_f[bk2][:, sl], scalar=lng.f[:, g:g + 1], in1=Cg.f[:, sl],
                    op0=ALU.mult, op1=ALU.add),
                    reads=[bank_reg[bk2], lng.reg, Cg.reg], writes=[tmpg[i2].reg])
            P.op("pool", lambda e, i2=i2, t=t: e.tensor_tensor(
                out=yaTv[:, :, t * 128:(t + 1) * 128], in0=tmpg[i2].f.rearrange("p (g t) -> p g t", g=4),
                in1=uTv[:, :, t * 128:(t + 1) * 128], op=ALU.mult),
                reads=[tmpg[i2].reg, uT.reg], writes=[yaT.reg])

        SK2 = 3
        sw0 = slot_w_thunks(0)
        for t in range(16 + SK2):
            if t < len(sw0):
                sw0[t]()
            if t < 16:
                if t % 4 == 0:
                    for c in range(4):
                        u_group(t // 4, c)
                v_front(t)
            if t >= SK2:
                v_back(t - SK2)
        dump("dbg_uT", uT.b, uT.reg, 4 * S, isbf=True)
        A.release(m2)
        if debug and "dbg_yaT" in dbg and s == 0:
            stg = Buf("dbgstg2", 4 * S)
            P.op("dve", lambda e, stg=stg: e.tensor_copy(out=stg.f, in_=yaT.b), reads=[yaT.reg], writes=[stg.reg])
            P.dma("sp", dbg["dbg_yaT"], stg.f, reads=[stg.reg], writes=[], semreg=stg.reg)
            out_regs.append(stg.reg)
            A.release(m2)

        if upto < 2.05:
            return
        A.release(mW)
        m3 = A.mark()
        qks = [Buf("qks%d" % i, 768 // 2) for i in range(3)]
        rts = [[Buf("rt%d_%d" % (i, k), 96) for k in range(4)] for i in range(2)]
        qT = Buf("qT", 3 * S // 2)
        qTv = qT.b.rearrange("p (g t) -> p g t", g=3)
        kT = Buf("kT", 3 * SP // 2)
        kTv = kT.b.rearrange("p (g t) -> p g t", g=3)
        VB0 = (0, 17, 37)
        Vt = Buf("Vt", 53 * 128 // 2)
        Vtv = Vt.b.rearrange("p (b d) -> p b d", b=53)
        NPT = 5
        LA = 3
        PT = [Buf("PT%d" % i, 128) for i in range(NPT)]
        accU = Buf("accU", S)
        accD = Buf("accD", S)
        P.op("pool", lambda e: e.memset(kTv[:, :, 0:PAD], 0.0), writes=[kT.reg])
        P.op("pool", lambda e: e.memset(kTv[:, :, PAD + S:SP], 0.0), writes=[kT.reg])

        blocks = []
        for b in range(17):
            blocks.append((0, slice(PAD + 128 * b - 64, PAD + 128 * b + 64)))
        for r in range(4):
            for b in range(5):
                st = PAD + 4 * (128 * b - 64) + r
                blocks.append((1, slice(st, st + 4 * 127 + 1, 4)))
        for r in range(16):
            blocks.append((2, slice(PAD + r, PAD + r + 16 * 127 + 1, 16)))
        vgroups = []
        vb = 0
        while vb < 53:
            g = blocks[vb][0]
            nb = 0
            while vb + nb < 53 and nb < 4 and blocks[vb + nb][0] == g:
                nb += 1
            vgroups.append((vb, nb, g))
            vb += nb

        mAB = masks.b[:, 0:256]
        mA2 = masks.b[:, 256:384]
        mB2 = masks.b[:, 384:512]
        mC = masks.b[:, 512:640]
        tasks = []
        for g in range(3):
            d = (1, 4, 16)[g]
            L = S // d
            if g < 2:
                nqb = L // 128
                for r in range(d):
                    for b in range(nqb + 1):
                        ks = PAD + d * (128 * b - 64) + r
                        ksl = slice(ks, ks + d * 127 + 1, d) if d > 1 else slice(ks, ks + 128)
                        qlo = max(b - 1, 0)
                        qhi = min(b, nqb - 1)
                        mk = mB2 if b == 0 else (mA2 if b == nqb else mAB)
                        tasks.append(dict(g=g, r=r, b=b, nqb=nqb, L=L, ksl=ksl,
                                          qsl=slice(r * L + 128 * qlo, r * L + 128 * (qhi + 1)),
                                          nq=(qhi - qlo + 1) * 128, mk=mk, vbi=VB0[g] + r * (nqb + 1) + b))
            else:
                for r in range(16):
                    tasks.append(dict(g=2, r=r, b=0, nqb=1, L=128, ksl=slice(PAD + r, PAD + r + 16 * 127 + 1, 16),
                                      qsl=slice(r * 128, (r + 1) * 128), nq=128, mk=mC, vbi=VB0[2] + r))

        pt_ctr = [0]

        def finalize_chunk(j, c4):
            sl = slice(c4 * 512, (c4 + 1) * 512)
            P.op("dve", lambda e, sl=sl: e.reciprocal(out=accD.f[:, sl], in_=accD.f[:, sl]), reads=[accD.reg],
                 writes=[accD.reg])
            P.op("dve", lambda e, j=j, sl=sl: e.tensor_tensor(out=ybTv[:, j, sl], in0=accU.f[:, sl], in1=accD.f[:, sl],
                                                             op=ALU.mult),
                 reads=[accU.reg, accD.reg], writes=[ybT.reg])

        for j in range(4):
            swn = slot_w_thunks(j + 1) if j + 1 < 4 else []
            Wq = Wqk[j % 2]
            Wqv = Wq.b.rearrange("p (k n) -> p k n", k=KC)
            Wvj = Wv3[j % 2]
            Wvjv = Wvj.b.rearrange("p (k n) -> p k n", k=KC)

            def qk_front(t, Wqv=Wqv, Wq=Wq):
                pr = next_pair((0, 2))
                qs = qks[t % 3]
                qsv = qs.b.rearrange("p (m d) -> p m d", m=6)
                rt = rts[t % 2]

                def _mm(e, pr=pr, t=t, Wqv=Wqv):
                    ins = None
                    for h in range(2):
                        for kc in range(KC):
                            ins = e.matmul(bank_f[pr + h][:, 0:384], nTv[:, kc, PAD + t * 128:PAD + (t + 1) * 128],
                                           Wqv[:, kc, h * 384:(h + 1) * 384], start=(kc == 0), stop=(kc == KC - 1))
                    return ins
                P.op("pe", _mm, reads=[Wq.reg, nT.reg], writes=[bank_reg[pr], bank_reg[pr + 1]])
                pv = psum[pr // 2].rearrange("p (h n) -> p h n", h=2)[:, :, 0:384].rearrange("p h (m d) -> p h m d", m=3)
                breg = [bank_reg[pr], bank_reg[pr + 1]]
                cosb = cosT.f[:, t * 16:(t + 1) * 16].unsqueeze(1).unsqueeze(1).broadcast_to([128, 2, 3, 16])
                sinb = sinT.f[:, t * 16:(t + 1) * 16].unsqueeze(1).unsqueeze(1).broadcast_to([128, 2, 3, 16])
                x1 = pv[:, :, :, 0:16]
                x2 = pv[:, :, :, 16:32]
                qs4 = qs.b.rearrange("p (h m d) -> p h m d", h=2, m=3)

                def v4(bf):
                    return bf.f.rearrange("p (h m d) -> p h m d", h=2, m=3)
                for k_, (xa, tb_) in enumerate(((x1, cosb), (x2, sinb), (x2, cosb), (x1, sinb))):
                    P.op("dve", lambda e, xa=xa, tb_=tb_, o_=rt[k_]: e.tensor_tensor(out=v4(o_), in0=xa, in1=tb_, op=ALU.mult),
                         reads=breg + [cosT.reg, sinT.reg], writes=[rt[k_].reg])
                for h in range(2):
                    P.op("act", lambda e, pv=pv, h=h, qsv=qsv: e.copy(out=qsv[:, h * 3:(h + 1) * 3, 32:128],
                                                                    in_=pv[:, h, :, 32:128]),
                         reads=[breg[h]], writes=[qs.reg])
                P.op("pool", lambda e, qs4=qs4, rt=rt: e.tensor_tensor(out=qs4[:, :, :, 0:16], in0=v4(rt[0]), in1=v4(rt[1]),
                                                                        op=ALU.subtract),
                     reads=[rt[0].reg, rt[1].reg], writes=[qs.reg])
                P.op("pool", lambda e, qs4=qs4, rt=rt: e.tensor_tensor(out=qs4[:, :, :, 16:32], in0=v4(rt[2]), in1=v4(rt[3]),
                                                                        op=ALU.add),
                     reads=[rt[2].reg, rt[3].reg], writes=[qs.reg])

            def qk_back(t):
                qs = qks[t % 3]
                qsv = qs.b.rearrange("p (m d) -> p m d", m=6)
                bt = next_bank((4, 5))
                btb = bank_f[bt].bitcast(BF16)

                def _tr(e, btb=btb, qsv=qsv):
                    ins = None
                    for m in range(6):
                        ins = e.transpose(btb[:, m * 128:(m + 1) * 128], qsv[:, m, :], identB.b)
                    return ins
                P.op("pe", _tr, reads=[qs.reg, identB.reg], writes=[bank_reg[bt]])
                btv = btb[:, 0:768].rearrange("p (m t) -> p m t", m=6)
                P.op("act", lambda e, btv=btv, t=t: e.copy(out=kTv[:, :, PAD + t * 128:PAD + (t + 1) * 128],
                                                          in_=btv[:, 3:6, :]),
                     reads=[bank_reg[bt]], writes=[kT.reg])
                P.op("act", lambda e, btv=btv, t=t: e.copy(
                    out=qTv[:, 2, :].rearrange("p (r i) -> p r i", r=16)[:, :, 8 * t:8 * t + 8],
                    in_=btv[:, 2, :].rearrange("p (i r) -> p r i", r=16)),
                    reads=[bank_reg[bt]], writes=[qT.reg])
                P.op("dve", lambda e, btv=btv, t=t: e.tensor_copy(out=qTv[:, 0, t * 128:(t + 1) * 128], in_=btv[:, 0, :]),
                     reads=[bank_reg[bt]], writes=[qT.reg])
                P.op("dve", lambda e, btv=btv, t=t: e.tensor_copy(
                    out=qTv[:, 1, :].rearrange("p (r i) -> p r i", r=4)[:, :, 32 * t:32 * t + 32],
                    in_=btv[:, 1, :].rearrange("p (i r) -> p r i", r=4)),
                    reads=[bank_reg[bt]], writes=[qT.reg])

            def v_group(gi, Wvjv=Wvjv, Wvj=Wvj):
                vb, nb, g = vgroups[gi]
                bk = next_bank((6, 7))

                def _mm(e, bk=bk, vb=vb, nb=nb, g=g, Wvjv=Wvjv):
                    ins = None
                    for i in range(nb):
                        tok = blocks[vb + i][1]
                        for kc in range(KC):
                            ins = e.matmul(bank_f[bk][:, i * 128:(i + 1) * 128], nTv[:, kc, tok],
                                           Wvjv[:, kc, g * 128:(g + 1) * 128], start=(kc == 0), stop=(kc == KC - 1))
                    return ins
                P.op("pe", _mm, reads=[nT.reg, Wvj.reg], writes=[bank_reg[bk]])
                dstv = Vt.b[:, vb * 128:(vb + nb) * 128]
                if gi % 2 == 0:
                    P.op("act", lambda e, bk=bk, dstv=dstv, nb=nb: e.copy(out=dstv, in_=bank_f[bk][:, 0:nb * 128]),
                         reads=[bank_reg[bk]], writes=[Vt.reg])
                else:
                    P.op("dve", lambda e, bk=bk, dstv=dstv, nb=nb: e.tensor_copy(out=dstv, in_=bank_f[bk][:, 0:nb * 128]),
                         reads=[bank_reg[bk]], writes=[Vt.reg])

            if upto < 2.15:
                return
            vgi = 0
            QSK = 2
            for t in range(16 + QSK):
                if s == nseq - 1 and 2 in phases and t < 16 and conv:
                    conv.pop(0)()
                if t < len(swn):
                    swn[t]()
                if j >= 1 and t in (3, 6, 9, 12):
                    finalize_chunk(j - 1, (t - 3) // 3)
                if t < 16:
                    qk_front(t)
                if vgi < len(vgroups):
                    v_group(vgi)
                    vgi += 1
                if t >= QSK:
                    qk_back(t - QSK)
            while vgi < len(vgroups):
                v_group(vgi)
                vgi += 1
            if upto < 2.25:
                return

            def evac_OD(g, pq, bo, bd):
                for (bkx, acc) in ((bo, accU), (bd, accD)):
                    if g == 0:
                        dst = acc.f[:, pq * 512:(pq + 1) * 512]
                        P.op("act", lambda e, bkx=bkx, dst=dst: e.copy(out=dst, in_=bank_f[bkx]),
                             reads=[bank_reg[bkx]], writes=[acc.reg])
                    elif g == 1:
                        dst = acc.f.rearrange("p (i r) -> p r i", r=4)[:, pq, :]
                        P.op("dve", lambda e, bkx=bkx, dst=dst: e.tensor_tensor(out=dst, in0=bank_f[bkx], in1=dst,
                                                                                 op=ALU.add),
                             reads=[bank_reg[bkx], acc.reg], writes=[acc.reg])
                    else:
                        dst = acc.f.rearrange("p (i r) -> p r i", r=16)[:, 4 * pq:4 * pq + 4, :]
                        src = bank_f[bkx].rearrange("p (r i) -> p r i", r=4)
                        P.op("dve", lambda e, src=src, dst=dst: e.tensor_tensor(out=dst, in0=src, in1=dst, op=ALU.add),
                             reads=[bank_reg[bkx], acc.reg], writes=[acc.reg])

            def att_front(T):
                bs = next_bank((0, 1, 2, 3))
                pt = PT[pt_ctr[0] % NPT]
                pt_ctr[0] += 1
                T["pt"] = pt
                nq = T["nq"]
                P.op("pe", lambda e, bs=bs, T=T, nq=nq: e.matmul(bank_f[bs][:, 0:nq], kTv[:, T["g"], T["ksl"]],
                                                               qTv[:, T["g"], T["qsl"]], start=True, stop=True),
                     reads=[kT.reg, qT.reg], writes=[bank_reg[bs]])
                P.op("act", lambda e, bs=bs, pt=pt, nq=nq: e.activation(out=pt.b[:, 0:nq], in_=bank_f[bs][:, 0:nq],
                                                                       func=AF.Exp, scale=SCALE),
                     reads=[bank_reg[bs]], writes=[pt.reg])
                P.op("pool", lambda e, pt=pt, nq=nq, mk=T["mk"]: e.tensor_tensor(out=pt.b[:, 0:nq], in0=pt.b[:, 0:nq],
                                                                                in1=mk, op=ALU.mult),
                     reads=[pt.reg, masks.reg], writes=[pt.reg])

            ost = {"bo": None, "bd": None}

            def att_back(T):
                g, r, b, nqb, L, vbi, pt = T["g"], T["r"], T["b"], T["nqb"], T["L"], T["vbi"], T["pt"]
                if g == 2:
                    if r % 4 == 0:
                        ost["bo"] = next_pair((4, 6))
                        ost["bd"] = ost["bo"] + 1
                    bo, bd = ost["bo"], ost["bd"]
                    cs = slice((r % 4) * 128, (r % 4) * 128 + 128)

                    def _pv(e, bo=bo, bd=bd, cs=cs, vbi=vbi, pt=pt):
                        e.matmul(bank_f[bo][:, cs], Vtv[:, vbi, :], pt.b[:, 0:128], start=True, stop=True)
                        return e.matmul(bank_f[bd][:, cs], onesB.b, pt.b[:, 0:128], start=True, stop=True)
                    P.op("pe", _pv, reads=[Vt.reg, pt.reg, onesB.reg], writes=[bank_reg[bo], bank_reg[bd]])
                    if r % 4 == 3:
                        evac_OD(2, r // 4, bo, bd)
                    return
                col = 0
                if b >= 1:
                    qb = b - 1
                    bo, bd = ost["bo"], ost["bd"]
                    cs = slice((qb % 4) * 128, (qb % 4) * 128 + 128)

                    def _fin(e, bo=bo, bd=bd, cs=cs, vbi=vbi, pt=pt):
                        e.matmul(bank_f[bo][:, cs], Vtv[:, vbi, :], pt.b[:, 0:128], start=False, stop=True)
                        return e.matmul(bank_f[bd][:, cs], onesB.b, pt.b[:, 0:128], start=False, stop=True)
                    P.op("pe", _fin, reads=[Vt.reg, pt.reg, onesB.reg], writes=[bank_reg[bo], bank_reg[bd]])
                    col = 128
                    if qb % 4 == 3 or qb == nqb - 1:
                        evac_OD(g, (r * L + 128 * qb) // 512, bo, bd)
                if b <= nqb - 1:
                    qb = b
                    if qb % 4 == 0:
                        ost["bo"] = next_pair((4, 6))
                        ost["bd"] = ost["bo"] + 1
                    bo, bd = ost["bo"], ost["bd"]
                    cs = slice((qb % 4) * 128, (qb % 4) * 128 + 128)

                    def _sta(e, bo=bo, bd=bd, cs=cs, vbi=vbi, pt=pt, col=col):
                        e.matmul(bank_f[bo][:, cs], Vtv[:, vbi, :], pt.b[:, col:col + 128], start=True, stop=False)
                        return e.matmul(bank_f[bd][:, cs], onesB.b, pt.b[:, col:col + 128], start=True, stop=False)
                    P.op("pe", _sta, reads=[Vt.reg, pt.reg, onesB.reg], writes=[bank_reg[bo], bank_reg[bd]])

            ntk = len(tasks) if upto >= 2.5 else (18 if upto < 2.35 else 38)
            tl = [dict(T) for T in tasks[:ntk]]
            for i in range(ntk + LA):
                if i < ntk:
                    att_front(tl[i])
                if i >= LA:
                    att_back(tl[i - LA])
            if upto < 2.65:
                return
            if j == 3:
                for c4 in range(4):
                    finalize_chunk(3, c4)
        A.release(m3)
        if debug and "dbg_ybT" in dbg and s == 0:
            stg = Buf("dbgstg3", 4 * S)
            P.op("dve", lambda e, stg=stg: e.tensor_copy(out=stg.f, in_=ybT.b), reads=[ybT.reg], writes=[stg.reg])
            P.dma("sp", dbg["dbg_ybT"], stg.f, reads=[stg.reg], writes=[], semreg=stg.reg)
            out_regs.append(stg.reg)
            A.release(m3)

        if upto < 4:
            return
        A.release(mQ)
        m4 = A.mark()
        mT = Buf("mT", KC * S // 2)
        mTv = mT.b.rearrange("p (k t) -> p k t", k=KC)
        Wo = Buf("Wo", KC * D // 2)
        Wov = Wo.b.rearrange("p (k n) -> p k n", k=KC)
        Wg = [Buf("Wg%d" % i, KC * 256 // 2) for i in range(2)]
        Wab = [Buf("Wab%d" % i, 2 * 4 * 128 // 2) for i in range(2)]
        def load_c(c):
            b = c % 2
            src = w_in[:, 5632:7680].rearrange("(k p) (m n) -> p k m n", p=128, n=1024)
            dstw = Wg[b].b.rearrange("p (k m n) -> p k m n", k=KC, m=2)
            for m_ in range(2):
                wload(dstw[:, :, m_, :], src[:, :, m_, c * 128:(c + 1) * 128], Wg[b].reg)
            wab = Wab[b].b.rearrange("p (w k n) -> p w k n", w=2, k=4)
            wload(wab[:, 0, :, :], w_a[:, c * 128:(c + 1) * 128].rearrange("(k p) n -> p k n", p=128), Wab[b].reg)
            wload(wab[:, 1, :, :], w_b[:, c * 128:(c + 1) * 128].rearrange("(k p) n -> p k n", p=128), Wab[b].reg)

        load_c(0)
        ta = [Buf("ta%d" % i, 512) for i in range(2)]
        tb = [Buf("tb%d" % i, 512) for i in range(2)]
        m1b = [Buf("m1b%d" % i, 512) for i in range(2)]
        m2b = [Buf("m2b%d" % i, 512) for i in range(2)]

        it = 0
        for c in range(8):
            if c + 1 < 8:
                load_c(c + 1)
            if c == 1:
                wload(Wov[:, 0:4, :], w_out[0:512, :].rearrange("(k p) n -> p k n", p=128), Wo.reg)
            if c == 2:
                wload(Wov[:, 4:8, :], w_out[512:1024, :].rearrange("(k p) n -> p k n", p=128), Wo.reg)
            wg = Wg[c % 2]
            wgv = wg.b.rearrange("p (k m n) -> p k m n", k=KC, m=2)
            wab = Wab[c % 2]
            wabv = wab.b.rearrange("p (w k n) -> p w k n", w=2, k=4)
            for q in range(4):
                i2 = it % 2
                it += 1
                tsl = slice(PAD + q * 512, PAD + (q + 1) * 512)
                qsl = slice(q * 512, (q + 1) * 512)
                bga, bgb, bA, bB = next_bank(), next_bank(), next_bank(), next_bank()

                def _mm(e, bga=bga, bgb=bgb, bA=bA, bB=bB, wgv=wgv, wabv=wabv, tsl=tsl, qsl=qsl):
                    ins = None
                    for kc in range(KC):
                        e.matmul(bank_f[bga], wgv[:, kc, 0, :], nTv[:, kc, tsl], start=(kc == 0), stop=(kc == KC - 1))
                    for kc in range(KC):
                        e.matmul(bank_f[bgb], wgv[:, kc, 1, :], nTv[:, kc, tsl], start=(kc == 0), stop=(kc == KC - 1))
                    for kc in range(4):
                        e.matmul(bank_f[bA], wabv[:, 0, kc, :], yaTv[:, kc, qsl], start=(kc == 0), stop=(kc == 3))
                    for kc in range(4):
                        ins = e.matmul(bank_f[bB], wabv[:, 1, kc, :], ybTv[:, kc, qsl], start=(kc == 0), stop=(kc == 3))
                    return ins
                P.op("pe", _mm, reads=[wg.reg, wab.reg, nT.reg, yaT.reg, ybT.reg],
                     writes=[bank_reg[bga], bank_reg[bgb], bank_reg[bA], bank_reg[bB]])
                P.op("act", lambda e, bga=bga, i2=i2, c=c: e.activation(out=ta[i2].f, in_=bank_f[bga], func=AF.Tanh,
                                                                       bias=hbg.f[:, c:c + 1], scale=0.5),
                     reads=[bank_reg[bga], hbg.reg], writes=[ta[i2].reg])
                P.op("act", lambda e, bgb=bgb, i2=i2, c=c: e.activation(out=tb[i2].f, in_=bank_f[bgb], func=AF.Tanh,
                                                                       bias=hbg.f[:, 8 + c:9 + c], scale=0.5),
                     reads=[bank_reg[bgb], hbg.reg], writes=[tb[i2].reg])
                P.op("dve", lambda e, bA=bA, i2=i2: e.scalar_tensor_tensor(out=m1b[i2].f, in0=ta[i2].f, scalar=1.0,
                                                                          in1=bank_f[bA], op0=ALU.add, op1=ALU.mult),
                     reads=[ta[i2].reg, bank_reg[bA]], writes=[m1b[i2].reg])
                P.op("dve", lambda e, bB=bB, i2=i2: e.scalar_tensor_tensor(out=m2b[i2].f, in0=tb[i2].f, scalar=1.0,
                                                                          in1=bank_f[bB], op0=ALU.add, op1=ALU.mult),
                     reads=[tb[i2].reg, bank_reg[bB]], writes=[m2b[i2].reg])
                P.op("pool", lambda e, i2=i2, c=c, qsl=qsl: e.tensor_tensor(out=mTv[:, c, qsl], in0=m1b[i2].f,
                                                                           in1=m2b[i2].f, op=ALU.add),
                     reads=[m1b[i2].reg, m2b[i2].reg], writes=[mT.reg])
        if s == nseq - 1 and 2 in phases:
            top_save = A.top
            A.top = mP1
            p2w["Wup"] = Buf("Wup", KC * 4096 // 2)
            p2w["wup_r"] = [A.reg("Wup_c%d" % i, p2w["Wup"].st, p2w["Wup"].n) for i in range(4)]
            wupv_ = p2w["Wup"].b.rearrange("p (k n) -> p k n", k=KC)
            A.top = top_save
        xt = [Buf("xt%d" % i, D) for i in range(2)]
        h1 = [Buf("h1_%d" % i, D) for i in range(2)]
        junk2 = Buf("junk2", D // 2)
        ss2 = [Buf("ss2_%d" % i, 8) for i in range(2)]
        for t in range(16):
            i2 = t % 2
            P.dma("sp", xt[i2].f, x[s, t * 128:(t + 1) * 128, :], writes=[xt[i2].reg], semreg=xt[i2].reg)
            if s + 1 < nseq:
                if t == 0:
                    stage1_dma(s + 1, 0)
                    stage1_dma(s + 1, 1)
                if t in (9, 13):
                    stage1_dma(s + 1, 2 + (t - 9) // 4)
                if t % 4 == 3:
                    stage1_front(s + 1, t // 4)
                if t >= 6 and t % 4 == 2:
                    stage1_back(s + 1, (t - 6) // 4)
            if "Wup" in p2w and s == nseq - 1 and t % 4 == 1:
                i_ = t // 4
                P.dma("sp", p2w["Wup"].b.rearrange("p (k n) -> p k n", k=KC)[:, :, i_ * 1024:(i_ + 1) * 1024],
                      wup_bf[:, i_ * 1024:(i_ + 1) * 1024].rearrange("(k p) n -> p k n", p=128),
                      reads=[wupd[i_]], writes=[p2w["wup_r"][i_]], semreg=p2w["wup_r"][i_])
            pr = next_pair((0, 2))

            def _mm(e, pr=pr, t=t):
                ins = None
                for h in range(2):
                    for kc in range(KC):
                        ins = e.matmul(bank_f[pr + h], mTv[:, kc, t * 128:(t + 1) * 128], Wov[:, kc, h * 512:(h + 1) * 512],
                                       start=(kc == 0), stop=(kc == KC - 1))
                return ins
            P.op("pe", _mm, reads=[mT.reg, Wo.reg], writes=[bank_reg[pr], bank_reg[pr + 1]])
            yv = psum[pr // 2]
            breg = [bank_reg[pr], bank_reg[pr + 1]]
            P.op("act", lambda e, yv=yv, i2=i2: e.activation(out=junk2.b,
                                                            in_=yv[:, :], func=AF.Square, scale=0.5,
                                                            accum_out=ss2[i2].f[:, 0:1]),
                 reads=breg, writes=[junk2.reg, ss2[i2].reg])
            rstd_op(ss2[i2].f[:, 0:1], ss2[i2].f[:, 0:1], 1, 1.0 / D, [ss2[i2].reg], [ss2[i2].reg], post=0.5, mode="sqrt")
            P.op("dve", lambda e, yv=yv, i2=i2: e.scalar_tensor_tensor(out=h1[i2].f, in0=yv[:, :], scalar=ss2[i2].f[:, 0:1],
                                                                      in1=gpost.f, op0=ALU.mult, op1=ALU.mult),
                 reads=breg + [ss2[i2].reg, gpost.reg], writes=[h1[i2].reg])
            P.op("dve" if s == nseq - 1 else "pool",
                 lambda e, i2=i2: e.tensor_tensor(out=h1[i2].f, in0=h1[i2].f, in1=xt[i2].f, op=ALU.add),
                 reads=[h1[i2].reg, xt[i2].reg], writes=[h1[i2].reg])
            P.dma("sp", out[s, t * 128:(t + 1) * 128, :], h1[i2].f, reads=[h1[i2].reg], writes=[h1dram[s][t]],
                  semreg=h1[i2].reg)
            out_regs.append(h1[i2].reg)
        if s + 1 < nseq:
            stage1_back(s + 1, 3)
        A.release(mS)

    p2w = {}
    wupd = [P.reg("wupd%d" % i) for i in range(4)]
    wdnd = [P.reg("wdnd%d" % i) for i in range(4)]
    conv = []
    for i in range(4):
        for rb in range(8):
            conv.append(lambda i=i, rb=rb: P.dma("pool", wup_bf[rb * 128:(rb + 1) * 128, i * 1024:(i + 1) * 1024],
                                                 w_up[rb * 128:(rb + 1) * 128, i * 1024:(i + 1) * 1024],
                                                 writes=[wupd[i]], semreg=wupd[i]))
    for i in range(4):
        for rb in range(8):
            r0 = i * 1024 + rb * 128
            conv.append(lambda i=i, r0=r0: P.dma("pool", wdn_bf[r0:r0 + 128, :], w_down[r0:r0 + 128, :],
                                                 writes=[wdnd[i]], semreg=wdnd[i]))
    h1dram = [[P.reg("h1d_%d_%d" % (s, t)) for t in range(16)] for s in range(nseq)]

    if 1 in phases:
        for s in range(nseq):
            phase1(s)

    def phase2():
        A.release(mP1)
        if "Wup" in p2w:
            Wup = p2w["Wup"]
            wup_r = p2w["wup_r"]
            A.top = Wup.st + Wup.n
            Wupv = Wup.b.rearrange("p (k n) -> p k n", k=KC)
        else:
            Wup = Buf("Wup", KC * 4096 // 2)
            Wupv = Wup.b.rearrange("p (k n) -> p k n", k=KC)
            wup_r = [A.reg("Wup_c%d" % i, Wup.st, Wup.n) for i in range(4)]
            while conv:
                conv.pop(0)()
            for i in range(4):
                P.dma("sp", Wupv[:, :, i * 1024:(i + 1) * 1024],
                      wup_bf[:, i * 1024:(i + 1) * 1024].rearrange("(k p) n -> p k n", p=128),
                      reads=[wupd[i]], writes=[wup_r[i]], semreg=wup_r[i])
        Wdn = Buf("Wdn", 32 * D // 2)
        Wdnv = Wdn.b.rearrange("p (k n) -> p k n", k=32)
        wdn_r = [A.reg("Wdn_c%d" % i, Wdn.st, Wdn.n) for i in range(4)]
        hin = [Buf("hin%d" % i, D) for i in range(2)]
        hres = [Buf("hres%d" % i, D) for i in range(2)]
        ot = [Buf("ot%d" % i, D) for i in range(2)]
        n2T = Buf("n2T", KC * 512 // 2)
        n2Tv = n2T.b.rearrange("p (k t) -> p k t", k=KC)
        hidT = Buf("hidT", 32 * 512 // 2)
        hidTv = hidT.b.rearrange("p (k t) -> p k t", k=32)
        rl = [Buf("rl%d" % i, 256) for i in range(2)]
        junk3 = Buf("junk3", D // 2)
        ss3 = [Buf("ss3_%d" % i, 8) for i in range(2)]
        ss4 = [Buf("ss4_%d" % i, 8) for i in range(2)]
        quads = [(s, q) for s in range(nseq) for q in range(4)]

        def pro_front(s, q, tt):
            hb = hin[tt % 2]
            sb = ss3[tt % 2]
            t = 4 * q + tt
            P.dma("sp", hb.f, out[s, t * 128:(t + 1) * 128, :], reads=[h1dram[s][t]], writes=[hb.reg], semreg=hb.reg)
            P.op("act", lambda e, hb=hb, sb=sb: e.activation(out=junk3.b, in_=hb.f, func=AF.Square,
                                                            accum_out=sb.f[:, 0:1]),
                 reads=[hb.reg], writes=[junk3.reg, sb.reg])
            rstd_op(sb.f[:, 0:1], sb.f[:, 0:1], 1, 1.0 / D, [sb.reg], [sb.reg], mode="sqrt")
            P.op("dve", lambda e, hb=hb, sb=sb: e.tensor_scalar(out=hb.f, in0=hb.f, scalar1=sb.f[:, 0:1], scalar2=None,
                                                               op0=ALU.mult),
                 reads=[hb.reg, sb.reg], writes=[hb.reg])

        def pro_back(s, q, tt):
            hb = hin[tt % 2]
            for half in range(2):
                bk = next_bank((6, 7))

                def _tr(e, bk=bk, hb=hb, half=half):
                    ins = None
                    for k4 in range(4):
                        kc = half * 4 + k4
                        ins = e.transpose(bank_f[bk][:, k4 * 128:(k4 + 1) * 128], hb.f[:, kc * 128:(kc + 1) * 128],
                                          identF.f)
                    return ins
                P.op("pe", _tr, reads=[hb.reg, identF.reg], writes=[bank_reg[bk]])
                for k4 in range(4):
                    kc = half * 4 + k4
                    dst = n2Tv[:, kc, tt * 128:(tt + 1) * 128]
                    if k4 % 2 == 0:
                        P.op("act", lambda e, bk=bk, k4=k4, kc=kc, dst=dst: e.activation(
                            out=dst, in_=bank_f[bk][:, k4 * 128:(k4 + 1) * 128], func=AF.Identity,
                            scale=gpre2.f[:, kc:kc + 1]),
                            reads=[bank_reg[bk], gpre2.reg], writes=[n2T.reg])
                    else:
                        P.op("dve", lambda e, bk=bk, k4=k4, kc=kc, dst=dst: e.tensor_scalar(
                            out=dst, in0=bank_f[bk][:, k4 * 128:(k4 + 1) * 128], scalar1=gpre2.f[:, kc:kc + 1],
                            scalar2=None, op0=ALU.mult),
                            reads=[bank_reg[bk], gpre2.reg], writes=[n2T.reg])

        def up(s, q, nxt=None):
            for fc in range(32):
                if nxt is not None and fc in (10, 22):
                    pro_front(nxt[0], nxt[1], 0 if fc == 10 else 1)
                bk = next_bank((0, 1, 2, 3))
                rb = rl[fc % 2]

                def _mm(e, bk=bk, fc=fc):
                    ins = None
                    for kc in range(KC):
                        ins = e.matmul(bank_f[bk], Wupv[:, kc, fc * 128:(fc + 1) * 128], n2Tv[:, kc, :],
                                       start=(kc == 0), stop=(kc == KC - 1))
                    return ins
                P.op("pe", _mm, reads=[wup_r[fc // 8], n2T.reg], writes=[bank_reg[bk]])
                P.op("act", lambda e, bk=bk, rb=rb: e.activation(out=rb.b, in_=bank_f[bk], func=AF.Relu),
                     reads=[bank_reg[bk]], writes=[rb.reg])
                P.op("dve", lambda e, bk=bk, rb=rb, fc=fc: e.tensor_tensor(out=hidTv[:, fc, :], in0=bank_f[bk], in1=rb.b,
                                                                          op=ALU.mult),
                     reads=[bank_reg[bk], rb.reg], writes=[hidT.reg])

        def down(s, q, mid=None):
            for tt in range(4):
                if tt == 2 and mid is not None:
                    mid()
                i2 = tt % 2
                t = 4 * q + tt
                pr = next_pair((4, 6))
                P.dma("sp", hres[i2].f, out[s, t * 128:(t + 1) * 128, :], reads=[h1dram[s][t]], writes=[hres[i2].reg],
                      semreg=hres[i2].reg)

                def _mm(e, pr=pr, tt=tt):
                    ins = None
                    for h in range(2):
                        for fc in range(32):
                            ins = e.matmul(bank_f[pr + h], hidTv[:, fc, tt * 128:(tt + 1) * 128],
                                           Wdnv[:, fc, h * 512:(h + 1) * 512], start=(fc == 0), stop=(fc == 31))
                    return ins
                P.op("pe", _mm, reads=[hidT.reg] + wdn_r, writes=[bank_reg[pr], bank_reg[pr + 1]])
                yv = psum[pr // 2]
                breg = [bank_reg[pr], bank_reg[pr + 1]]
                P.op("act", lambda e, yv=yv, i2=i2: e.activation(out=junk3.b, in_=yv[:, :], func=AF.Square,
                                                                accum_out=ss4[i2].f[:, 0:1]),
                     reads=breg, writes=[junk3.reg, ss4[i2].reg])
                rstd_op(ss4[i2].f[:, 0:1], ss4[i2].f[:, 0:1], 1, 1.0 / D, [ss4[i2].reg], [ss4[i2].reg], mode="sqrt")
                P.op("dve", lambda e, yv=yv, i2=i2: e.scalar_tensor_tensor(out=ot[i2].f, in0=yv[:, :],
                                                                          scalar=ss4[i2].f[:, 0:1], in1=gpost2.f,
                                                                          op0=ALU.mult, op1=ALU.mult),
                     reads=breg + [ss4[i2].reg, gpost2.reg], writes=[ot[i2].reg])
                P.op("pool", lambda e, i2=i2: e.tensor_tensor(out=ot[i2].f, in0=ot[i2].f, in1=hres[i2].f, op=ALU.add),
                     reads=[ot[i2].reg, hres[i2].reg], writes=[ot[i2].reg])
                P.dma("sp", out[s, t * 128:(t + 1) * 128, :], ot[i2].f, reads=[ot[i2].reg, hres[i2].reg],
                      writes=[h1dram[s][t]], semreg=ot[i2].reg)
                out_regs.append(ot[i2].reg)

        for tt in range(4):
            pro_front(quads[0][0], quads[0][1], tt) if tt < 2 else None
        pro_back(quads[0][0], quads[0][1], 0)
        pro_back(quads[0][0], quads[0][1], 1)
        for tt in (2, 3):
            pro_front(quads[0][0], quads[0][1], tt)
        for tt in (2, 3):
            pro_back(quads[0][0], quads[0][1], tt)
        while conv:
            conv.pop(0)()
        for i in range(4):
            P.dma("sp", Wdnv[:, i * 8:(i + 1) * 8, :],
                  wdn_bf[i * 1024:(i + 1) * 1024, :].rearrange("(k p) n -> p k n", p=128),
                  reads=[wdnd[i]], writes=[wdn_r[i]], semreg=wdn_r[i])
        for i, (s, q) in enumerate(quads):
            nxt = quads[i + 1] if i + 1 < len(quads) else None
            up(s, q, nxt)
            mid = None
            if nxt is not None:
                pro_back(nxt[0], nxt[1], 0)
                pro_back(nxt[0], nxt[1], 1)
                pro_front(nxt[0], nxt[1], 2)
                pro_front(nxt[0], nxt[1], 3)

                def mid(nxt=nxt):
                    pro_back(nxt[0], nxt[1], 2)
                    pro_back(nxt[0], nxt[1], 3)
            down(s, q, mid)

    if 2 in phases:
        phase2()

    seen = []
    for r in out_regs:
        if r not in seen:
            seen.append(r)
    P.final_wait("sp", seen)

    P.resolve()
    with stack:
        with nc.Block() as block:
            @block.tensor
            def _(e):
                P.emit_engine("pe", e)

            @block.scalar
            def _(e):
                P.emit_engine("act", e)

            @block.vector
            def _(e):
                P.emit_engine("dve", e)

            @block.gpsimd
            def _(e):
                P.emit_engine("pool", e)

            @block.sync
            def _(e):
                P.emit_engine("sp", e)
    return nc, A.peak


def _consts():
    c = {}
    c["c_ident"] = np.eye(128, dtype=np.float32)
    p = np.arange(128)[:, None]
    q = np.arange(128)[None, :]
    mA = (q >= p)
    mB = (q <= p)
    mA2 = mA & (p < 64)
    mB2 = mB & (p >= 64)
    mC = np.abs(q - p) <= 64
    c["c_masks"] = np.ascontiguousarray(np.concatenate([mA, mB, mA2, mB2, mC], axis=1).astype(np.float32))
    inv_freq = 500000.0 ** (-np.arange(0, 32, 2, dtype=np.float32) / 32.0)
    pos = (np.arange(16)[None, :, None] * 128 + np.arange(128)[:, None, None]).astype(np.float32)
    ang = pos * inv_freq[None, None, :].astype(np.float32)
    c["c_cos"] = np.ascontiguousarray(np.cos(ang).astype(np.float32).reshape(128, 256))
    c["c_sin"] = np.ascontiguousarray(np.sin(ang).astype(np.float32).reshape(128, 256))
    return c


def _prep_inputs(inputs):
    f = lambda a: np.ascontiguousarray(np.asarray(a, dtype=np.float32))
    shared = {
        "w_in": f(inputs["w_in"][0]),
        "w_spatial": f(inputs["w_spatial"][0]),
        "w_branch_a": f(inputs["w_branch_a"][0]),
        "w_branch_b": f(inputs["w_branch_b"][0]),
        "w_out": f(inputs["w_out"][0]),
        "w_up": f(inputs["w_up"][0]),
        "w_down": f(inputs["w_down"][0]),
        "c_gpre": f(np.asarray(inputs["norm_mix_pre"][0]).reshape(8, 128).T),
        "c_gpre2": f(np.asarray(inputs["norm_mlp_pre"][0]).reshape(8, 128).T),
        "c_lng": f(np.asarray(inputs["ln_v_gain"][0]).reshape(4, 128).T),
        "c_bgate": f(np.asarray(inputs["b_gate"][0]).reshape(16, 128).T),
        "c_lnb": f(np.asarray(inputs["ln_v_bias"][0]).reshape(1, 512)),
        "c_bsp": f(np.asarray(inputs["b_spatial"][0]).reshape(1, 512)),
        "c_gpost": f(np.broadcast_to(np.asarray(inputs["norm_mix_post"][0])[None, :], (128, D))),
        "c_gpost2": f(np.broadcast_to(np.asarray(inputs["norm_mlp_post"][0])[None, :], (128, D))),
    }
    shared.update(_consts())
    return shared


_CACHE = {}


def kernel(**inputs):
    x = np.asarray(inputs["x"], dtype=np.float32)
    shared = _prep_inputs(inputs)
    if "nc" not in _CACHE:
        _CACHE["nc"] = build(SEQ_PER_CORE)[0]
    nc = _CACHE["nc"]
    in_maps = []
    for c in range(NCORES):
        m = dict(shared)
        m["x"] = np.ascontiguousarray(x[c * SEQ_PER_CORE:(c + 1) * SEQ_PER_CORE])
        in_maps.append(m)
    res = run_bass_kernel_spmd(nc, in_maps, core_ids=list(range(NCORES)))
    outs = [np.asarray(r["out"], dtype=np.float32) for r in res.results]
    return np.concatenate(outs, axis=0)
```

```python
import math
import numpy as np
import concourse.bass as bass
import concourse.mybir as mybir
from concourse.bass_utils import run_bass_kernel_spmd

F32 = mybir.dt.float32
BF16 = mybir.dt.bfloat16
AF = mybir.ActivationFunctionType
ALU = mybir.AluOpType

S = 2048
D = 1024
KC = 8
PAD = 256
SP = S + 2 * PAD
EPS = 1e-6
NCORES = 8
SEQ_PER_CORE = 2
SCALE = 1.0 / math.sqrt(128.0)
SAME_ENGINE_SYNC = True
import os as _os
SKIP = set(_os.environ.get('KSKIP', '').split(','))
POOL_DMA_MAX_OUTSTANDING = 2


class Reg:
    __slots__ = ("name", "w", "r", "dsem", "dcnt", "alias", "excl")

    def __init__(self, name):
        self.name = name
        self.excl = False
        self.w = None
        self.r = []
        self.dsem = None
        self.dcnt = 0
        self.alias = []


class Op:
    __slots__ = ("eng", "fn", "waits", "inc", "count", "dreg", "dval")

    def __init__(self, eng, fn):
        self.eng = eng
        self.fn = fn
        self.waits = []
        self.inc = False
        self.count = None
        self.dreg = None
        self.dval = None


class Prog:
    ENGS = ("pe", "act", "dve", "pool", "sp")

    def __init__(self, nc, stack):
        self.nc = nc
        self.stack = stack
        self.ops = {e: [] for e in self.ENGS}
        self.sems = {e: stack.enter_context(nc.semaphore("s_" + e)) for e in ("pe", "act", "dve", "pool")}
        self.nreg = 0

    def reg(self, name, alias=()):
        r = Reg(name)
        r.alias = list(alias)
        return r

    def _dep(self, o, tok):
        if tok is None:
            return
        if tok[0] == "op":
            p = tok[1]
            if p.eng == o.eng and (p.eng in ("pe", "sp") or not SAME_ENGINE_SYNC):
                return
            p.inc = True
        o.waits.append(tok)

    def _deps(self, o, reads, writes):
        for r in reads:
            for a in r.alias:
                self._dep(o, a.w)
            self._dep(o, r.w)
            if r.excl:
                for t in r.r:
                    if t[0] == "op" and t[1].eng != o.eng:
                        self._dep(o, t)
        for w in writes:
            for a in w.alias:
                self._dep(o, a.w)
                for t in a.r:
                    self._dep(o, t)
            w.alias = []
            self._dep(o, w.w)
            for t in w.r:
                self._dep(o, t)

    def _mark(self, tok, reads, writes):
        for r in reads:
            key = tok[1].eng if tok[0] == "op" else ("d", id(tok[1]))
            r.r = [t for t in r.r if (t[1].eng if t[0] == "op" else ("d", id(t[1]))) != key]
            r.r.append(tok)
        for w in writes:
            w.w = tok
            w.r = []

    def op(self, eng, fn, reads=(), writes=()):
        o = Op(eng, fn)
        self._deps(o, reads, writes)
        self.ops[eng].append(o)
        self._mark(("op", o), reads, writes)
        return o

    def dma(self, eng, out, in_, reads=(), writes=(), semreg=None):
        if semreg.dsem is None:
            self.nreg += 1
            semreg.dsem = self.stack.enter_context(self.nc.semaphore("d%d_%s" % (self.nreg, semreg.name)))
        o = Op(eng, lambda e: e.dma_start(out=out, in_=in_))
        self._deps(o, reads, writes)
        if eng == "pool" and POOL_DMA_MAX_OUTSTANDING:
            hist = self.__dict__.setdefault("pool_dma_hist", [])
            if len(hist) >= POOL_DMA_MAX_OUTSTANDING:
                self._dep(o, hist[-POOL_DMA_MAX_OUTSTANDING])
            hist.append(("dma", semreg, 16 * (semreg.dcnt + 1)))
        semreg.dcnt += 1
        o.dreg = semreg
        o.dval = 16 * semreg.dcnt
        self.ops[eng].append(o)
        self._mark(("dma", semreg, o.dval), reads, writes)
        return o

    def resolve(self):
        for e in self.ENGS:
            c = 0
            for o in self.ops[e]:
                if o.inc:
                    c += 1
                    o.count = c
            self.maxcount = getattr(self, "maxcount", {})
            self.maxcount[e] = c

    def emit_engine(self, e, h):
        waited = {}
        for o in self.ops[e]:
            for tok in o.waits:
                if tok[0] == "op":
                    sem, val = self.sems[tok[1].eng], tok[1].count
                else:
                    sem, val = tok[1].dsem, tok[2]
                k = id(sem)
                if waited.get(k, 0) >= val:
                    continue
                waited[k] = val
                h.wait_ge(sem, val)
            if o.fn is None:
                continue
            ins = o.fn(h)
            if o.dreg is not None:
                ins.then_inc(o.dreg.dsem, 16)
            elif o.inc:
                ins.then_inc(self.sems[e], 1)

    def final_wait(self, eng, regs):
        o = Op(eng, None)
        for r in regs:
            self._dep(o, r.w)
            for t in r.r:
                self._dep(o, t)
        self.ops[eng].append(o)
        return o


def build(nseq=SEQ_PER_CORE, debug=None, phases=(1, 2), upto=99):
    from contextlib import ExitStack

    nc = bass.Bass("TRN2", target_bir_lowering=False)
    stack = ExitStack()

    def din(name, shape, dt=F32):
        return nc.dram_tensor(name, list(shape), dt, kind="ExternalInput").ap()

    x = din("x", [nseq, S, D])
    w_in = din("w_in", [D, 7680])
    w_sp = din("w_spatial", [4, 128, 128])
    w_a = din("w_branch_a", [512, D])
    w_b = din("w_branch_b", [512, D])
    w_out = din("w_out", [D, D])
    w_up = din("w_up", [D, 4096])
    w_down = din("w_down", [4096, D])
    c_ident = din("c_ident", [128, 128])
    c_masks = din("c_masks", [128, 640])
    c_cos = din("c_cos", [128, 256])
    c_sin = din("c_sin", [128, 256])
    c_gpre = din("c_gpre", [128, 8])
    c_gpre2 = din("c_gpre2", [128, 8])
    c_lng = din("c_lng", [128, 4])
    c_bgate = din("c_bgate", [128, 16])
    c_lnb = din("c_lnb", [1, 512])
    c_bsp = din("c_bsp", [1, 512])
    c_gpost = din("c_gpost", [128, D])
    c_gpost2 = din("c_gpost2", [128, D])
    out = nc.dram_tensor("out", [nseq, S, D], F32, kind="ExternalOutput").ap()
    dbg = {}
    if debug:
        for name, shape in debug.items():
            dbg[name] = nc.dram_tensor(name, list(shape), F32, kind="ExternalOutput").ap()

    ARENA_F = 53200
    arena = stack.enter_context(nc.sbuf_tensor("arena", [128, ARENA_F], F32))
    psum = [stack.enter_context(nc.psum_tensor("ps%d" % i, [128, 1024], F32)) for i in range(4)]
    P = Prog(nc, stack)

    class Arena:
        def __init__(self):
            self.top = 0
            self.hist = []
            self.peak = 0

        def alloc(self, name, nwords):
            nwords = (nwords + 7) // 8 * 8
            st = self.top
            self.top += nwords
            assert self.top <= ARENA_F, "SBUF arena overflow at %s: %d" % (name, self.top)
            self.peak = max(self.peak, self.top)
            return st

        def reg(self, name, st, nwords):
            en = st + nwords
            al = [r for (a, b, r) in self.hist if a < en and st < b]
            r = P.reg(name, al)
            self.hist.append((st, en, r))
            return r

        def mark(self):
            return self.top

        def release(self, m):
            self.top = m

    A = Arena()

    class Buf:
        def __init__(self, name, nwords, nregs=1):
            self.st = A.alloc(name, nwords)
            self.n = nwords
            self.f = arena[:, self.st:self.st + nwords]
            self.b = arena[:, self.st:self.st + nwords].bitcast(BF16)
            self.reg = A.reg(name, self.st, nwords)

    def dump(name, src_ap, reg, n, isbf=False):
        if not (debug and name in dbg):
            return
        if isbf:
            stg = Buf("stg_" + name, n)
            P.op("dve", lambda e, stg=stg: e.tensor_copy(out=stg.f, in_=src_ap), reads=[reg], writes=[stg.reg])
            P.dma("sp", dbg[name], stg.f, reads=[stg.reg], writes=[], semreg=stg.reg)
            out_regs.append(stg.reg)
        else:
            P.dma("sp", dbg[name], src_ap, reads=[reg], writes=[], semreg=reg)
            out_regs.append(reg)

    out_regs = []
    bank_f = []
    bank_reg = []
    for i in range(8):
        bank_f.append(psum[i // 2][:, (i % 2) * 512:(i % 2) * 512 + 512])
        bank_reg.append(P.reg("bank%d" % i))
        bank_reg[-1].excl = True
    bank_ctr = {}
    ALLB = (0, 1, 2, 3, 4, 5, 6, 7)
    bank_pool = [ALLB]

    def next_bank(pool=None):
        pool = pool or bank_pool[0]
        c = bank_ctr.get(pool, 0)
        bank_ctr[pool] = c + 1
        return pool[c % len(pool)]

    def next_pair(pool=None):
        pool = pool or (0, 2, 4, 6)
        key = ("pair",) + pool
        c = bank_ctr.get(key, 0)
        bank_ctr[key] = c + 1
        return pool[c % len(pool)]

    identF = Buf("identF", 128)
    gpre = Buf("gpre", 8)
    gpre2 = Buf("gpre2", 8)
    lng = Buf("lng", 8)
    hbg = Buf("hbg", 16)
    gpost2 = Buf("gpost2", 1024)
    neghalf = Buf("neghalf", 8)
    epsT = Buf("epsT", 8)
    gpost = Buf("gpost", 1024)
    mP1 = A.mark()
    identB = Buf("identB", 64)
    masks = Buf("masks", 320)
    onesB = Buf("onesB", 64)
    cosT = Buf("cosT", 256)
    sinT = Buf("sinT", 256)
    Cg = Buf("Cg", 512)
    WsT = Buf("WsT", 256)
    constR = P.reg("constdma")

    def sq(eng):
        return eng

    P.dma("sp", identF.f, c_ident, writes=[identF.reg], semreg=identF.reg)
    P.dma("sp", cosT.f, c_cos, writes=[cosT.reg], semreg=cosT.reg)
    P.dma("sp", sinT.f, c_sin, writes=[sinT.reg], semreg=sinT.reg)
    P.dma("sp", gpre.f, c_gpre, writes=[gpre.reg], semreg=gpre.reg)
    P.dma("sp", gpre2.f, c_gpre2, writes=[gpre2.reg], semreg=gpre2.reg)
    P.dma("sp", lng.f[:, 0:4], c_lng, writes=[lng.reg], semreg=lng.reg)
    P.dma("sp", hbg.f, c_bgate, writes=[hbg.reg], semreg=hbg.reg)
    P.dma("sp", gpost.f, c_gpost, writes=[gpost.reg], semreg=gpost.reg)
    P.dma("sp", gpost2.f, c_gpost2, writes=[gpost2.reg], semreg=gpost2.reg)
    P.dma("pool", masks.b, c_masks, writes=[masks.reg], semreg=masks.reg)

    P.op("dve", lambda e: e.tensor_scalar(out=hbg.f, in0=hbg.f, scalar1=0.5, scalar2=None, op0=ALU.mult),
         reads=[hbg.reg], writes=[hbg.reg])
    P.op("dve", lambda e: e.memset(onesB.b, 1.0), writes=[onesB.reg])
    P.op("dve", lambda e: e.memset(neghalf.f, -0.5), writes=[neghalf.reg])
    P.op("dve", lambda e: e.memset(epsT.f[:, 0:1], EPS), writes=[epsT.reg])
    P.op("dve", lambda e: e.memset(epsT.f[:, 1:2], 4.0 * EPS), writes=[epsT.reg])
    P.op("dve", lambda e: e.tensor_copy(out=identB.b, in_=identF.f), reads=[identF.reg], writes=[identB.reg])

    def rstd_op(dst, src, n, mul, reads, writes, post=1.0, mode="pow"):
        if mode == "pow":
            P.op("dve", lambda e: e.tensor_scalar(out=dst, in0=src, scalar1=mul, scalar2=EPS, op0=ALU.mult, op1=ALU.add),
                 reads=reads, writes=writes)
            P.op("pool", lambda e: e.tensor_tensor(out=dst, in0=dst, in1=neghalf.f[:, 0:n], op=ALU.pow),
                 reads=writes + [neghalf.reg], writes=writes)
            if post != 1.0:
                P.op("dve", lambda e: e.tensor_scalar(out=dst, in0=dst, scalar1=post, scalar2=None, op0=ALU.mult),
                     reads=writes, writes=writes)
        else:
            assert post in (1.0, 0.5)
            bcol = 0 if post == 1.0 else 1
            P.op("act", lambda e: e.activation(out=dst, in_=src, func=AF.Sqrt, scale=mul / (post * post),
                                               bias=epsT.f[:, bcol:bcol + 1]),
                 reads=reads + [epsT.reg], writes=writes)
            P.op("dve", lambda e: e.reciprocal(out=dst, in_=dst), reads=writes, writes=writes)

    m0 = A.mark()
    wsp = Buf("wsp", 512)
    wsTf = Buf("wsTf", 512)
    rows = Buf("rows", 512 * 3 + 128)
    onesF = Buf("onesF", 8)
    P.dma("sp", wsp.f.rearrange("p (g s) -> p g s", g=4), w_sp.rearrange("g t s -> t g s"),
          writes=[wsp.reg], semreg=wsp.reg)
    P.dma("sp", rows.f[0:1, 512:1024], c_lnb, writes=[rows.reg], semreg=rows.reg)
    P.dma("sp", rows.f[0:1, 1024:1536], c_bsp, writes=[rows.reg], semreg=rows.reg)
    P.op("dve", lambda e: e.memset(rows.f[0:1, 1536:1664], 1.0), reads=[], writes=[rows.reg])
    P.op("dve", lambda e: e.memset(onesF.f, 1.0), writes=[onesF.reg])
    bk = next_bank()

    def _tr_ws(e, bk=bk):
        ins = None
        for g in range(4):
            ins = e.transpose(bank_f[bk][:, g * 128:(g + 1) * 128], wsp.f[:, g * 128:(g + 1) * 128], identF.f)
        return ins
    P.op("pe", _tr_ws, reads=[wsp.reg, identF.reg], writes=[bank_reg[bk]])
    P.op("act", lambda e, bk=bk: e.copy(out=wsTf.f, in_=bank_f[bk]), reads=[bank_reg[bk]], writes=[wsTf.reg])
    P.op("dve", lambda e, bk=bk: e.tensor_copy(out=WsT.b, in_=bank_f[bk]), reads=[bank_reg[bk]], writes=[WsT.reg])
    bk = next_bank()

    def _rs(e, bk=bk):
        ins = None
        for g in range(4):
            ins = e.matmul(bank_f[bk][0:1, g * 128:(g + 1) * 128], onesF.f[:, 0:1], wsTf.f[:, g * 128:(g + 1) * 128],
                           start=True, stop=True)
        return ins
    P.op("pe", _rs, reads=[wsTf.reg, onesF.reg], writes=[bank_reg[bk]])
    P.op("act", lambda e, bk=bk: e.copy(out=rows.f[0:1, 0:512], in_=bank_f[bk][0:1, :]),
         reads=[bank_reg[bk]], writes=[rows.reg])
    bk = next_bank()

    def _cg(e, bk=bk):
        ins = None
        for g in range(4):
            sl = slice(g * 128, (g + 1) * 128)
            e.matmul(bank_f[bk][:, sl], rows.f[0:1, 512 + g * 128:512 + (g + 1) * 128], rows.f[0:1, sl],
                     start=True, stop=False)
            ins = e.matmul(bank_f[bk][:, sl], rows.f[0:1, 1536:1664], rows.f[0:1, 1024 + g * 128:1024 + (g + 1) * 128],
                           start=False, stop=True)
        return ins
    P.op("pe", _cg, reads=[rows.reg], writes=[bank_reg[bk]])
    P.op("act", lambda e, bk=bk: e.copy(out=Cg.f, in_=bank_f[bk]), reads=[bank_reg[bk]], writes=[Cg.reg])
    dump("dbg_Cg", Cg.f, Cg.reg, 512)
    dump("dbg_wsTf", wsTf.f, wsTf.reg, 512)
    dump("dbg_rows", rows.f[0:1, :], rows.reg, 1664)
    dump("dbg_WsT", WsT.b, WsT.reg, 512, isbf=True)
    A.release(m0)

    def wload(buf_ap, src_ap, reg, reads=()):
        P.dma("pool", buf_ap, src_ap, reads=list(reads), writes=[reg], semreg=reg)

    nT = Buf("nT", KC * SP // 2)
    nTv = nT.b.rearrange("p (k t) -> p k t", k=KC)
    yaT = Buf("yaT", 4 * S // 2)
    yaTv = yaT.b.rearrange("p (k t) -> p k t", k=4)
    ybT = Buf("ybT", 4 * S // 2)
    ybTv = ybT.b.rearrange("p (k t) -> p k t", k=4)
    junk = Buf("junk", D // 2)
    Wu = Buf("Wu", KC * 512 // 2)
    Wuv = Wu.b.rearrange("p (k n) -> p k n", k=KC)
    Wv = Buf("Wv", KC * 512 // 2)
    Wvv = Wv.b.rearrange("p (k n) -> p k n", k=KC)
    ssb = [Buf("ss%d" % i, 8) for i in range(2)]
    P.op("pool", lambda e: e.memset(nTv[:, :, 0:PAD], 0.0), writes=[nT.reg])
    P.op("pool", lambda e: e.memset(nTv[:, :, PAD + S:SP], 0.0), writes=[nT.reg])
    xq_f = [yaT.f, ybT.f]
    xq_reg = [yaT.reg, ybT.reg]

    def stage1_dma(s, q):
        xv = xq_f[q % 2].rearrange("p (t d) -> p t d", t=4)
        P.dma("sp", xv, x[s, q * 512:(q + 1) * 512, :].rearrange("(t p) d -> p t d", p=128),
              writes=[xq_reg[q % 2]], semreg=xq_reg[q % 2])

    def stage1_front(s, q):
        xf = xq_f[q % 2]
        xreg = xq_reg[q % 2]
        ss = ssb[q % 2]
        xv = xf.rearrange("p (t d) -> p t d", t=4)
        for tt in range(4):
            P.op("act", lambda e, xv=xv, tt=tt, ss=ss: e.activation(out=junk.b, in_=xv[:, tt, :], func=AF.Square,
                                                                     accum_out=ss.f[:, tt:tt + 1]),
                 reads=[xreg], writes=[junk.reg, ss.reg])
        rstd_op(ss.f[:, 0:4], ss.f[:, 0:4], 4, 1.0 / D, [ss.reg], [ss.reg], mode="sqrt")
        for tt in range(4):
            P.op("dve", lambda e, xv=xv, tt=tt, ss=ss: e.tensor_scalar(out=xv[:, tt, :], in0=xv[:, tt, :],
                                                                        scalar1=ss.f[:, tt:tt + 1], scalar2=None,
                                                                        op0=ALU.mult),
                 reads=[xreg, ss.reg], writes=[xreg])

    def stage1_back(s, q):
        xf = xq_f[q % 2]
        xreg = xq_reg[q % 2]
        xv = xf.rearrange("p (t d) -> p t d", t=4)
        for kc in range(KC):
            bk = next_bank((4, 5, 6, 7))

            def _tr(e, bk=bk, xv=xv, kc=kc):
                ins = None
                for tt in range(4):
                    ins = e.transpose(bank_f[bk][:, tt * 128:(tt + 1) * 128], xv[:, tt, kc * 128:(kc + 1) * 128],
                                      identF.f)
                return ins
            P.op("pe", _tr, reads=[xreg, identF.reg], writes=[bank_reg[bk]])
            dst = nTv[:, kc, PAD + q * 512:PAD + (q + 1) * 512]
            if kc % 2 == 0:
                P.op("act", lambda e, bk=bk, dst=dst, kc=kc: e.activation(out=dst, in_=bank_f[bk], func=AF.Identity,
                                                                           scale=gpre.f[:, kc:kc + 1]),
                     reads=[bank_reg[bk], gpre.reg], writes=[nT.reg])
            else:
                P.op("dve", lambda e, bk=bk, dst=dst, kc=kc: e.tensor_scalar(out=dst, in0=bank_f[bk],
                                                                              scalar1=gpre.f[:, kc:kc + 1],
                                                                              scalar2=None, op0=ALU.mult),
                     reads=[bank_reg[bk], gpre.reg], writes=[nT.reg])

    def stage1_quad(s, q, after_dma=None, dma=True):
        if dma:
            stage1_dma(s, q)
        if after_dma is not None:
            after_dma(xq_reg[q % 2])
        stage1_front(s, q)
        stage1_back(s, q)

    def phase1(s):
        mS = A.mark()

        mQ = A.mark()
        Wqk = [Buf("Wqk%d" % i, KC * 768 // 2) for i in range(2)]
        Wv3 = [Buf("Wv3_%d" % i, KC * 384 // 2) for i in range(2)]
        mW = A.mark()

        def slot_w_thunks(j):
            b = j % 2
            th = []
            src = w_in[:, 1024:4096].rearrange("(k p) (m c) -> p k m c", p=128, c=512)
            dstw = Wqk[b].b.rearrange("p (k m c) -> p k m c", k=KC, m=6)
            for m_ in range(6):
                th.append(lambda m_=m_, src=src, dstw=dstw: wload(dstw[:, :, m_, :], src[:, :, m_, j * 128:(j + 1) * 128],
                                                                  Wqk[b].reg))
            src2 = w_in[:, 4096:5632].rearrange("(k p) (m c) -> p k m c", p=128, c=512)
            dstw2 = Wv3[b].b.rearrange("p (k m c) -> p k m c", k=KC, m=3)
            for m_ in range(3):
                th.append(lambda m_=m_, src2=src2, dstw2=dstw2: wload(dstw2[:, :, m_, :],
                                                                      src2[:, :, m_, j * 128:(j + 1) * 128], Wv3[b].reg))
            return th

        def _wuv(xreg):
            wload(Wuv, w_in[:, 0:512].rearrange("(k p) n -> p k n", p=128), Wu.reg, reads=[xreg])
            wload(Wvv, w_in[:, 512:1024].rearrange("(k p) n -> p k n", p=128), Wv.reg)
        if s == 0:
            for q in range(4):
                stage1_quad(0, q, after_dma=_wuv if q == 0 else None)
        m1 = A.mark()
        if debug and "dbg_nT" in dbg and s == 0:
            stg = Buf("dbgstg", KC * SP)
            P.op("dve", lambda e, stg=stg: e.tensor_copy(out=stg.f, in_=nT.b), reads=[nT.reg], writes=[stg.reg])
            P.dma("sp", dbg["dbg_nT"], stg.f, reads=[stg.reg], writes=[], semreg=stg.reg)
            out_regs.append(stg.reg)
            A.release(m1)

        if upto < 2:
            return
        m2 = A.mark()
        uT = Buf("uT", 4 * S // 2)
        uTv = uT.b.rearrange("p (k t) -> p k t", k=4)
        NB2 = 4
        vg = [Buf("vg%d" % i, 512) for i in range(NB2)]
        vn = [Buf("vn%d" % i, 256) for i in range(NB2)]
        tmpg = [Buf("tmpg%d" % i, 512) for i in range(NB2)]
        st6 = [Buf("st6_%d" % i, 8) for i in range(NB2)]
        mv = [Buf("mv%d" % i, 8) for i in range(NB2)]

        def u_group(q, c):
            bk = next_bank((3, 4, 5))

            def _mm(e, bk=bk, q=q, c=c):
                ins = None
                for kc in range(KC):
                    ins = e.matmul(bank_f[bk], Wuv[:, kc, c * 128:(c + 1) * 128],
                                   nTv[:, kc, PAD + q * 512:PAD + (q + 1) * 512], start=(kc == 0), stop=(kc == KC - 1))
                return ins
            P.op("pe", _mm, reads=[Wu.reg, nT.reg], writes=[bank_reg[bk]])
            P.op("act", lambda e, bk=bk, q=q, c=c: e.activation(out=uTv[:, c, q * 512:(q + 1) * 512], in_=bank_f[bk],
                                                                 func=AF.Gelu_apprx_tanh),
                 reads=[bank_reg[bk]], writes=[uT.reg])

        def v_front(t):
            bk = next_bank((0, 1, 2))
            i2 = t % NB2

            def _mm(e, bk=bk, t=t):
                ins = None
                for kc in range(KC):
                    ins = e.matmul(bank_f[bk], nTv[:, kc, PAD + t * 128:PAD + (t + 1) * 128], Wvv[:, kc, :],
                                   start=(kc == 0), stop=(kc == KC - 1))
                return ins
            P.op("pe", _mm, reads=[Wv.reg, nT.reg], writes=[bank_reg[bk]])
            P.op("act", lambda e, bk=bk, i2=i2: e.activation(out=vg[i2].f, in_=bank_f[bk], func=AF.Gelu_apprx_tanh),
                 reads=[bank_reg[bk]], writes=[vg[i2].reg])
            P.op("dve", lambda e, i2=i2: e.bn_stats(out=st6[i2].f[:, 0:6], in_=vg[i2].f), reads=[vg[i2].reg],
                 writes=[st6[i2].reg])
            P.op("dve", lambda e, i2=i2: e.bn_aggr(out=mv[i2].f[:, 0:2], in_=st6[i2].f[:, 0:6]), reads=[st6[i2].reg],
                 writes=[mv[i2].reg])
            rstd_op(mv[i2].f[:, 1:2], mv[i2].f[:, 1:2], 1, 1.0, [mv[i2].reg], [mv[i2].reg])
            P.op("dve", lambda e, i2=i2: e.tensor_scalar(out=vn[i2].b, in0=vg[i2].f, scalar1=mv[i2].f[:, 0:1],
                                                          scalar2=mv[i2].f[:, 1:2], op0=ALU.subtract, op1=ALU.mult),
                 reads=[vg[i2].reg, mv[i2].reg], writes=[vn[i2].reg])

        def v_back(t):
            i2 = t % NB2
            bk2 = next_bank((6, 7))

            def _sp(e, bk2=bk2, i2=i2):
                ins = None
                for g in range(4):
                    sl = slice(g * 128, (g + 1) * 128)
                    ins = e.matmul(bank_f[bk2][:, sl], vn[i2].b[:, sl], WsT.b[:, sl], start=True, stop=True)
                return ins
            P.op("pe", _sp, reads=[vn[i2].reg, WsT.reg], writes=[bank_reg[bk2]])
            for g in range(4):
                sl = slice(g * 128, (g + 1) * 128)
                P.op("dve", lambda e, bk2=bk2, i2=i2, g=g, sl=sl: e.scalar_tensor_tensor(
                    out=tmpg[i2].f[:, sl], in0=bank_f[bk2][:, sl], scalar=lng.f[:, g:g + 1], in1=Cg.f[:, sl],
                    op0=ALU.mult, op1=ALU.add),
                    reads=[bank_reg[bk2], lng.reg, Cg.reg], writes=[tmpg[i2].reg])
            P.op("pool", lambda e, i2=i2, t=t: e.tensor_tensor(
                out=yaTv[:, :, t * 128:(t + 1) * 128], in0=tmpg[i2].f.rearrange("p (g t) -> p g t", g=4),
                in1=uTv[:, :, t * 128:(t + 1) * 128], op=ALU.mult),
                reads=[tmpg[i2].reg, uT.reg], writes=[yaT.reg])

        SK2 = 3
        sw0 = slot_w_thunks(0)
        for t in range(16 + SK2):
            if t < len(sw0):
                sw0[t]()
            if t < 16:
                if t % 4 == 0:
                    for c in range(4):
                        u_group(t // 4, c)
                v_front(t)
            if t >= SK2:
                v_back(t - SK2)
        dump("dbg_uT", uT.b, uT.reg, 4 * S, isbf=True)
        A.release(m2)
        if debug and "dbg_yaT" in dbg and s == 0:
            stg = Buf("dbgstg2", 4 * S)
            P.op("dve", lambda e, stg=stg: e.tensor_copy(out=stg.f, in_=yaT.b), reads=[yaT.reg], writes=[stg.reg])
            P.dma("sp", dbg["dbg_yaT"], stg.f, reads=[stg.reg], writes=[], semreg=stg.reg)
            out_regs.append(stg.reg)
            A.release(m2)

        if upto < 2.05:
            return
        A.release(mW)
        m3 = A.mark()
        qks = [Buf("qks%d" % i, 768 // 2) for i in range(3)]
        rts = [[Buf("rt%d_%d" % (i, k), 96) for k in range(4)] for i in range(2)]
        qT = Buf("qT", 3 * S // 2)
        qTv = qT.b.rearrange("p (g t) -> p g t", g=3)
        kT = Buf("kT", 3 * SP // 2)
        kTv = kT.b.rearrange("p (g t) -> p g t", g=3)
        VB0 = (0, 17, 37)
        Vt = Buf("Vt", 53 * 128 // 2)
        Vtv = Vt.b.rearrange("p (b d) -> p b d", b=53)
        NPT = 5
        LA = 3
        PT = [Buf("PT%d" % i, 128) for i in range(NPT)]
        accU = Buf("accU", S)
        accD = Buf("accD", S)
        P.op("pool", lambda e: e.memset(kTv[:, :, 0:PAD], 0.0), writes=[kT.reg])
        P.op("pool", lambda e: e.memset(kTv[:, :, PAD + S:SP], 0.0), writes=[kT.reg])

        blocks = []
        for b in range(17):
            blocks.append((0, slice(PAD + 128 * b - 64, PAD + 128 * b + 64)))
        for r in range(4):
            for b in range(5):
                st = PAD + 4 * (128 * b - 64) + r
                blocks.append((1, slice(st, st + 4 * 127 + 1, 4)))
        for r in range(16):
            blocks.append((2, slice(PAD + r, PAD + r + 16 * 127 + 1, 16)))
        vgroups = []
        vb = 0
        while vb < 53:
            g = blocks[vb][0]
            nb = 0
            while vb + nb < 53 and nb < 4 and blocks[vb + nb][0] == g:
                nb += 1
            vgroups.append((vb, nb, g))
            vb += nb

        mAB = masks.b[:, 0:256]
        mA2 = masks.b[:, 256:384]
        mB2 = masks.b[:, 384:512]
        mC = masks.b[:, 512:640]
        tasks = []
        for g in range(3):
            d = (1, 4, 16)[g]
            L = S // d
            if g < 2:
                nqb = L // 128
                for r in range(d):
                    for b in range(nqb + 1):
                        ks = PAD + d * (128 * b - 64) + r
                        ksl = slice(ks, ks + d * 127 + 1, d) if d > 1 else slice(ks, ks + 128)
                        qlo = max(b - 1, 0)
                        qhi = min(b, nqb - 1)
                        mk = mB2 if b == 0 else (mA2 if b == nqb else mAB)
                        tasks.append(dict(g=g, r=r, b=b, nqb=nqb, L=L, ksl=ksl,
                                          qsl=slice(r * L + 128 * qlo, r * L + 128 * (qhi + 1)),
                                          nq=(qhi - qlo + 1) * 128, mk=mk, vbi=VB0[g] + r * (nqb + 1) + b))
            else:
                for r in range(16):
                    tasks.append(dict(g=2, r=r, b=0, nqb=1, L=128, ksl=slice(PAD + r, PAD + r + 16 * 127 + 1, 16),
                                      qsl=slice(r * 128, (r + 1) * 128), nq=128, mk=mC, vbi=VB0[2] + r))

        pt_ctr = [0]

        def finalize_chunk(j, c4):
            sl = slice(c4 * 512, (c4 + 1) * 512)
            P.op("dve", lambda e, sl=sl: e.reciprocal(out=accD.f[:, sl], in_=accD.f[:, sl]), reads=[accD.reg],
                 writes=[accD.reg])
            P.op("dve", lambda e, j=j, sl=sl: e.tensor_tensor(out=ybTv[:, j, sl], in0=accU.f[:, sl], in1=accD.f[:, sl],
                                                             op=ALU.mult),
                 reads=[accU.reg, accD.reg], writes=[ybT.reg])

        for j in range(4):
            swn = slot_w_thunks(j + 1) if j + 1 < 4 else []
            Wq = Wqk[j % 2]
            Wqv = Wq.b.rearrange("p (k n) -> p k n", k=KC)
            Wvj = Wv3[j % 2]
            Wvjv = Wvj.b.rearrange("p (k n) -> p k n", k=KC)

            def qk_front(t, Wqv=Wqv, Wq=Wq):
                pr = next_pair((0, 2))
                qs = qks[t % 3]
                qsv = qs.b.rearrange("p (m d) -> p m d", m=6)
                rt = rts[t % 2]

                def _mm(e, pr=pr, t=t, Wqv=Wqv):
                    ins = None
                    for h in range(2):
                        for kc in range(KC):
                            ins = e.matmul(bank_f[pr + h][:, 0:384], nTv[:, kc, PAD + t * 128:PAD + (t + 1) * 128],
                                           Wqv[:, kc, h * 384:(h + 1) * 384], start=(kc == 0), stop=(kc == KC - 1))
                    return ins
                P.op("pe", _mm, reads=[Wq.reg, nT.reg], writes=[bank_reg[pr], bank_reg[pr + 1]])
                pv = psum[pr // 2].rearrange("p (h n) -> p h n", h=2)[:, :, 0:384].rearrange("p h (m d) -> p h m d", m=3)
                breg = [bank_reg[pr], bank_reg[pr + 1]]
                cosb = cosT.f[:, t * 16:(t + 1) * 16].unsqueeze(1).unsqueeze(1).broadcast_to([128, 2, 3, 16])
                sinb = sinT.f[:, t * 16:(t + 1) * 16].unsqueeze(1).unsqueeze(1).broadcast_to([128, 2, 3, 16])
                x1 = pv[:, :, :, 0:16]
                x2 = pv[:, :, :, 16:32]
                qs4 = qs.b.rearrange("p (h m d) -> p h m d", h=2, m=3)

                def v4(bf):
                    return bf.f.rearrange("p (h m d) -> p h m d", h=2, m=3)
                for k_, (xa, tb_) in enumerate(((x1, cosb), (x2, sinb), (x2, cosb), (x1, sinb))):
                    P.op("dve", lambda e, xa=xa, tb_=tb_, o_=rt[k_]: e.tensor_tensor(out=v4(o_), in0=xa, in1=tb_, op=ALU.mult),
                         reads=breg + [cosT.reg, sinT.reg], writes=[rt[k_].reg])
                for h in range(2):
                    P.op("act", lambda e, pv=pv, h=h, qsv=qsv: e.copy(out=qsv[:, h * 3:(h + 1) * 3, 32:128],
                                                                    in_=pv[:, h, :, 32:128]),
                         reads=[breg[h]], writes=[qs.reg])
                P.op("pool", lambda e, qs4=qs4, rt=rt: e.tensor_tensor(out=qs4[:, :, :, 0:16], in0=v4(rt[0]), in1=v4(rt[1]),
                                                                        op=ALU.subtract),
                     reads=[rt[0].reg, rt[1].reg], writes=[qs.reg])
                P.op("pool", lambda e, qs4=qs4, rt=rt: e.tensor_tensor(out=qs4[:, :, :, 16:32], in0=v4(rt[2]), in1=v4(rt[3]),
                                                                        op=ALU.add),
                     reads=[rt[2].reg, rt[3].reg], writes=[qs.reg])

            def qk_back(t):
                qs = qks[t % 3]
                qsv = qs.b.rearrange("p (m d) -> p m d", m=6)
                bt = next_bank((4, 5))
                btb = bank_f[bt].bitcast(BF16)

                def _tr(e, btb=btb, qsv=qsv):
                    ins = None
                    for m in range(6):
                        ins = e.transpose(btb[:, m * 128:(m + 1) * 128], qsv[:, m, :], identB.b)
                    return ins
                P.op("pe", _tr, reads=[qs.reg, identB.reg], writes=[bank_reg[bt]])
                btv = btb[:, 0:768].rearrange("p (m t) -> p m t", m=6)
                P.op("act", lambda e, btv=btv, t=t: e.copy(out=kTv[:, :, PAD + t * 128:PAD + (t + 1) * 128],
                                                          in_=btv[:, 3:6, :]),
                     reads=[bank_reg[bt]], writes=[kT.reg])
                P.op("act", lambda e, btv=btv, t=t: e.copy(
                    out=qTv[:, 2, :].rearrange("p (r i) -> p r i", r=16)[:, :, 8 * t:8 * t + 8],
                    in_=btv[:, 2, :].rearrange("p (i r) -> p r i", r=16)),
                    reads=[bank_reg[bt]], writes=[qT.reg])
                P.op("dve", lambda e, btv=btv, t=t: e.tensor_copy(out=qTv[:, 0, t * 128:(t + 1) * 128], in_=btv[:, 0, :]),
                     reads=[bank_reg[bt]], writes=[qT.reg])
                P.op("dve", lambda e, btv=btv, t=t: e.tensor_copy(
                    out=qTv[:, 1, :].rearrange("p (r i) -> p r i", r=4)[:, :, 32 * t:32 * t + 32],
                    in_=btv[:, 1, :].rearrange("p (i r) -> p r i", r=4)),
                    reads=[bank_reg[bt]], writes=[qT.reg])

            def v_group(gi, Wvjv=Wvjv, Wvj=Wvj):
                vb, nb, g = vgroups[gi]
                bk = next_bank((6, 7))

                def _mm(e, bk=bk, vb=vb, nb=nb, g=g, Wvjv=Wvjv):
                    ins = None
                    for i in range(nb):
                        tok = blocks[vb + i][1]
                        for kc in range(KC):
                            ins = e.matmul(bank_f[bk][:, i * 128:(i + 1) * 128], nTv[:, kc, tok],
                                           Wvjv[:, kc, g * 128:(g + 1) * 128], start=(kc == 0), stop=(kc == KC - 1))
                    return ins
                P.op("pe", _mm, reads=[nT.reg, Wvj.reg], writes=[bank_reg[bk]])
                dstv = Vt.b[:, vb * 128:(vb + nb) * 128]
                if gi % 2 == 0:
                    P.op("act", lambda e, bk=bk, dstv=dstv, nb=nb: e.copy(out=dstv, in_=bank_f[bk][:, 0:nb * 128]),
                         reads=[bank_reg[bk]], writes=[Vt.reg])
                else:
                    P.op("dve", lambda e, bk=bk, dstv=dstv, nb=nb: e.tensor_copy(out=dstv, in_=bank_f[bk][:, 0:nb * 128]),
                         reads=[bank_reg[bk]], writes=[Vt.reg])

            if upto < 2.15:
                return
            vgi = 0
            QSK = 2
            for t in range(16 + QSK):
                if t < len(swn):
                    swn[t]()
                if j >= 1 and t in (3, 6, 9, 12):
                    finalize_chunk(j - 1, (t - 3) // 3)
                if t < 16:
                    qk_front(t)
                if vgi < len(vgroups):
                    v_group(vgi)
                    vgi += 1
                if t >= QSK:
                    qk_back(t - QSK)
            while vgi < len(vgroups):
                v_group(vgi)
                vgi += 1
            if upto < 2.25:
                return

            def evac_OD(g, pq, bo, bd):
                for (bkx, acc) in ((bo, accU), (bd, accD)):
                    if g == 0:
                        dst = acc.f[:, pq * 512:(pq + 1) * 512]
                        P.op("act", lambda e, bkx=bkx, dst=dst: e.copy(out=dst, in_=bank_f[bkx]),
                             reads=[bank_reg[bkx]], writes=[acc.reg])
                    elif g == 1:
                        dst = acc.f.rearrange("p (i r) -> p r i", r=4)[:, pq, :]
                        P.op("dve", lambda e, bkx=bkx, dst=dst: e.tensor_tensor(out=dst, in0=bank_f[bkx], in1=dst,
                                                                                 op=ALU.add),
                             reads=[bank_reg[bkx], acc.reg], writes=[acc.reg])
                    else:
                        dst = acc.f.rearrange("p (i r) -> p r i", r=16)[:, 4 * pq:4 * pq + 4, :]
                        src = bank_f[bkx].rearrange("p (r i) -> p r i", r=4)
                        P.op("dve", lambda e, src=src, dst=dst: e.tensor_tensor(out=dst, in0=src, in1=dst, op=ALU.add),
                             reads=[bank_reg[bkx], acc.reg], writes=[acc.reg])

            def att_front(T):
                bs = next_bank((0, 1, 2, 3))
                pt = PT[pt_ctr[0] % NPT]
                pt_ctr[0] += 1
                T["pt"] = pt
                nq = T["nq"]
                P.op("pe", lambda e, bs=bs, T=T, nq=nq: e.matmul(bank_f[bs][:, 0:nq], kTv[:, T["g"], T["ksl"]],
                                                               qTv[:, T["g"], T["qsl"]], start=True, stop=True),
                     reads=[kT.reg, qT.reg], writes=[bank_reg[bs]])
                P.op("act", lambda e, bs=bs, pt=pt, nq=nq: e.activation(out=pt.b[:, 0:nq], in_=bank_f[bs][:, 0:nq],
                                                                       func=AF.Exp, scale=SCALE),
                     reads=[bank_reg[bs]], writes=[pt.reg])
                P.op("pool", lambda e, pt=pt, nq=nq, mk=T["mk"]: e.tensor_tensor(out=pt.b[:, 0:nq], in0=pt.b[:, 0:nq],
                                                                                in1=mk, op=ALU.mult),
                     reads=[pt.reg, masks.reg], writes=[pt.reg])

            ost = {"bo": None, "bd": None}

            def att_back(T):
                g, r, b, nqb, L, vbi, pt = T["g"], T["r"], T["b"], T["nqb"], T["L"], T["vbi"], T["pt"]
                if g == 2:
                    if r % 4 == 0:
                        ost["bo"] = next_pair((4, 6))
                        ost["bd"] = ost["bo"] + 1
                    bo, bd = ost["bo"], ost["bd"]
                    cs = slice((r % 4) * 128, (r % 4) * 128 + 128)

                    def _pv(e, bo=bo, bd=bd, cs=cs, vbi=vbi, pt=pt):
                        e.matmul(bank_f[bo][:, cs], Vtv[:, vbi, :], pt.b[:, 0:128], start=True, stop=True)
                        return e.matmul(bank_f[bd][:, cs], onesB.b, pt.b[:, 0:128], start=True, stop=True)
                    P.op("pe", _pv, reads=[Vt.reg, pt.reg, onesB.reg], writes=[bank_reg[bo], bank_reg[bd]])
                    if r % 4 == 3:
                        evac_OD(2, r // 4, bo, bd)
                    return
                col = 0
                if b >= 1:
                    qb = b - 1
                    bo, bd = ost["bo"], ost["bd"]
                    cs = slice((qb % 4) * 128, (qb % 4) * 128 + 128)

                    def _fin(e, bo=bo, bd=bd, cs=cs, vbi=vbi, pt=pt):
                        e.matmul(bank_f[bo][:, cs], Vtv[:, vbi, :], pt.b[:, 0:128], start=False, stop=True)
                        return e.matmul(bank_f[bd][:, cs], onesB.b, pt.b[:, 0:128], start=False, stop=True)
                    P.op("pe", _fin, reads=[Vt.reg, pt.reg, onesB.reg], writes=[bank_reg[bo], bank_reg[bd]])
                    col = 128
                    if qb % 4 == 3 or qb == nqb - 1:
                        evac_OD(g, (r * L + 128 * qb) // 512, bo, bd)
                if b <= nqb - 1:
                    qb = b
                    if qb % 4 == 0:
                        ost["bo"] = next_pair((4, 6))
                        ost["bd"] = ost["bo"] + 1
                    bo, bd = ost["bo"], ost["bd"]
                    cs = slice((qb % 4) * 128, (qb % 4) * 128 + 128)

                    def _sta(e, bo=bo, bd=bd, cs=cs, vbi=vbi, pt=pt, col=col):
                        e.matmul(bank_f[bo][:, cs], Vtv[:, vbi, :], pt.b[:, col:col + 128], start=True, stop=False)
                        return e.matmul(bank_f[bd][:, cs], onesB.b, pt.b[:, col:col + 128], start=True, stop=False)
                    P.op("pe", _sta, reads=[Vt.reg, pt.reg, onesB.reg], writes=[bank_reg[bo], bank_reg[bd]])

            ntk = len(tasks) if upto >= 2.5 else (18 if upto < 2.35 else 38)
            tl = [dict(T) for T in tasks[:ntk]]
            for i in range(ntk + LA):
                if i < ntk:
                    att_front(tl[i])
                if i >= LA:
                    att_back(tl[i - LA])
            if upto < 2.65:
                return
            if j == 3:
                for c4 in range(4):
                    finalize_chunk(3, c4)
        A.release(m3)
        if debug and "dbg_ybT" in dbg and s == 0:
            stg = Buf("dbgstg3", 4 * S)
            P.op("dve", lambda e, stg=stg: e.tensor_copy(out=stg.f, in_=ybT.b), reads=[ybT.reg], writes=[stg.reg])
            P.dma("sp", dbg["dbg_ybT"], stg.f, reads=[stg.reg], writes=[], semreg=stg.reg)
            out_regs.append(stg.reg)
            A.release(m3)

        if upto < 4:
            return
        A.release(mQ)
        m4 = A.mark()
        mT = Buf("mT", KC * S // 2)
        mTv = mT.b.rearrange("p (k t) -> p k t", k=KC)
        Wo = Buf("Wo", KC * D // 2)
        Wov = Wo.b.rearrange("p (k n) -> p k n", k=KC)
        Wg = [Buf("Wg%d" % i, KC * 256 // 2) for i in range(2)]
        Wab = [Buf("Wab%d" % i, 2 * 4 * 128 // 2) for i in range(2)]
        def load_c(c):
            b = c % 2
            src = w_in[:, 5632:7680].rearrange("(k p) (m n) -> p k m n", p=128, n=1024)
            dstw = Wg[b].b.rearrange("p (k m n) -> p k m n", k=KC, m=2)
            for m_ in range(2):
                wload(dstw[:, :, m_, :], src[:, :, m_, c * 128:(c + 1) * 128], Wg[b].reg)
            wab = Wab[b].b.rearrange("p (w k n) -> p w k n", w=2, k=4)
            wload(wab[:, 0, :, :], w_a[:, c * 128:(c + 1) * 128].rearrange("(k p) n -> p k n", p=128), Wab[b].reg)
            wload(wab[:, 1, :, :], w_b[:, c * 128:(c + 1) * 128].rearrange("(k p) n -> p k n", p=128), Wab[b].reg)

        load_c(0)
        ta = [Buf("ta%d" % i, 512) for i in range(2)]
        tb = [Buf("tb%d" % i, 512) for i in range(2)]
        m1b = [Buf("m1b%d" % i, 512) for i in range(2)]
        m2b = [Buf("m2b%d" % i, 512) for i in range(2)]

        it = 0
        for c in range(8):
            if c + 1 < 8:
                load_c(c + 1)
            if c == 1:
                wload(Wov[:, 0:4, :], w_out[0:512, :].rearrange("(k p) n -> p k n", p=128), Wo.reg)
            if c == 2:
                wload(Wov[:, 4:8, :], w_out[512:1024, :].rearrange("(k p) n -> p k n", p=128), Wo.reg)
            wg = Wg[c % 2]
            wgv = wg.b.rearrange("p (k m n) -> p k m n", k=KC, m=2)
            wab = Wab[c % 2]
            wabv = wab.b.rearrange("p (w k n) -> p w k n", w=2, k=4)
            for q in range(4):
                i2 = it % 2
                it += 1
                tsl = slice(PAD + q * 512, PAD + (q + 1) * 512)
                qsl = slice(q * 512, (q + 1) * 512)
                bga, bgb, bA, bB = next_bank(), next_bank(), next_bank(), next_bank()

                def _mm(e, bga=bga, bgb=bgb, bA=bA, bB=bB, wgv=wgv, wabv=wabv, tsl=tsl, qsl=qsl):
                    ins = None
                    for kc in range(KC):
                        e.matmul(bank_f[bga], wgv[:, kc, 0, :], nTv[:, kc, tsl], start=(kc == 0), stop=(kc == KC - 1))
                    for kc in range(KC):
                        e.matmul(bank_f[bgb], wgv[:, kc, 1, :], nTv[:, kc, tsl], start=(kc == 0), stop=(kc == KC - 1))
                    for kc in range(4):
                        e.matmul(bank_f[bA], wabv[:, 0, kc, :], yaTv[:, kc, qsl], start=(kc == 0), stop=(kc == 3))
                    for kc in range(4):
                        ins = e.matmul(bank_f[bB], wabv[:, 1, kc, :], ybTv[:, kc, qsl], start=(kc == 0), stop=(kc == 3))
                    return ins
                P.op("pe", _mm, reads=[wg.reg, wab.reg, nT.reg, yaT.reg, ybT.reg],
                     writes=[bank_reg[bga], bank_reg[bgb], bank_reg[bA], bank_reg[bB]])
                P.op("act", lambda e, bga=bga, i2=i2, c=c: e.activation(out=ta[i2].f, in_=bank_f[bga], func=AF.Tanh,
                                                                       bias=hbg.f[:, c:c + 1], scale=0.5),
                     reads=[bank_reg[bga], hbg.reg], writes=[ta[i2].reg])
                P.op("act", lambda e, bgb=bgb, i2=i2, c=c: e.activation(out=tb[i2].f, in_=bank_f[bgb], func=AF.Tanh,
                                                                       bias=hbg.f[:, 8 + c:9 + c], scale=0.5),
                     reads=[bank_reg[bgb], hbg.reg], writes=[tb[i2].reg])
                P.op("dve", lambda e, bA=bA, i2=i2: e.scalar_tensor_tensor(out=m1b[i2].f, in0=ta[i2].f, scalar=1.0,
                                                                          in1=bank_f[bA], op0=ALU.add, op1=ALU.mult),
                     reads=[ta[i2].reg, bank_reg[bA]], writes=[m1b[i2].reg])
                P.op("dve", lambda e, bB=bB, i2=i2: e.scalar_tensor_tensor(out=m2b[i2].f, in0=tb[i2].f, scalar=1.0,
                                                                          in1=bank_f[bB], op0=ALU.add, op1=ALU.mult),
                     reads=[tb[i2].reg, bank_reg[bB]], writes=[m2b[i2].reg])
                P.op("pool", lambda e, i2=i2, c=c, qsl=qsl: e.tensor_tensor(out=mTv[:, c, qsl], in0=m1b[i2].f,
                                                                           in1=m2b[i2].f, op=ALU.add),
                     reads=[m1b[i2].reg, m2b[i2].reg], writes=[mT.reg])
        if s == nseq - 1 and 2 in phases:
            top_save = A.top
            A.top = mP1
            p2w["Wup"] = Buf("Wup", KC * 4096 // 2)
            p2w["wup_r"] = [A.reg("Wup_c%d" % i, p2w["Wup"].st, p2w["Wup"].n) for i in range(4)]
            wupv_ = p2w["Wup"].b.rearrange("p (k n) -> p k n", k=KC)
            A.top = top_save
        xt = [Buf("xt%d" % i, D) for i in range(2)]
        h1 = [Buf("h1_%d" % i, D) for i in range(2)]
        junk2 = Buf("junk2", D // 2)
        ss2 = [Buf("ss2_%d" % i, 8) for i in range(2)]
        for t in range(16):
            i2 = t % 2
            P.dma("sp", xt[i2].f, x[s, t * 128:(t + 1) * 128, :], writes=[xt[i2].reg], semreg=xt[i2].reg)
            if s + 1 < nseq:
                if t == 0:
                    stage1_dma(s + 1, 0)
                    stage1_dma(s + 1, 1)
                if t in (9, 13):
                    stage1_dma(s + 1, 2 + (t - 9) // 4)
                if t % 4 == 3:
                    stage1_front(s + 1, t // 4)
                if t >= 6 and t % 4 == 2:
                    stage1_back(s + 1, (t - 6) // 4)
            if "Wup" in p2w and s == nseq - 1 and t % 4 == 1:
                i_ = t // 4
                wload(p2w["Wup"].b.rearrange("p (k n) -> p k n", k=KC)[:, :, i_ * 1024:(i_ + 1) * 1024],
                      w_up[:, i_ * 1024:(i_ + 1) * 1024].rearrange("(k p) n -> p k n", p=128), p2w["wup_r"][i_])
            pr = next_pair((0, 2))

            def _mm(e, pr=pr, t=t):
                ins = None
                for h in range(2):
                    for kc in range(KC):
                        ins = e.matmul(bank_f[pr + h], mTv[:, kc, t * 128:(t + 1) * 128], Wov[:, kc, h * 512:(h + 1) * 512],
                                       start=(kc == 0), stop=(kc == KC - 1))
                return ins
            P.op("pe", _mm, reads=[mT.reg, Wo.reg], writes=[bank_reg[pr], bank_reg[pr + 1]])
            yv = psum[pr // 2]
            breg = [bank_reg[pr], bank_reg[pr + 1]]
            P.op("act", lambda e, yv=yv, i2=i2: e.activation(out=junk2.b,
                                                            in_=yv[:, :], func=AF.Square, scale=0.5,
                                                            accum_out=ss2[i2].f[:, 0:1]),
                 reads=breg, writes=[junk2.reg, ss2[i2].reg])
            rstd_op(ss2[i2].f[:, 0:1], ss2[i2].f[:, 0:1], 1, 1.0 / D, [ss2[i2].reg], [ss2[i2].reg], post=0.5, mode="sqrt")
            P.op("dve", lambda e, yv=yv, i2=i2: e.scalar_tensor_tensor(out=h1[i2].f, in0=yv[:, :], scalar=ss2[i2].f[:, 0:1],
                                                                      in1=gpost.f, op0=ALU.mult, op1=ALU.mult),
                 reads=breg + [ss2[i2].reg, gpost.reg], writes=[h1[i2].reg])
            P.op("dve" if s == nseq - 1 else "pool",
                 lambda e, i2=i2: e.tensor_tensor(out=h1[i2].f, in0=h1[i2].f, in1=xt[i2].f, op=ALU.add),
                 reads=[h1[i2].reg, xt[i2].reg], writes=[h1[i2].reg])
            P.dma("sp", out[s, t * 128:(t + 1) * 128, :], h1[i2].f, reads=[h1[i2].reg], writes=[h1dram[s][t]],
                  semreg=h1[i2].reg)
            out_regs.append(h1[i2].reg)
        if s + 1 < nseq:
            stage1_back(s + 1, 3)
        A.release(mS)

    p2w = {}
    h1dram = [[P.reg("h1d_%d_%d" % (s, t)) for t in range(16)] for s in range(nseq)]

    if 1 in phases:
        for s in range(nseq):
            phase1(s)

    def phase2():
        A.release(mP1)
        if "Wup" in p2w:
            Wup = p2w["Wup"]
            wup_r = p2w["wup_r"]
            A.top = Wup.st + Wup.n
            Wupv = Wup.b.rearrange("p (k n) -> p k n", k=KC)
        else:
            Wup = Buf("Wup", KC * 4096 // 2)
            Wupv = Wup.b.rearrange("p (k n) -> p k n", k=KC)
            wup_r = [A.reg("Wup_c%d" % i, Wup.st, Wup.n) for i in range(4)]
            for i in range(4):
                wload(Wupv[:, :, i * 1024:(i + 1) * 1024],
                      w_up[:, i * 1024:(i + 1) * 1024].rearrange("(k p) n -> p k n", p=128), wup_r[i])
        Wdn = Buf("Wdn", 32 * D // 2)
        Wdnv = Wdn.b.rearrange("p (k n) -> p k n", k=32)
        wdn_r = [A.reg("Wdn_c%d" % i, Wdn.st, Wdn.n) for i in range(4)]
        hin = [Buf("hin%d" % i, D) for i in range(2)]
        hres = [Buf("hres%d" % i, D) for i in range(2)]
        ot = [Buf("ot%d" % i, D) for i in range(2)]
        n2T = Buf("n2T", KC * 512 // 2)
        n2Tv = n2T.b.rearrange("p (k t) -> p k t", k=KC)
        hidT = Buf("hidT", 32 * 512 // 2)
        hidTv = hidT.b.rearrange("p (k t) -> p k t", k=32)
        rl = [Buf("rl%d" % i, 256) for i in range(2)]
        junk3 = Buf("junk3", D // 2)
        ss3 = [Buf("ss3_%d" % i, 8) for i in range(2)]
        ss4 = [Buf("ss4_%d" % i, 8) for i in range(2)]
        quads = [(s, q) for s in range(nseq) for q in range(4)]

        def pro_front(s, q, tt):
            hb = hin[tt % 2]
            sb = ss3[tt % 2]
            t = 4 * q + tt
            P.dma("sp", hb.f, out[s, t * 128:(t + 1) * 128, :], reads=[h1dram[s][t]], writes=[hb.reg], semreg=hb.reg)
            P.op("act", lambda e, hb=hb, sb=sb: e.activation(out=junk3.b, in_=hb.f, func=AF.Square,
                                                            accum_out=sb.f[:, 0:1]),
                 reads=[hb.reg], writes=[junk3.reg, sb.reg])
            rstd_op(sb.f[:, 0:1], sb.f[:, 0:1], 1, 1.0 / D, [sb.reg], [sb.reg], mode="sqrt")
            P.op("dve", lambda e, hb=hb, sb=sb: e.tensor_scalar(out=hb.f, in0=hb.f, scalar1=sb.f[:, 0:1], scalar2=None,
                                                               op0=ALU.mult),
                 reads=[hb.reg, sb.reg], writes=[hb.reg])

        def pro_back(s, q, tt):
            hb = hin[tt % 2]
            for half in range(2):
                bk = next_bank((6, 7))

                def _tr(e, bk=bk, hb=hb, half=half):
                    ins = None
                    for k4 in range(4):
                        kc = half * 4 + k4
                        ins = e.transpose(bank_f[bk][:, k4 * 128:(k4 + 1) * 128], hb.f[:, kc * 128:(kc + 1) * 128],
                                          identF.f)
                    return ins
                P.op("pe", _tr, reads=[hb.reg, identF.reg], writes=[bank_reg[bk]])
                for k4 in range(4):
                    kc = half * 4 + k4
                    dst = n2Tv[:, kc, tt * 128:(tt + 1) * 128]
                    if k4 % 2 == 0:
                        P.op("act", lambda e, bk=bk, k4=k4, kc=kc, dst=dst: e.activation(
                            out=dst, in_=bank_f[bk][:, k4 * 128:(k4 + 1) * 128], func=AF.Identity,
                            scale=gpre2.f[:, kc:kc + 1]),
                            reads=[bank_reg[bk], gpre2.reg], writes=[n2T.reg])
                    else:
                        P.op("dve", lambda e, bk=bk, k4=k4, kc=kc, dst=dst: e.tensor_scalar(
                            out=dst, in0=bank_f[bk][:, k4 * 128:(k4 + 1) * 128], scalar1=gpre2.f[:, kc:kc + 1],
                            scalar2=None, op0=ALU.mult),
                            reads=[bank_reg[bk], gpre2.reg], writes=[n2T.reg])

        def up(s, q, nxt=None):
            for fc in range(32):
                if nxt is not None and fc in (10, 22):
                    pro_front(nxt[0], nxt[1], 0 if fc == 10 else 1)
                bk = next_bank((0, 1, 2, 3))
                rb = rl[fc % 2]

                def _mm(e, bk=bk, fc=fc):
                    ins = None
                    for kc in range(KC):
                        ins = e.matmul(bank_f[bk], Wupv[:, kc, fc * 128:(fc + 1) * 128], n2Tv[:, kc, :],
                                       start=(kc == 0), stop=(kc == KC - 1))
                    return ins
                P.op("pe", _mm, reads=[wup_r[fc // 8], n2T.reg], writes=[bank_reg[bk]])
                P.op("act", lambda e, bk=bk, rb=rb: e.activation(out=rb.b, in_=bank_f[bk], func=AF.Relu),
                     reads=[bank_reg[bk]], writes=[rb.reg])
                P.op("dve", lambda e, bk=bk, rb=rb, fc=fc: e.tensor_tensor(out=hidTv[:, fc, :], in0=bank_f[bk], in1=rb.b,
                                                                          op=ALU.mult),
                     reads=[bank_reg[bk], rb.reg], writes=[hidT.reg])

        def down(s, q, mid=None):
            for tt in range(4):
                if tt == 2 and mid is not None:
                    mid()
                i2 = tt % 2
                t = 4 * q + tt
                pr = next_pair((4, 6))
                P.dma("sp", hres[i2].f, out[s, t * 128:(t + 1) * 128, :], reads=[h1dram[s][t]], writes=[hres[i2].reg],
                      semreg=hres[i2].reg)

                def _mm(e, pr=pr, tt=tt):
                    ins = None
                    for h in range(2):
                        for fc in range(32):
                            ins = e.matmul(bank_f[pr + h], hidTv[:, fc, tt * 128:(tt + 1) * 128],
                                           Wdnv[:, fc, h * 512:(h + 1) * 512], start=(fc == 0), stop=(fc == 31))
                    return ins
                P.op("pe", _mm, reads=[hidT.reg] + wdn_r, writes=[bank_reg[pr], bank_reg[pr + 1]])
                yv = psum[pr // 2]
                breg = [bank_reg[pr], bank_reg[pr + 1]]
                P.op("act", lambda e, yv=yv, i2=i2: e.activation(out=junk3.b, in_=yv[:, :], func=AF.Square,
                                                                accum_out=ss4[i2].f[:, 0:1]),
                     reads=breg, writes=[junk3.reg, ss4[i2].reg])
                rstd_op(ss4[i2].f[:, 0:1], ss4[i2].f[:, 0:1], 1, 1.0 / D, [ss4[i2].reg], [ss4[i2].reg], mode="sqrt")
                P.op("dve", lambda e, yv=yv, i2=i2: e.scalar_tensor_tensor(out=ot[i2].f, in0=yv[:, :],
                                                                          scalar=ss4[i2].f[:, 0:1], in1=gpost2.f,
                                                                          op0=ALU.mult, op1=ALU.mult),
                     reads=breg + [ss4[i2].reg, gpost2.reg], writes=[ot[i2].reg])
                P.op("pool", lambda e, i2=i2: e.tensor_tensor(out=ot[i2].f, in0=ot[i2].f, in1=hres[i2].f, op=ALU.add),
                     reads=[ot[i2].reg, hres[i2].reg], writes=[ot[i2].reg])
                P.dma("sp", out[s, t * 128:(t + 1) * 128, :], ot[i2].f, reads=[ot[i2].reg, hres[i2].reg],
                      writes=[h1dram[s][t]], semreg=ot[i2].reg)
                out_regs.append(ot[i2].reg)

        for tt in range(4):
            pro_front(quads[0][0], quads[0][1], tt) if tt < 2 else None
        pro_back(quads[0][0], quads[0][1], 0)
        pro_back(quads[0][0], quads[0][1], 1)
        for tt in (2, 3):
            pro_front(quads[0][0], quads[0][1], tt)
        for tt in (2, 3):
            pro_back(quads[0][0], quads[0][1], tt)
        for i in range(4):
            wload(Wdnv[:, i * 8:(i + 1) * 8, :],
                  w_down[i * 1024:(i + 1) * 1024, :].rearrange("(k p) n -> p k n", p=128), wdn_r[i])
        for i, (s, q) in enumerate(quads):
            nxt = quads[i + 1] if i + 1 < len(quads) else None
            up(s, q, nxt)
            mid = None
            if nxt is not None:
                pro_back(nxt[0], nxt[1], 0)
                pro_back(nxt[0], nxt[1], 1)
                pro_front(nxt[0], nxt[1], 2)
                pro_front(nxt[0], nxt[1], 3)

                def mid(nxt=nxt):
                    pro_back(nxt[0], nxt[1], 2)
                    pro_back(nxt[0], nxt[1], 3)
            down(s, q, mid)

    if 2 in phases:
        phase2()

    seen = []
    for r in out_regs:
        if r not in seen:
            seen.append(r)
    P.final_wait("sp", seen)

    P.resolve()
    with stack:
        with nc.Block() as block:
            @block.tensor
            def _(e):
                P.emit_engine("pe", e)

            @block.scalar
            def _(e):
                P.emit_engine("act", e)

            @block.vector
            def _(e):
                P.emit_engine("dve", e)

            @block.gpsimd
            def _(e):
                P.emit_engine("pool", e)

            @block.sync
            def _(e):
                P.emit_engine("sp", e)
    return nc, A.peak


def _consts():
    c = {}
    c["c_ident"] = np.eye(128, dtype=np.float32)
    p = np.arange(128)[:, None]
    q = np.arange(128)[None, :]
    mA = (q >= p)
    mB = (q <= p)
    mA2 = mA & (p < 64)
    mB2 = mB & (p >= 64)
    mC = np.abs(q - p) <= 64
    c["c_masks"] = np.ascontiguousarray(np.concatenate([mA, mB, mA2, mB2, mC], axis=1).astype(np.float32))
    inv_freq = 500000.0 ** (-np.arange(0, 32, 2, dtype=np.float64) / 32.0)
    pos = (np.arange(16)[None, :, None] * 128 + np.arange(128)[:, None, None]).astype(np.float64)
    ang = pos * inv_freq[None, None, :]
    c["c_cos"] = np.ascontiguousarray(np.cos(ang).astype(np.float32).reshape(128, 256))
    c["c_sin"] = np.ascontiguousarray(np.sin(ang).astype(np.float32).reshape(128, 256))
    return c


def _prep_inputs(inputs):
    f = lambda a: np.ascontiguousarray(np.asarray(a, dtype=np.float32))
    shared = {
        "w_in": f(inputs["w_in"][0]),
        "w_spatial": f(inputs["w_spatial"][0]),
        "w_branch_a": f(inputs["w_branch_a"][0]),
        "w_branch_b": f(inputs["w_branch_b"][0]),
        "w_out": f(inputs["w_out"][0]),
        "w_up": f(inputs["w_up"][0]),
        "w_down": f(inputs["w_down"][0]),
        "c_gpre": f(np.asarray(inputs["norm_mix_pre"][0]).reshape(8, 128).T),
        "c_gpre2": f(np.asarray(inputs["norm_mlp_pre"][0]).reshape(8, 128).T),
        "c_lng": f(np.asarray(inputs["ln_v_gain"][0]).reshape(4, 128).T),
        "c_bgate": f(np.asarray(inputs["b_gate"][0]).reshape(16, 128).T),
        "c_lnb": f(np.asarray(inputs["ln_v_bias"][0]).reshape(1, 512)),
        "c_bsp": f(np.asarray(inputs["b_spatial"][0]).reshape(1, 512)),
        "c_gpost": f(np.broadcast_to(np.asarray(inputs["norm_mix_post"][0])[None, :], (128, D))),
        "c_gpost2": f(np.broadcast_to(np.asarray(inputs["norm_mlp_post"][0])[None, :], (128, D))),
    }
    shared.update(_consts())
    return shared


_CACHE = {}


def kernel(**inputs):
    x = np.asarray(inputs["x"], dtype=np.float32)
    shared = _prep_inputs(inputs)
    if "nc" not in _CACHE:
        _CACHE["nc"] = build(SEQ_PER_CORE)[0]
    nc = _CACHE["nc"]
    in_maps = []
    for c in range(NCORES):
        m = dict(shared)
        m["x"] = np.ascontiguousarray(x[c * SEQ_PER_CORE:(c + 1) * SEQ_PER_CORE])
        in_maps.append(m)
    res = run_bass_kernel_spmd(nc, in_maps, core_ids=list(range(NCORES)))
    outs = [np.asarray(r["out"], dtype=np.float32) for r in res.results]
    return np.concatenate(outs, axis=0)
```

```python
import math
import numpy as np
import concourse.bass as bass
import concourse.mybir as mybir
from concourse.bass_utils import run_bass_kernel_spmd

F32 = mybir.dt.float32
BF16 = mybir.dt.bfloat16
AF = mybir.ActivationFunctionType
ALU = mybir.AluOpType

S = 2048
D = 1024
KC = 8
PAD = 256
SP = S + 2 * PAD
EPS = 1e-6
NCORES = 8
SEQ_PER_CORE = 2
SCALE = 1.0 / math.sqrt(128.0)
SAME_ENGINE_SYNC = True
import os as _os
SKIP = set(_os.environ.get('KSKIP', '').split(','))
POOL_DMA_MAX_OUTSTANDING = 2


class Reg:
    __slots__ = ("name", "w", "r", "dsem", "dcnt", "alias", "excl")

    def __init__(self, name):
        self.name = name
        self.excl = False
        self.w = None
        self.r = []
        self.dsem = None
        self.dcnt = 0
        self.alias = []


class Op:
    __slots__ = ("eng", "fn", "waits", "inc", "count", "dreg", "dval")

    def __init__(self, eng, fn):
        self.eng = eng
        self.fn = fn
        self.waits = []
        self.inc = False
        self.count = None
        self.dreg = None
        self.dval = None


class Prog:
    ENGS = ("pe", "act", "dve", "pool", "sp")

    def __init__(self, nc, stack):
        self.nc = nc
        self.stack = stack
        self.ops = {e: [] for e in self.ENGS}
        self.sems = {e: stack.enter_context(nc.semaphore("s_" + e)) for e in ("pe", "act", "dve", "pool")}
        self.nreg = 0

    def reg(self, name, alias=()):
        r = Reg(name)
        r.alias = list(alias)
        return r

    def _dep(self, o, tok):
        if tok is None:
            return
        if tok[0] == "op":
            p = tok[1]
            if p.eng == o.eng and (p.eng in ("pe", "sp") or not SAME_ENGINE_SYNC):
                return
            p.inc = True
        o.waits.append(tok)

    def _deps(self, o, reads, writes):
        for r in reads:
            for a in r.alias:
                self._dep(o, a.w)
            self._dep(o, r.w)
            if r.excl:
                for t in r.r:
                    if t[0] == "op" and t[1].eng != o.eng:
                        self._dep(o, t)
        for w in writes:
            for a in w.alias:
                self._dep(o, a.w)
                for t in a.r:
                    self._dep(o, t)
            w.alias = []
            self._dep(o, w.w)
            for t in w.r:
                self._dep(o, t)

    def _mark(self, tok, reads, writes):
        for r in reads:
            key = tok[1].eng if tok[0] == "op" else ("d", id(tok[1]))
            r.r = [t for t in r.r if (t[1].eng if t[0] == "op" else ("d", id(t[1]))) != key]
            r.r.append(tok)
        for w in writes:
            w.w = tok
            w.r = []

    def op(self, eng, fn, reads=(), writes=()):
        o = Op(eng, fn)
        self._deps(o, reads, writes)
        self.ops[eng].append(o)
        self._mark(("op", o), reads, writes)
        return o

    def dma(self, eng, out, in_, reads=(), writes=(), semreg=None):
        if semreg.dsem is None:
            self.nreg += 1
            semreg.dsem = self.stack.enter_context(self.nc.semaphore("d%d_%s" % (self.nreg, semreg.name)))
        o = Op(eng, lambda e: e.dma_start(out=out, in_=in_))
        self._deps(o, reads, writes)
        if eng == "pool" and POOL_DMA_MAX_OUTSTANDING:
            hist = self.__dict__.setdefault("pool_dma_hist", [])
            if len(hist) >= POOL_DMA_MAX_OUTSTANDING:
                self._dep(o, hist[-POOL_DMA_MAX_OUTSTANDING])
            hist.append(("dma", semreg, 16 * (semreg.dcnt + 1)))
        semreg.dcnt += 1
        o.dreg = semreg
        o.dval = 16 * semreg.dcnt
        self.ops[eng].append(o)
        self._mark(("dma", semreg, o.dval), reads, writes)
        return o

    def resolve(self):
        for e in self.ENGS:
            c = 0
            for o in self.ops[e]:
                if o.inc:
                    c += 1
                    o.count = c
            self.maxcount = getattr(self, "maxcount", {})
            self.maxcount[e] = c

    def emit_engine(self, e, h):
        waited = {}
        for o in self.ops[e]:
            for tok in o.waits:
                if tok[0] == "op":
                    sem, val = self.sems[tok[1].eng], tok[1].count
                else:
                    sem, val = tok[1].dsem, tok[2]
                k = id(sem)
                if waited.get(k, 0) >= val:
                    continue
                waited[k] = val
                h.wait_ge(sem, val)
            if o.fn is None:
                continue
            ins = o.fn(h)
            if o.dreg is not None:
                ins.then_inc(o.dreg.dsem, 16)
            elif o.inc:
                ins.then_inc(self.sems[e], 1)

    def final_wait(self, eng, regs):
        o = Op(eng, None)
        for r in regs:
            self._dep(o, r.w)
            for t in r.r:
                self._dep(o, t)
        self.ops[eng].append(o)
        return o


def build(nseq=SEQ_PER_CORE, debug=None, phases=(1, 2), upto=99):
    from contextlib import ExitStack

    nc = bass.Bass("TRN2", target_bir_lowering=False)
    stack = ExitStack()

    def din(name, shape, dt=F32):
        return nc.dram_tensor(name, list(shape), dt, kind="ExternalInput").ap()

    x = din("x", [nseq, S, D])
    w_in = din("w_in", [D, 7680])
    w_sp = din("w_spatial", [4, 128, 128])
    w_a = din("w_branch_a", [512, D])
    w_b = din("w_branch_b", [512, D])
    w_out = din("w_out", [D, D])
    w_up = din("w_up", [D, 4096])
    w_down = din("w_down", [4096, D])
    c_ident = din("c_ident", [128, 128])
    c_masks = din("c_masks", [128, 640])
    c_cos = din("c_cos", [128, 256])
    c_sin = din("c_sin", [128, 256])
    c_gpre = din("c_gpre", [128, 8])
    c_gpre2 = din("c_gpre2", [128, 8])
    c_lng = din("c_lng", [128, 4])
    c_bgate = din("c_bgate", [128, 16])
    c_lnb = din("c_lnb", [1, 512])
    c_bsp = din("c_bsp", [1, 512])
    c_gpost = din("c_gpost", [128, D])
    c_gpost2 = din("c_gpost2", [128, D])
    out = nc.dram_tensor("out", [nseq, S, D], F32, kind="ExternalOutput").ap()
    dbg = {}
    if debug:
        for name, shape in debug.items():
            dbg[name] = nc.dram_tensor(name, list(shape), F32, kind="ExternalOutput").ap()

    ARENA_F = 53200
    arena = stack.enter_context(nc.sbuf_tensor("arena", [128, ARENA_F], F32))
    psum = [stack.enter_context(nc.psum_tensor("ps%d" % i, [128, 1024], F32)) for i in range(4)]
    P = Prog(nc, stack)

    class Arena:
        def __init__(self):
            self.top = 0
            self.hist = []
            self.peak = 0

        def alloc(self, name, nwords):
            nwords = (nwords + 7) // 8 * 8
            st = self.top
            self.top += nwords
            assert self.top <= ARENA_F, "SBUF arena overflow at %s: %d" % (name, self.top)
            self.peak = max(self.peak, self.top)
            return st

        def reg(self, name, st, nwords):
            en = st + nwords
            al = [r for (a, b, r) in self.hist if a < en and st < b]
            r = P.reg(name, al)
            self.hist.append((st, en, r))
            return r

        def mark(self):
            return self.top

        def release(self, m):
            self.top = m

    A = Arena()

    class Buf:
        def __init__(self, name, nwords, nregs=1):
            self.st = A.alloc(name, nwords)
            self.n = nwords
            self.f = arena[:, self.st:self.st + nwords]
            self.b = arena[:, self.st:self.st + nwords].bitcast(BF16)
            self.reg = A.reg(name, self.st, nwords)

    def dump(name, src_ap, reg, n, isbf=False):
        if not (debug and name in dbg):
            return
        if isbf:
            stg = Buf("stg_" + name, n)
            P.op("dve", lambda e, stg=stg: e.tensor_copy(out=stg.f, in_=src_ap), reads=[reg], writes=[stg.reg])
            P.dma("sp", dbg[name], stg.f, reads=[stg.reg], writes=[], semreg=stg.reg)
            out_regs.append(stg.reg)
        else:
            P.dma("sp", dbg[name], src_ap, reads=[reg], writes=[], semreg=reg)
            out_regs.append(reg)

    out_regs = []
    bank_f = []
    bank_reg = []
    for i in range(8):
        bank_f.append(psum[i // 2][:, (i % 2) * 512:(i % 2) * 512 + 512])
        bank_reg.append(P.reg("bank%d" % i))
        bank_reg[-1].excl = True
    bank_ctr = {}
    ALLB = (0, 1, 2, 3, 4, 5, 6, 7)
    bank_pool = [ALLB]

    def next_bank(pool=None):
        pool = pool or bank_pool[0]
        c = bank_ctr.get(pool, 0)
        bank_ctr[pool] = c + 1
        return pool[c % len(pool)]

    def next_pair(pool=None):
        pool = pool or (0, 2, 4, 6)
        key = ("pair",) + pool
        c = bank_ctr.get(key, 0)
        bank_ctr[key] = c + 1
        return pool[c % len(pool)]

    identF = Buf("identF", 128)
    gpre = Buf("gpre", 8)
    gpre2 = Buf("gpre2", 8)
    lng = Buf("lng", 8)
    hbg = Buf("hbg", 16)
    gpost2 = Buf("gpost2", 1024)
    neghalf = Buf("neghalf", 8)
    epsT = Buf("epsT", 8)
    gpost = Buf("gpost", 1024)
    mP1 = A.mark()
    identB = Buf("identB", 64)
    masks = Buf("masks", 320)
    onesB = Buf("onesB", 64)
    cosT = Buf("cosT", 256)
    sinT = Buf("sinT", 256)
    Cg = Buf("Cg", 512)
    WsT = Buf("WsT", 256)
    constR = P.reg("constdma")

    def sq(eng):
        return eng

    P.dma("sp", identF.f, c_ident, writes=[identF.reg], semreg=identF.reg)
    P.dma("sp", cosT.f, c_cos, writes=[cosT.reg], semreg=cosT.reg)
    P.dma("sp", sinT.f, c_sin, writes=[sinT.reg], semreg=sinT.reg)
    P.dma("sp", gpre.f, c_gpre, writes=[gpre.reg], semreg=gpre.reg)
    P.dma("sp", gpre2.f, c_gpre2, writes=[gpre2.reg], semreg=gpre2.reg)
    P.dma("sp", lng.f[:, 0:4], c_lng, writes=[lng.reg], semreg=lng.reg)
    P.dma("sp", hbg.f, c_bgate, writes=[hbg.reg], semreg=hbg.reg)
    P.dma("sp", gpost.f, c_gpost, writes=[gpost.reg], semreg=gpost.reg)
    P.dma("sp", gpost2.f, c_gpost2, writes=[gpost2.reg], semreg=gpost2.reg)
    P.dma("pool", masks.b, c_masks, writes=[masks.reg], semreg=masks.reg)

    P.op("dve", lambda e: e.tensor_scalar(out=hbg.f, in0=hbg.f, scalar1=0.5, scalar2=None, op0=ALU.mult),
         reads=[hbg.reg], writes=[hbg.reg])
    P.op("dve", lambda e: e.memset(onesB.b, 1.0), writes=[onesB.reg])
    P.op("dve", lambda e: e.memset(neghalf.f, -0.5), writes=[neghalf.reg])
    P.op("dve", lambda e: e.memset(epsT.f[:, 0:1], EPS), writes=[epsT.reg])
    P.op("dve", lambda e: e.memset(epsT.f[:, 1:2], 4.0 * EPS), writes=[epsT.reg])
    P.op("dve", lambda e: e.tensor_copy(out=identB.b, in_=identF.f), reads=[identF.reg], writes=[identB.reg])

    def rstd_op(dst, src, n, mul, reads, writes, post=1.0, mode="pow"):
        if mode == "pow":
            P.op("dve", lambda e: e.tensor_scalar(out=dst, in0=src, scalar1=mul, scalar2=EPS, op0=ALU.mult, op1=ALU.add),
                 reads=reads, writes=writes)
            P.op("pool", lambda e: e.tensor_tensor(out=dst, in0=dst, in1=neghalf.f[:, 0:n], op=ALU.pow),
                 reads=writes + [neghalf.reg], writes=writes)
            if post != 1.0:
                P.op("dve", lambda e: e.tensor_scalar(out=dst, in0=dst, scalar1=post, scalar2=None, op0=ALU.mult),
                     reads=writes, writes=writes)
        else:
            assert post in (1.0, 0.5)
            bcol = 0 if post == 1.0 else 1
            P.op("act", lambda e: e.activation(out=dst, in_=src, func=AF.Sqrt, scale=mul / (post * post),
                                               bias=epsT.f[:, bcol:bcol + 1]),
                 reads=reads + [epsT.reg], writes=writes)
            P.op("dve", lambda e: e.reciprocal(out=dst, in_=dst), reads=writes, writes=writes)

    m0 = A.mark()
    wsp = Buf("wsp", 512)
    wsTf = Buf("wsTf", 512)
    rows = Buf("rows", 512 * 3 + 128)
    onesF = Buf("onesF", 8)
    P.dma("sp", wsp.f.rearrange("p (g s) -> p g s", g=4), w_sp.rearrange("g t s -> t g s"),
          writes=[wsp.reg], semreg=wsp.reg)
    P.dma("sp", rows.f[0:1, 512:1024], c_lnb, writes=[rows.reg], semreg=rows.reg)
    P.dma("sp", rows.f[0:1, 1024:1536], c_bsp, writes=[rows.reg], semreg=rows.reg)
    P.op("dve", lambda e: e.memset(rows.f[0:1, 1536:1664], 1.0), reads=[], writes=[rows.reg])
    P.op("dve", lambda e: e.memset(onesF.f, 1.0), writes=[onesF.reg])
    bk = next_bank()

    def _tr_ws(e, bk=bk):
        ins = None
        for g in range(4):
            ins = e.transpose(bank_f[bk][:, g * 128:(g + 1) * 128], wsp.f[:, g * 128:(g + 1) * 128], identF.f)
        return ins
    P.op("pe", _tr_ws, reads=[wsp.reg, identF.reg], writes=[bank_reg[bk]])
    P.op("act", lambda e, bk=bk: e.copy(out=wsTf.f, in_=bank_f[bk]), reads=[bank_reg[bk]], writes=[wsTf.reg])
    P.op("dve", lambda e, bk=bk: e.tensor_copy(out=WsT.b, in_=bank_f[bk]), reads=[bank_reg[bk]], writes=[WsT.reg])
    bk = next_bank()

    def _rs(e, bk=bk):
        ins = None
        for g in range(4):
            ins = e.matmul(bank_f[bk][0:1, g * 128:(g + 1) * 128], onesF.f[:, 0:1], wsTf.f[:, g * 128:(g + 1) * 128],
                           start=True, stop=True)
        return ins
    P.op("pe", _rs, reads=[wsTf.reg, onesF.reg], writes=[bank_reg[bk]])
    P.op("act", lambda e, bk=bk: e.copy(out=rows.f[0:1, 0:512], in_=bank_f[bk][0:1, :]),
         reads=[bank_reg[bk]], writes=[rows.reg])
    bk = next_bank()

    def _cg(e, bk=bk):
        ins = None
        for g in range(4):
            sl = slice(g * 128, (g + 1) * 128)
            e.matmul(bank_f[bk][:, sl], rows.f[0:1, 512 + g * 128:512 + (g + 1) * 128], rows.f[0:1, sl],
                     start=True, stop=False)
            ins = e.matmul(bank_f[bk][:, sl], rows.f[0:1, 1536:1664], rows.f[0:1, 1024 + g * 128:1024 + (g + 1) * 128],
                           start=False, stop=True)
        return ins
    P.op("pe", _cg, reads=[rows.reg], writes=[bank_reg[bk]])
    P.op("act", lambda e, bk=bk: e.copy(out=Cg.f, in_=bank_f[bk]), reads=[bank_reg[bk]], writes=[Cg.reg])
    dump("dbg_Cg", Cg.f, Cg.reg, 512)
    dump("dbg_wsTf", wsTf.f, wsTf.reg, 512)
    dump("dbg_rows", rows.f[0:1, :], rows.reg, 1664)
    dump("dbg_WsT", WsT.b, WsT.reg, 512, isbf=True)
    A.release(m0)

    def wload(buf_ap, src_ap, reg, reads=()):
        P.dma("pool", buf_ap, src_ap, reads=list(reads), writes=[reg], semreg=reg)

    nT = Buf("nT", KC * SP // 2)
    nTv = nT.b.rearrange("p (k t) -> p k t", k=KC)
    yaT = Buf("yaT", 4 * S // 2)
    yaTv = yaT.b.rearrange("p (k t) -> p k t", k=4)
    ybT = Buf("ybT", 4 * S // 2)
    ybTv = ybT.b.rearrange("p (k t) -> p k t", k=4)
    junk = Buf("junk", D // 2)
    Wu = Buf("Wu", KC * 512 // 2)
    Wuv = Wu.b.rearrange("p (k n) -> p k n", k=KC)
    Wv = Buf("Wv", KC * 512 // 2)
    Wvv = Wv.b.rearrange("p (k n) -> p k n", k=KC)
    ssb = [Buf("ss%d" % i, 8) for i in range(2)]
    P.op("pool", lambda e: e.memset(nTv[:, :, 0:PAD], 0.0), writes=[nT.reg])
    P.op("pool", lambda e: e.memset(nTv[:, :, PAD + S:SP], 0.0), writes=[nT.reg])
    xq_f = [yaT.f, ybT.f]
    xq_reg = [yaT.reg, ybT.reg]

    def stage1_dma(s, q):
        xv = xq_f[q % 2].rearrange("p (t d) -> p t d", t=4)
        P.dma("sp", xv, x[s, q * 512:(q + 1) * 512, :].rearrange("(t p) d -> p t d", p=128),
              writes=[xq_reg[q % 2]], semreg=xq_reg[q % 2])

    def stage1_front(s, q):
        xf = xq_f[q % 2]
        xreg = xq_reg[q % 2]
        ss = ssb[q % 2]
        xv = xf.rearrange("p (t d) -> p t d", t=4)
        for tt in range(4):
            P.op("act", lambda e, xv=xv, tt=tt, ss=ss: e.activation(out=junk.b, in_=xv[:, tt, :], func=AF.Square,
                                                                     accum_out=ss.f[:, tt:tt + 1]),
                 reads=[xreg], writes=[junk.reg, ss.reg])
        rstd_op(ss.f[:, 0:4], ss.f[:, 0:4], 4, 1.0 / D, [ss.reg], [ss.reg], mode="sqrt")
        for tt in range(4):
            P.op("dve", lambda e, xv=xv, tt=tt, ss=ss: e.tensor_scalar(out=xv[:, tt, :], in0=xv[:, tt, :],
                                                                        scalar1=ss.f[:, tt:tt + 1], scalar2=None,
                                                                        op0=ALU.mult),
                 reads=[xreg, ss.reg], writes=[xreg])

    def stage1_back(s, q):
        xf = xq_f[q % 2]
        xreg = xq_reg[q % 2]
        xv = xf.rearrange("p (t d) -> p t d", t=4)
        for kc in range(KC):
            bk = next_bank((4, 5, 6, 7))

            def _tr(e, bk=bk, xv=xv, kc=kc):
                ins = None
                for tt in range(4):
                    ins = e.transpose(bank_f[bk][:, tt * 128:(tt + 1) * 128], xv[:, tt, kc * 128:(kc + 1) * 128],
                                      identF.f)
                return ins
            P.op("pe", _tr, reads=[xreg, identF.reg], writes=[bank_reg[bk]])
            dst = nTv[:, kc, PAD + q * 512:PAD + (q + 1) * 512]
            if kc % 2 == 0:
                P.op("act", lambda e, bk=bk, dst=dst, kc=kc: e.activation(out=dst, in_=bank_f[bk], func=AF.Identity,
                                                                           scale=gpre.f[:, kc:kc + 1]),
                     reads=[bank_reg[bk], gpre.reg], writes=[nT.reg])
            else:
                P.op("dve", lambda e, bk=bk, dst=dst, kc=kc: e.tensor_scalar(out=dst, in0=bank_f[bk],
                                                                              scalar1=gpre.f[:, kc:kc + 1],
                                                                              scalar2=None, op0=ALU.mult),
                     reads=[bank_reg[bk], gpre.reg], writes=[nT.reg])

    def stage1_quad(s, q, after_dma=None, dma=True):
        if dma:
            stage1_dma(s, q)
        if after_dma is not None:
            after_dma(xq_reg[q % 2])
        stage1_front(s, q)
        stage1_back(s, q)

    def phase1(s):
        mS = A.mark()

        mQ = A.mark()
        Wqk = [Buf("Wqk%d" % i, KC * 768 // 2) for i in range(2)]
        Wv3 = [Buf("Wv3_%d" % i, KC * 384 // 2) for i in range(2)]
        mW = A.mark()

        def slot_w_thunks(j):
            b = j % 2
            th = []
            src = w_in[:, 1024:4096].rearrange("(k p) (m c) -> p k m c", p=128, c=512)
            dstw = Wqk[b].b.rearrange("p (k m c) -> p k m c", k=KC, m=6)
            for m_ in range(6):
                th.append(lambda m_=m_, src=src, dstw=dstw: wload(dstw[:, :, m_, :], src[:, :, m_, j * 128:(j + 1) * 128],
                                                                  Wqk[b].reg))
            src2 = w_in[:, 4096:5632].rearrange("(k p) (m c) -> p k m c", p=128, c=512)
            dstw2 = Wv3[b].b.rearrange("p (k m c) -> p k m c", k=KC, m=3)
            for m_ in range(3):
                th.append(lambda m_=m_, src2=src2, dstw2=dstw2: wload(dstw2[:, :, m_, :],
                                                                      src2[:, :, m_, j * 128:(j + 1) * 128], Wv3[b].reg))
            return th

        def _wuv(xreg):
            wload(Wuv, w_in[:, 0:512].rearrange("(k p) n -> p k n", p=128), Wu.reg, reads=[xreg])
            wload(Wvv, w_in[:, 512:1024].rearrange("(k p) n -> p k n", p=128), Wv.reg)
        if s == 0:
            for q in range(4):
                stage1_quad(0, q, after_dma=_wuv if q == 0 else None)
        m1 = A.mark()
        if debug and "dbg_nT" in dbg and s == 0:
            stg = Buf("dbgstg", KC * SP)
            P.op("dve", lambda e, stg=stg: e.tensor_copy(out=stg.f, in_=nT.b), reads=[nT.reg], writes=[stg.reg])
            P.dma("sp", dbg["dbg_nT"], stg.f, reads=[stg.reg], writes=[], semreg=stg.reg)
            out_regs.append(stg.reg)
            A.release(m1)

        if upto < 2:
            return
        m2 = A.mark()
        uT = Buf("uT", 4 * S // 2)
        uTv = uT.b.rearrange("p (k t) -> p k t", k=4)
        NB2 = 4
        vg = [Buf("vg%d" % i, 512) for i in range(NB2)]
        vn = [Buf("vn%d" % i, 256) for i in range(NB2)]
        tmpg = [Buf("tmpg%d" % i, 512) for i in range(NB2)]
        st6 = [Buf("st6_%d" % i, 8) for i in range(NB2)]
        mv = [Buf("mv%d" % i, 8) for i in range(NB2)]

        def u_group(q, c):
            bk = next_bank((3, 4, 5))

            def _mm(e, bk=bk, q=q, c=c):
                ins = None
                for kc in range(KC):
                    ins = e.matmul(bank_f[bk], Wuv[:, kc, c * 128:(c + 1) * 128],
                                   nTv[:, kc, PAD + q * 512:PAD + (q + 1) * 512], start=(kc == 0), stop=(kc == KC - 1))
                return ins
            P.op("pe", _mm, reads=[Wu.reg, nT.reg], writes=[bank_reg[bk]])
            P.op("act", lambda e, bk=bk, q=q, c=c: e.activation(out=uTv[:, c, q * 512:(q + 1) * 512], in_=bank_f[bk],
                                                                 func=AF.Gelu_apprx_tanh),
                 reads=[bank_reg[bk]], writes=[uT.reg])

        def v_front(t):
            bk = next_bank((0, 1, 2))
            i2 = t % NB2

            def _mm(e, bk=bk, t=t):
                ins = None
                for kc in range(KC):
                    ins = e.matmul(bank_f[bk], nTv[:, kc, PAD + t * 128:PAD + (t + 1) * 128], Wvv[:, kc, :],
                                   start=(kc == 0), stop=(kc == KC - 1))
                return ins
            P.op("pe", _mm, reads=[Wv.reg, nT.reg], writes=[bank_reg[bk]])
            P.op("act", lambda e, bk=bk, i2=i2: e.activation(out=vg[i2].f, in_=bank_f[bk], func=AF.Gelu_apprx_tanh),
                 reads=[bank_reg[bk]], writes=[vg[i2].reg])
            P.op("dve", lambda e, i2=i2: e.bn_stats(out=st6[i2].f[:, 0:6], in_=vg[i2].f), reads=[vg[i2].reg],
                 writes=[st6[i2].reg])
            P.op("dve", lambda e, i2=i2: e.bn_aggr(out=mv[i2].f[:, 0:2], in_=st6[i2].f[:, 0:6]), reads=[st6[i2].reg],
                 writes=[mv[i2].reg])
            rstd_op(mv[i2].f[:, 1:2], mv[i2].f[:, 1:2], 1, 1.0, [mv[i2].reg], [mv[i2].reg])
            P.op("dve", lambda e, i2=i2: e.scalar_tensor_tensor(out=mv[i2].f[:, 2:3], in0=mv[i2].f[:, 0:1], scalar=-1.0,
                                                                 in1=mv[i2].f[:, 1:2], op0=ALU.mult, op1=ALU.mult),
                 reads=[mv[i2].reg], writes=[mv[i2].reg])
            P.op("act", lambda e, i2=i2: e.activation(out=vn[i2].b, in_=vg[i2].f, func=AF.Identity,
                                                       scale=mv[i2].f[:, 1:2], bias=mv[i2].f[:, 2:3]),
                 reads=[vg[i2].reg, mv[i2].reg], writes=[vn[i2].reg])

        def v_back(t):
            i2 = t % NB2
            bk2 = next_bank((6, 7))

            def _sp(e, bk2=bk2, i2=i2):
                ins = None
                for g in range(4):
                    sl = slice(g * 128, (g + 1) * 128)
                    ins = e.matmul(bank_f[bk2][:, sl], vn[i2].b[:, sl], WsT.b[:, sl], start=True, stop=True)
                return ins
            P.op("pe", _sp, reads=[vn[i2].reg, WsT.reg], writes=[bank_reg[bk2]])
            for g in range(4):
                sl = slice(g * 128, (g + 1) * 128)
                P.op("dve", lambda e, bk2=bk2, i2=i2, g=g, sl=sl: e.scalar_tensor_tensor(
                    out=tmpg[i2].f[:, sl], in0=bank_f[bk2][:, sl], scalar=lng.f[:, g:g + 1], in1=Cg.f[:, sl],
                    op0=ALU.mult, op1=ALU.add),
                    reads=[bank_reg[bk2], lng.reg, Cg.reg], writes=[tmpg[i2].reg])
            P.op("pool", lambda e, i2=i2, t=t: e.tensor_tensor(
                out=yaTv[:, :, t * 128:(t + 1) * 128], in0=tmpg[i2].f.rearrange("p (g t) -> p g t", g=4),
                in1=uTv[:, :, t * 128:(t + 1) * 128], op=ALU.mult),
                reads=[tmpg[i2].reg, uT.reg], writes=[yaT.reg])

        SK2 = 3
        sw0 = slot_w_thunks(0)
        for t in range(16 + SK2):
            if t < len(sw0):
                sw0[t]()
            if t < 16:
                if t % 4 == 0:
                    for c in range(4):
                        u_group(t // 4, c)
                v_front(t)
            if t >= SK2:
                v_back(t - SK2)
        dump("dbg_uT", uT.b, uT.reg, 4 * S, isbf=True)
        A.release(m2)
        if debug and "dbg_yaT" in dbg and s == 0:
            stg = Buf("dbgstg2", 4 * S)
            P.op("dve", lambda e, stg=stg: e.tensor_copy(out=stg.f, in_=yaT.b), reads=[yaT.reg], writes=[stg.reg])
            P.dma("sp", dbg["dbg_yaT"], stg.f, reads=[stg.reg], writes=[], semreg=stg.reg)
            out_regs.append(stg.reg)
            A.release(m2)

        if upto < 2.05:
            return
        A.release(mW)
        m3 = A.mark()
        qks = [Buf("qks%d" % i, 768 // 2) for i in range(3)]
        rts = [[Buf("rt%d_%d" % (i, k), 96) for k in range(4)] for i in range(2)]
        qT = Buf("qT", 3 * S // 2)
        qTv = qT.b.rearrange("p (g t) -> p g t", g=3)
        kT = Buf("kT", 3 * SP // 2)
        kTv = kT.b.rearrange("p (g t) -> p g t", g=3)
        VB0 = (0, 17, 37)
        Vt = Buf("Vt", 53 * 128 // 2)
        Vtv = Vt.b.rearrange("p (b d) -> p b d", b=53)
        NPT = 5
        LA = 3
        PT = [Buf("PT%d" % i, 128) for i in range(NPT)]
        accU = Buf("accU", S)
        accD = Buf("accD", S)
        P.op("pool", lambda e: e.memset(kTv[:, :, 0:PAD], 0.0), writes=[kT.reg])
        P.op("pool", lambda e: e.memset(kTv[:, :, PAD + S:SP], 0.0), writes=[kT.reg])

        blocks = []
        for b in range(17):
            blocks.append((0, slice(PAD + 128 * b - 64, PAD + 128 * b + 64)))
        for r in range(4):
            for b in range(5):
                st = PAD + 4 * (128 * b - 64) + r
                blocks.append((1, slice(st, st + 4 * 127 + 1, 4)))
        for r in range(16):
            blocks.append((2, slice(PAD + r, PAD + r + 16 * 127 + 1, 16)))
        vgroups = []
        vb = 0
        while vb < 53:
            g = blocks[vb][0]
            nb = 0
            while vb + nb < 53 and nb < 4 and blocks[vb + nb][0] == g:
                nb += 1
            vgroups.append((vb, nb, g))
            vb += nb

        mAB = masks.b[:, 0:256]
        mA2 = masks.b[:, 256:384]
        mB2 = masks.b[:, 384:512]
        mC = masks.b[:, 512:640]
        tasks = []
        for g in range(3):
            d = (1, 4, 16)[g]
            L = S // d
            if g < 2:
                nqb = L // 128
                for r in range(d):
                    for b in range(nqb + 1):
                        ks = PAD + d * (128 * b - 64) + r
                        ksl = slice(ks, ks + d * 127 + 1, d) if d > 1 else slice(ks, ks + 128)
                        qlo = max(b - 1, 0)
                        qhi = min(b, nqb - 1)
                        mk = mB2 if b == 0 else (mA2 if b == nqb else mAB)
                        tasks.append(dict(g=g, r=r, b=b, nqb=nqb, L=L, ksl=ksl,
                                          qsl=slice(r * L + 128 * qlo, r * L + 128 * (qhi + 1)),
                                          nq=(qhi - qlo + 1) * 128, mk=mk, vbi=VB0[g] + r * (nqb + 1) + b))
            else:
                for r in range(16):
                    tasks.append(dict(g=2, r=r, b=0, nqb=1, L=128, ksl=slice(PAD + r, PAD + r + 16 * 127 + 1, 16),
                                      qsl=slice(r * 128, (r + 1) * 128), nq=128, mk=mC, vbi=VB0[2] + r))

        pt_ctr = [0]

        def finalize_chunk(j, c4):
            sl = slice(c4 * 512, (c4 + 1) * 512)
            P.op("dve", lambda e, sl=sl: e.reciprocal(out=accD.f[:, sl], in_=accD.f[:, sl]), reads=[accD.reg],
                 writes=[accD.reg])
            P.op("dve", lambda e, j=j, sl=sl: e.tensor_tensor(out=ybTv[:, j, sl], in0=accU.f[:, sl], in1=accD.f[:, sl],
                                                             op=ALU.mult),
                 reads=[accU.reg, accD.reg], writes=[ybT.reg])

        for j in range(4):
            swn = slot_w_thunks(j + 1) if j + 1 < 4 else []
            Wq = Wqk[j % 2]
            Wqv = Wq.b.rearrange("p (k n) -> p k n", k=KC)
            Wvj = Wv3[j % 2]
            Wvjv = Wvj.b.rearrange("p (k n) -> p k n", k=KC)

            def qk_front(t, Wqv=Wqv, Wq=Wq):
                pr = next_pair((0, 2))
                qs = qks[t % 3]
                qsv = qs.b.rearrange("p (m d) -> p m d", m=6)
                rt = rts[t % 2]

                def _mm(e, pr=pr, t=t, Wqv=Wqv):
                    ins = None
                    for h in range(2):
                        for kc in range(KC):
                            ins = e.matmul(bank_f[pr + h][:, 0:384], nTv[:, kc, PAD + t * 128:PAD + (t + 1) * 128],
                                           Wqv[:, kc, h * 384:(h + 1) * 384], start=(kc == 0), stop=(kc == KC - 1))
                    return ins
                P.op("pe", _mm, reads=[Wq.reg, nT.reg], writes=[bank_reg[pr], bank_reg[pr + 1]])
                pv = psum[pr // 2].rearrange("p (h n) -> p h n", h=2)[:, :, 0:384].rearrange("p h (m d) -> p h m d", m=3)
                breg = [bank_reg[pr], bank_reg[pr + 1]]
                cosb = cosT.f[:, t * 16:(t + 1) * 16].unsqueeze(1).unsqueeze(1).broadcast_to([128, 2, 3, 16])
                sinb = sinT.f[:, t * 16:(t + 1) * 16].unsqueeze(1).unsqueeze(1).broadcast_to([128, 2, 3, 16])
                x1 = pv[:, :, :, 0:16]
                x2 = pv[:, :, :, 16:32]
                qs4 = qs.b.rearrange("p (h m d) -> p h m d", h=2, m=3)

                def v4(bf):
                    return bf.f.rearrange("p (h m d) -> p h m d", h=2, m=3)
                for k_, (xa, tb_) in enumerate(((x1, cosb), (x2, sinb), (x2, cosb), (x1, sinb))):
                    P.op("dve", lambda e, xa=xa, tb_=tb_, o_=rt[k_]: e.tensor_tensor(out=v4(o_), in0=xa, in1=tb_, op=ALU.mult),
                         reads=breg + [cosT.reg, sinT.reg], writes=[rt[k_].reg])
                for h in range(2):
                    P.op("act", lambda e, pv=pv, h=h, qsv=qsv: e.copy(out=qsv[:, h * 3:(h + 1) * 3, 32:128],
                                                                    in_=pv[:, h, :, 32:128]),
                         reads=[breg[h]], writes=[qs.reg])
                P.op("pool", lambda e, qs4=qs4, rt=rt: e.tensor_tensor(out=qs4[:, :, :, 0:16], in0=v4(rt[0]), in1=v4(rt[1]),
                                                                        op=ALU.subtract),
                     reads=[rt[0].reg, rt[1].reg], writes=[qs.reg])
                P.op("pool", lambda e, qs4=qs4, rt=rt: e.tensor_tensor(out=qs4[:, :, :, 16:32], in0=v4(rt[2]), in1=v4(rt[3]),
                                                                        op=ALU.add),
                     reads=[rt[2].reg, rt[3].reg], writes=[qs.reg])

            def qk_back(t):
                qs = qks[t % 3]
                qsv = qs.b.rearrange("p (m d) -> p m d", m=6)
                bt = next_bank((4, 5))
                btb = bank_f[bt].bitcast(BF16)

                def _tr(e, btb=btb, qsv=qsv):
                    ins = None
                    for m in range(6):
                        ins = e.transpose(btb[:, m * 128:(m + 1) * 128], qsv[:, m, :], identB.b)
                    return ins
                P.op("pe", _tr, reads=[qs.reg, identB.reg], writes=[bank_reg[bt]])
                btv = btb[:, 0:768].rearrange("p (m t) -> p m t", m=6)
                P.op("act", lambda e, btv=btv, t=t: e.copy(out=kTv[:, :, PAD + t * 128:PAD + (t + 1) * 128],
                                                          in_=btv[:, 3:6, :]),
                     reads=[bank_reg[bt]], writes=[kT.reg])
                P.op("act", lambda e, btv=btv, t=t: e.copy(
                    out=qTv[:, 2, :].rearrange("p (r i) -> p r i", r=16)[:, :, 8 * t:8 * t + 8],
                    in_=btv[:, 2, :].rearrange("p (i r) -> p r i", r=16)),
                    reads=[bank_reg[bt]], writes=[qT.reg])
                P.op("dve", lambda e, btv=btv, t=t: e.tensor_copy(out=qTv[:, 0, t * 128:(t + 1) * 128], in_=btv[:, 0, :]),
                     reads=[bank_reg[bt]], writes=[qT.reg])
                P.op("dve", lambda e, btv=btv, t=t: e.tensor_copy(
                    out=qTv[:, 1, :].rearrange("p (r i) -> p r i", r=4)[:, :, 32 * t:32 * t + 32],
                    in_=btv[:, 1, :].rearrange("p (i r) -> p r i", r=4)),
                    reads=[bank_reg[bt]], writes=[qT.reg])

            def v_group(gi, Wvjv=Wvjv, Wvj=Wvj):
                vb, nb, g = vgroups[gi]
                bk = next_bank((6, 7))

                def _mm(e, bk=bk, vb=vb, nb=nb, g=g, Wvjv=Wvjv):
                    ins = None
                    for i in range(nb):
                        tok = blocks[vb + i][1]
                        for kc in range(KC):
                            ins = e.matmul(bank_f[bk][:, i * 128:(i + 1) * 128], nTv[:, kc, tok],
                                           Wvjv[:, kc, g * 128:(g + 1) * 128], start=(kc == 0), stop=(kc == KC - 1))
                    return ins
                P.op("pe", _mm, reads=[nT.reg, Wvj.reg], writes=[bank_reg[bk]])
                dstv = Vt.b[:, vb * 128:(vb + nb) * 128]
                if gi % 2 == 0:
                    P.op("act", lambda e, bk=bk, dstv=dstv, nb=nb: e.copy(out=dstv, in_=bank_f[bk][:, 0:nb * 128]),
                         reads=[bank_reg[bk]], writes=[Vt.reg])
                else:
                    P.op("dve", lambda e, bk=bk, dstv=dstv, nb=nb: e.tensor_copy(out=dstv, in_=bank_f[bk][:, 0:nb * 128]),
                         reads=[bank_reg[bk]], writes=[Vt.reg])

            if upto < 2.15:
                return
            vgi = 0
            QSK = 2
            for t in range(16 + QSK):
                if t < len(swn):
                    swn[t]()
                if j >= 1 and t in (3, 6, 9, 12):
                    finalize_chunk(j - 1, (t - 3) // 3)
                if t < 16:
                    qk_front(t)
                if vgi < len(vgroups):
                    v_group(vgi)
                    vgi += 1
                if t >= QSK:
                    qk_back(t - QSK)
            while vgi < len(vgroups):
                v_group(vgi)
                vgi += 1
            if upto < 2.25:
                return

            def evac_OD(g, pq, bo, bd):
                for (bkx, acc) in ((bo, accU), (bd, accD)):
                    if g == 0:
                        dst = acc.f[:, pq * 512:(pq + 1) * 512]
                        P.op("act", lambda e, bkx=bkx, dst=dst: e.copy(out=dst, in_=bank_f[bkx]),
                             reads=[bank_reg[bkx]], writes=[acc.reg])
                    elif g == 1:
                        dst = acc.f.rearrange("p (i r) -> p r i", r=4)[:, pq, :]
                        P.op("dve", lambda e, bkx=bkx, dst=dst: e.tensor_tensor(out=dst, in0=bank_f[bkx], in1=dst,
                                                                                 op=ALU.add),
                             reads=[bank_reg[bkx], acc.reg], writes=[acc.reg])
                    else:
                        dst = acc.f.rearrange("p (i r) -> p r i", r=16)[:, 4 * pq:4 * pq + 4, :]
                        src = bank_f[bkx].rearrange("p (r i) -> p r i", r=4)
                        P.op("dve", lambda e, src=src, dst=dst: e.tensor_tensor(out=dst, in0=src, in1=dst, op=ALU.add),
                             reads=[bank_reg[bkx], acc.reg], writes=[acc.reg])

            def att_front(T):
                bs = next_bank((0, 1, 2, 3))
                pt = PT[pt_ctr[0] % NPT]
                pt_ctr[0] += 1
                T["pt"] = pt
                nq = T["nq"]
                P.op("pe", lambda e, bs=bs, T=T, nq=nq: e.matmul(bank_f[bs][:, 0:nq], kTv[:, T["g"], T["ksl"]],
                                                               qTv[:, T["g"], T["qsl"]], start=True, stop=True),
                     reads=[kT.reg, qT.reg], writes=[bank_reg[bs]])
                P.op("act", lambda e, bs=bs, pt=pt, nq=nq: e.activation(out=pt.b[:, 0:nq], in_=bank_f[bs][:, 0:nq],
                                                                       func=AF.Exp, scale=SCALE),
                     reads=[bank_reg[bs]], writes=[pt.reg])
                P.op("pool", lambda e, pt=pt, nq=nq, mk=T["mk"]: e.tensor_tensor(out=pt.b[:, 0:nq], in0=pt.b[:, 0:nq],
                                                                                in1=mk, op=ALU.mult),
                     reads=[pt.reg, masks.reg], writes=[pt.reg])

            ost = {"bo": None, "bd": None}

            def att_back(T):
                g, r, b, nqb, L, vbi, pt = T["g"], T["r"], T["b"], T["nqb"], T["L"], T["vbi"], T["pt"]
                if g == 2:
                    if r % 4 == 0:
                        ost["bo"] = next_pair((4, 6))
                        ost["bd"] = ost["bo"] + 1
                    bo, bd = ost["bo"], ost["bd"]
                    cs = slice((r % 4) * 128, (r % 4) * 128 + 128)

                    def _pv(e, bo=bo, bd=bd, cs=cs, vbi=vbi, pt=pt):
                        e.matmul(bank_f[bo][:, cs], Vtv[:, vbi, :], pt.b[:, 0:128], start=True, stop=True)
                        return e.matmul(bank_f[bd][:, cs], onesB.b, pt.b[:, 0:128], start=True, stop=True)
                    P.op("pe", _pv, reads=[Vt.reg, pt.reg, onesB.reg], writes=[bank_reg[bo], bank_reg[bd]])
                    if r % 4 == 3:
                        evac_OD(2, r // 4, bo, bd)
                    return
                col = 0
                if b >= 1:
                    qb = b - 1
                    bo, bd = ost["bo"], ost["bd"]
                    cs = slice((qb % 4) * 128, (qb % 4) * 128 + 128)

                    def _fin(e, bo=bo, bd=bd, cs=cs, vbi=vbi, pt=pt):
                        e.matmul(bank_f[bo][:, cs], Vtv[:, vbi, :], pt.b[:, 0:128], start=False, stop=True)
                        return e.matmul(bank_f[bd][:, cs], onesB.b, pt.b[:, 0:128], start=False, stop=True)
                    P.op("pe", _fin, reads=[Vt.reg, pt.reg, onesB.reg], writes=[bank_reg[bo], bank_reg[bd]])
                    col = 128
                    if qb % 4 == 3 or qb == nqb - 1:
                        evac_OD(g, (r * L + 128 * qb) // 512, bo, bd)
                if b <= nqb - 1:
                    qb = b
                    if qb % 4 == 0:
                        ost["bo"] = next_pair((4, 6))
                        ost["bd"] = ost["bo"] + 1
                    bo, bd = ost["bo"], ost["bd"]
                    cs = slice((qb % 4) * 128, (qb % 4) * 128 + 128)

                    def _sta(e, bo=bo, bd=bd, cs=cs, vbi=vbi, pt=pt, col=col):
                        e.matmul(bank_f[bo][:, cs], Vtv[:, vbi, :], pt.b[:, col:col + 128], start=True, stop=False)
                        return e.matmul(bank_f[bd][:, cs], onesB.b, pt.b[:, col:col + 128], start=True, stop=False)
                    P.op("pe", _sta, reads=[Vt.reg, pt.reg, onesB.reg], writes=[bank_reg[bo], bank_reg[bd]])

            ntk = len(tasks) if upto >= 2.5 else (18 if upto < 2.35 else 38)
            tl = [dict(T) for T in tasks[:ntk]]
            for i in range(ntk + LA):
                if i < ntk:
                    att_front(tl[i])
                if i >= LA:
                    att_back(tl[i - LA])
            if upto < 2.65:
                return
            if j == 3:
                for c4 in range(4):
                    finalize_chunk(3, c4)
        A.release(m3)
        if debug and "dbg_ybT" in dbg and s == 0:
            stg = Buf("dbgstg3", 4 * S)
            P.op("dve", lambda e, stg=stg: e.tensor_copy(out=stg.f, in_=ybT.b), reads=[ybT.reg], writes=[stg.reg])
            P.dma("sp", dbg["dbg_ybT"], stg.f, reads=[stg.reg], writes=[], semreg=stg.reg)
            out_regs.append(stg.reg)
            A.release(m3)

        if upto < 4:
            return
        A.release(mQ)
        m4 = A.mark()
        mT = Buf("mT", KC * S // 2)
        mTv = mT.b.rearrange("p (k t) -> p k t", k=KC)
        Wo = Buf("Wo", KC * D // 2)
        Wov = Wo.b.rearrange("p (k n) -> p k n", k=KC)
        Wg = [Buf("Wg%d" % i, KC * 256 // 2) for i in range(2)]
        Wab = [Buf("Wab%d" % i, 2 * 4 * 128 // 2) for i in range(2)]
        def load_c(c):
            b = c % 2
            src = w_in[:, 5632:7680].rearrange("(k p) (m n) -> p k m n", p=128, n=1024)
            dstw = Wg[b].b.rearrange("p (k m n) -> p k m n", k=KC, m=2)
            for m_ in range(2):
                wload(dstw[:, :, m_, :], src[:, :, m_, c * 128:(c + 1) * 128], Wg[b].reg)
            wab = Wab[b].b.rearrange("p (w k n) -> p w k n", w=2, k=4)
            wload(wab[:, 0, :, :], w_a[:, c * 128:(c + 1) * 128].rearrange("(k p) n -> p k n", p=128), Wab[b].reg)
            wload(wab[:, 1, :, :], w_b[:, c * 128:(c + 1) * 128].rearrange("(k p) n -> p k n", p=128), Wab[b].reg)

        load_c(0)
        ta = [Buf("ta%d" % i, 512) for i in range(2)]
        tb = [Buf("tb%d" % i, 512) for i in range(2)]
        m1b = [Buf("m1b%d" % i, 512) for i in range(2)]
        m2b = [Buf("m2b%d" % i, 512) for i in range(2)]

        it = 0
        for c in range(8):
            if c + 1 < 8:
                load_c(c + 1)
            if c == 1:
                wload(Wov[:, 0:4, :], w_out[0:512, :].rearrange("(k p) n -> p k n", p=128), Wo.reg)
            if c == 2:
                wload(Wov[:, 4:8, :], w_out[512:1024, :].rearrange("(k p) n -> p k n", p=128), Wo.reg)
            wg = Wg[c % 2]
            wgv = wg.b.rearrange("p (k m n) -> p k m n", k=KC, m=2)
            wab = Wab[c % 2]
            wabv = wab.b.rearrange("p (w k n) -> p w k n", w=2, k=4)
            for q in range(4):
                i2 = it % 2
                it += 1
                tsl = slice(PAD + q * 512, PAD + (q + 1) * 512)
                qsl = slice(q * 512, (q + 1) * 512)
                bga, bgb, bA, bB = next_bank(), next_bank(), next_bank(), next_bank()

                def _mm(e, bga=bga, bgb=bgb, bA=bA, bB=bB, wgv=wgv, wabv=wabv, tsl=tsl, qsl=qsl):
                    ins = None
                    for kc in range(KC):
                        e.matmul(bank_f[bga], wgv[:, kc, 0, :], nTv[:, kc, tsl], start=(kc == 0), stop=(kc == KC - 1))
                    for kc in range(KC):
                        e.matmul(bank_f[bgb], wgv[:, kc, 1, :], nTv[:, kc, tsl], start=(kc == 0), stop=(kc == KC - 1))
                    for kc in range(4):
                        e.matmul(bank_f[bA], wabv[:, 0, kc, :], yaTv[:, kc, qsl], start=(kc == 0), stop=(kc == 3))
                    for kc in range(4):
                        ins = e.matmul(bank_f[bB], wabv[:, 1, kc, :], ybTv[:, kc, qsl], start=(kc == 0), stop=(kc == 3))
                    return ins
                P.op("pe", _mm, reads=[wg.reg, wab.reg, nT.reg, yaT.reg, ybT.reg],
                     writes=[bank_reg[bga], bank_reg[bgb], bank_reg[bA], bank_reg[bB]])
                P.op("act", lambda e, bga=bga, i2=i2, c=c: e.activation(out=ta[i2].f, in_=bank_f[bga], func=AF.Tanh,
                                                                       bias=hbg.f[:, c:c + 1], scale=0.5),
                     reads=[bank_reg[bga], hbg.reg], writes=[ta[i2].reg])
                P.op("act", lambda e, bgb=bgb, i2=i2, c=c: e.activation(out=tb[i2].f, in_=bank_f[bgb], func=AF.Tanh,
                                                                       bias=hbg.f[:, 8 + c:9 + c], scale=0.5),
                     reads=[bank_reg[bgb], hbg.reg], writes=[tb[i2].reg])
                P.op("dve", lambda e, bA=bA, i2=i2: e.scalar_tensor_tensor(out=m1b[i2].f, in0=ta[i2].f, scalar=1.0,
                                                                          in1=bank_f[bA], op0=ALU.add, op1=ALU.mult),
                     reads=[ta[i2].reg, bank_reg[bA]], writes=[m1b[i2].reg])
                P.op("dve", lambda e, bB=bB, i2=i2: e.scalar_tensor_tensor(out=m2b[i2].f, in0=tb[i2].f, scalar=1.0,
                                                                          in1=bank_f[bB], op0=ALU.add, op1=ALU.mult),
                     reads=[tb[i2].reg, bank_reg[bB]], writes=[m2b[i2].reg])
                P.op("pool", lambda e, i2=i2, c=c, qsl=qsl: e.tensor_tensor(out=mTv[:, c, qsl], in0=m1b[i2].f,
                                                                           in1=m2b[i2].f, op=ALU.add),
                     reads=[m1b[i2].reg, m2b[i2].reg], writes=[mT.reg])
        if s == nseq - 1 and 2 in phases:
            top_save = A.top
            A.top = mP1
            p2w["Wup"] = Buf("Wup", KC * 4096 // 2)
            p2w["wup_r"] = [A.reg("Wup_c%d" % i, p2w["Wup"].st, p2w["Wup"].n) for i in range(4)]
            wupv_ = p2w["Wup"].b.rearrange("p (k n) -> p k n", k=KC)
            A.top = top_save
        xt = [Buf("xt%d" % i, D) for i in range(2)]
        h1 = [Buf("h1_%d" % i, D) for i in range(2)]
        junk2 = Buf("junk2", D // 2)
        ss2 = [Buf("ss2_%d" % i, 8) for i in range(2)]
        for t in range(16):
            i2 = t % 2
            P.dma("sp", xt[i2].f, x[s, t * 128:(t + 1) * 128, :], writes=[xt[i2].reg], semreg=xt[i2].reg)
            if s + 1 < nseq:
                if t == 0:
                    stage1_dma(s + 1, 0)
                    stage1_dma(s + 1, 1)
                if t in (9, 13):
                    stage1_dma(s + 1, 2 + (t - 9) // 4)
                if t % 4 == 3:
                    stage1_front(s + 1, t // 4)
                if t >= 6 and t % 4 == 2:
                    stage1_back(s + 1, (t - 6) // 4)
            if "Wup" in p2w and s == nseq - 1 and t % 4 == 1:
                i_ = t // 4
                wload(p2w["Wup"].b.rearrange("p (k n) -> p k n", k=KC)[:, :, i_ * 1024:(i_ + 1) * 1024],
                      w_up[:, i_ * 1024:(i_ + 1) * 1024].rearrange("(k p) n -> p k n", p=128), p2w["wup_r"][i_])
            pr = next_pair((0, 2))

            def _mm(e, pr=pr, t=t):
                ins = None
                for h in range(2):
                    for kc in range(KC):
                        ins = e.matmul(bank_f[pr + h], mTv[:, kc, t * 128:(t + 1) * 128], Wov[:, kc, h * 512:(h + 1) * 512],
                                       start=(kc == 0), stop=(kc == KC - 1))
                return ins
            P.op("pe", _mm, reads=[mT.reg, Wo.reg], writes=[bank_reg[pr], bank_reg[pr + 1]])
            yv = psum[pr // 2]
            breg = [bank_reg[pr], bank_reg[pr + 1]]
            P.op("act", lambda e, yv=yv, i2=i2: e.activation(out=junk2.b,
                                                            in_=yv[:, :], func=AF.Square, scale=0.5,
                                                            accum_out=ss2[i2].f[:, 0:1]),
                 reads=breg, writes=[junk2.reg, ss2[i2].reg])
            rstd_op(ss2[i2].f[:, 0:1], ss2[i2].f[:, 0:1], 1, 1.0 / D, [ss2[i2].reg], [ss2[i2].reg], post=0.5, mode="sqrt")
            P.op("dve", lambda e, yv=yv, i2=i2: e.scalar_tensor_tensor(out=h1[i2].f, in0=yv[:, :], scalar=ss2[i2].f[:, 0:1],
                                                                      in1=gpost.f, op0=ALU.mult, op1=ALU.mult),
                 reads=breg + [ss2[i2].reg, gpost.reg], writes=[h1[i2].reg])
            P.op("dve" if s == nseq - 1 else "pool",
                 lambda e, i2=i2: e.tensor_tensor(out=h1[i2].f, in0=h1[i2].f, in1=xt[i2].f, op=ALU.add),
                 reads=[h1[i2].reg, xt[i2].reg], writes=[h1[i2].reg])
            P.dma("sp", out[s, t * 128:(t + 1) * 128, :], h1[i2].f, reads=[h1[i2].reg], writes=[h1dram[s][t]],
                  semreg=h1[i2].reg)
            out_regs.append(h1[i2].reg)
        if s + 1 < nseq:
            stage1_back(s + 1, 3)
        A.release(mS)

    p2w = {}
    h1dram = [[P.reg("h1d_%d_%d" % (s, t)) for t in range(16)] for s in range(nseq)]

    if 1 in phases:
        for s in range(nseq):
            phase1(s)

    def phase2():
        A.release(mP1)
        if "Wup" in p2w:
            Wup = p2w["Wup"]
            wup_r = p2w["wup_r"]
            A.top = Wup.st + Wup.n
            Wupv = Wup.b.rearrange("p (k n) -> p k n", k=KC)
        else:
            Wup = Buf("Wup", KC * 4096 // 2)
            Wupv = Wup.b.rearrange("p (k n) -> p k n", k=KC)
            wup_r = [A.reg("Wup_c%d" % i, Wup.st, Wup.n) for i in range(4)]
            for i in range(4):
                wload(Wupv[:, :, i * 1024:(i + 1) * 1024],
                      w_up[:, i * 1024:(i + 1) * 1024].rearrange("(k p) n -> p k n", p=128), wup_r[i])
        Wdn = Buf("Wdn", 32 * D // 2)
        Wdnv = Wdn.b.rearrange("p (k n) -> p k n", k=32)
        wdn_r = [A.reg("Wdn_c%d" % i, Wdn.st, Wdn.n) for i in range(4)]
        hin = [Buf("hin%d" % i, D) for i in range(2)]
        hres = [Buf("hres%d" % i, D) for i in range(2)]
        ot = [Buf("ot%d" % i, D) for i in range(2)]
        n2T = Buf("n2T", KC * 512 // 2)
        n2Tv = n2T.b.rearrange("p (k t) -> p k t", k=KC)
        hidT = Buf("hidT", 32 * 512 // 2)
        hidTv = hidT.b.rearrange("p (k t) -> p k t", k=32)
        rl = [Buf("rl%d" % i, 256) for i in range(2)]
        junk3 = Buf("junk3", D // 2)
        ss3 = [Buf("ss3_%d" % i, 8) for i in range(2)]
        ss4 = [Buf("ss4_%d" % i, 8) for i in range(2)]
        quads = [(s, q) for s in range(nseq) for q in range(4)]

        def pro_front(s, q, tt):
            hb = hin[tt % 2]
            sb = ss3[tt % 2]
            t = 4 * q + tt
            P.dma("sp", hb.f, out[s, t * 128:(t + 1) * 128, :], reads=[h1dram[s][t]], writes=[hb.reg], semreg=hb.reg)
            P.op("act", lambda e, hb=hb, sb=sb: e.activation(out=junk3.b, in_=hb.f, func=AF.Square,
                                                            accum_out=sb.f[:, 0:1]),
                 reads=[hb.reg], writes=[junk3.reg, sb.reg])
            rstd_op(sb.f[:, 0:1], sb.f[:, 0:1], 1, 1.0 / D, [sb.reg], [sb.reg], mode="sqrt")
            P.op("dve", lambda e, hb=hb, sb=sb: e.tensor_scalar(out=hb.f, in0=hb.f, scalar1=sb.f[:, 0:1], scalar2=None,
                                                               op0=ALU.mult),
                 reads=[hb.reg, sb.reg], writes=[hb.reg])

        def pro_back(s, q, tt):
            hb = hin[tt % 2]
            for half in range(2):
                bk = next_bank((6, 7))

                def _tr(e, bk=bk, hb=hb, half=half):
                    ins = None
                    for k4 in range(4):
                        kc = half * 4 + k4
                        ins = e.transpose(bank_f[bk][:, k4 * 128:(k4 + 1) * 128], hb.f[:, kc * 128:(kc + 1) * 128],
                                          identF.f)
                    return ins
                P.op("pe", _tr, reads=[hb.reg, identF.reg], writes=[bank_reg[bk]])
                for k4 in range(4):
                    kc = half * 4 + k4
                    dst = n2Tv[:, kc, tt * 128:(tt + 1) * 128]
                    if k4 % 2 == 0:
                        P.op("act", lambda e, bk=bk, k4=k4, kc=kc, dst=dst: e.activation(
                            out=dst, in_=bank_f[bk][:, k4 * 128:(k4 + 1) * 128], func=AF.Identity,
                            scale=gpre2.f[:, kc:kc + 1]),
                            reads=[bank_reg[bk], gpre2.reg], writes=[n2T.reg])
                    else:
                        P.op("dve", lambda e, bk=bk, k4=k4, kc=kc, dst=dst: e.tensor_scalar(
                            out=dst, in0=bank_f[bk][:, k4 * 128:(k4 + 1) * 128], scalar1=gpre2.f[:, kc:kc + 1],
                            scalar2=None, op0=ALU.mult),
                            reads=[bank_reg[bk], gpre2.reg], writes=[n2T.reg])

        def up(s, q, nxt=None):
            for fc in range(32):
                if nxt is not None and fc in (10, 22):
                    pro_front(nxt[0], nxt[1], 0 if fc == 10 else 1)
                bk = next_bank((0, 1, 2, 3))
                rb = rl[fc % 2]

                def _mm(e, bk=bk, fc=fc):
                    ins = None
                    for kc in range(KC):
                        ins = e.matmul(bank_f[bk], Wupv[:, kc, fc * 128:(fc + 1) * 128], n2Tv[:, kc, :],
                                       start=(kc == 0), stop=(kc == KC - 1))
                    return ins
                P.op("pe", _mm, reads=[wup_r[fc // 8], n2T.reg], writes=[bank_reg[bk]])
                P.op("act", lambda e, bk=bk, rb=rb: e.activation(out=rb.b, in_=bank_f[bk], func=AF.Relu),
                     reads=[bank_reg[bk]], writes=[rb.reg])
                P.op("dve", lambda e, bk=bk, rb=rb, fc=fc: e.tensor_tensor(out=hidTv[:, fc, :], in0=bank_f[bk], in1=rb.b,
                                                                          op=ALU.mult),
                     reads=[bank_reg[bk], rb.reg], writes=[hidT.reg])

        def down(s, q, mid=None):
            for tt in range(4):
                if tt == 2 and mid is not None:
                    mid()
                i2 = tt % 2
                t = 4 * q + tt
                pr = next_pair((4, 6))
                P.dma("sp", hres[i2].f, out[s, t * 128:(t + 1) * 128, :], reads=[h1dram[s][t]], writes=[hres[i2].reg],
                      semreg=hres[i2].reg)

                def _mm(e, pr=pr, tt=tt):
                    ins = None
                    for h in range(2):
                        for fc in range(32):
                            ins = e.matmul(bank_f[pr + h], hidTv[:, fc, tt * 128:(tt + 1) * 128],
                                           Wdnv[:, fc, h * 512:(h + 1) * 512], start=(fc == 0), stop=(fc == 31))
                    return ins
                P.op("pe", _mm, reads=[hidT.reg] + wdn_r, writes=[bank_reg[pr], bank_reg[pr + 1]])
                yv = psum[pr // 2]
                breg = [bank_reg[pr], bank_reg[pr + 1]]
                P.op("act", lambda e, yv=yv, i2=i2: e.activation(out=junk3.b, in_=yv[:, :], func=AF.Square,
                                                                accum_out=ss4[i2].f[:, 0:1]),
                     reads=breg, writes=[junk3.reg, ss4[i2].reg])
                rstd_op(ss4[i2].f[:, 0:1], ss4[i2].f[:, 0:1], 1, 1.0 / D, [ss4[i2].reg], [ss4[i2].reg], mode="sqrt")
                P.op("dve", lambda e, yv=yv, i2=i2: e.scalar_tensor_tensor(out=ot[i2].f, in0=yv[:, :],
                                                                          scalar=ss4[i2].f[:, 0:1], in1=gpost2.f,
                                                                          op0=ALU.mult, op1=ALU.mult),
                     reads=breg + [ss4[i2].reg, gpost2.reg], writes=[ot[i2].reg])
                P.op("pool", lambda e, i2=i2: e.tensor_tensor(out=ot[i2].f, in0=ot[i2].f, in1=hres[i2].f, op=ALU.add),
                     reads=[ot[i2].reg, hres[i2].reg], writes=[ot[i2].reg])
                P.dma("sp", out[s, t * 128:(t + 1) * 128, :], ot[i2].f, reads=[ot[i2].reg, hres[i2].reg],
                      writes=[h1dram[s][t]], semreg=ot[i2].reg)
                out_regs.append(ot[i2].reg)

        for tt in range(4):
            pro_front(quads[0][0], quads[0][1], tt) if tt < 2 else None
        pro_back(quads[0][0], quads[0][1], 0)
        pro_back(quads[0][0], quads[0][1], 1)
        for tt in (2, 3):
            pro_front(quads[0][0], quads[0][1], tt)
        for tt in (2, 3):
            pro_back(quads[0][0], quads[0][1], tt)
        for i in range(4):
            wload(Wdnv[:, i * 8:(i + 1) * 8, :],
                  w_down[i * 1024:(i + 1) * 1024, :].rearrange("(k p) n -> p k n", p=128), wdn_r[i])
        for i, (s, q) in enumerate(quads):
            nxt = quads[i + 1] if i + 1 < len(quads) else None
            up(s, q, nxt)
            mid = None
            if nxt is not None:
                pro_back(nxt[0], nxt[1], 0)
                pro_back(nxt[0], nxt[1], 1)
                pro_front(nxt[0], nxt[1], 2)
                pro_front(nxt[0], nxt[1], 3)

                def mid(nxt=nxt):
                    pro_back(nxt[0], nxt[1], 2)
                    pro_back(nxt[0], nxt[1], 3)
            down(s, q, mid)

    if 2 in phases:
        phase2()

    seen = []
    for r in out_regs:
        if r not in seen:
            seen.append(r)
    P.final_wait("sp", seen)

    P.resolve()
    with stack:
        with nc.Block() as block:
            @block.tensor
            def _(e):
                P.emit_engine("pe", e)

            @block.scalar
            def _(e):
                P.emit_engine("act", e)

            @block.vector
            def _(e):
                P.emit_engine("dve", e)

            @block.gpsimd
            def _(e):
                P.emit_engine("pool", e)

            @block.sync
            def _(e):
                P.emit_engine("sp", e)
    return nc, A.peak


def _consts():
    c = {}
    c["c_ident"] = np.eye(128, dtype=np.float32)
    p = np.arange(128)[:, None]
    q = np.arange(128)[None, :]
    mA = (q >= p)
    mB = (q <= p)
    mA2 = mA & (p < 64)
    mB2 = mB & (p >= 64)
    mC = np.abs(q - p) <= 64
    c["c_masks"] = np.ascontiguousarray(np.concatenate([mA, mB, mA2, mB2, mC], axis=1).astype(np.float32))
    inv_freq = 500000.0 ** (-np.arange(0, 32, 2, dtype=np.float32) / 32.0)
    pos = (np.arange(16)[None, :, None] * 128 + np.arange(128)[:, None, None]).astype(np.float32)
    ang = pos * inv_freq[None, None, :].astype(np.float32)
    c["c_cos"] = np.ascontiguousarray(np.cos(ang).astype(np.float32).reshape(128, 256))
    c["c_sin"] = np.ascontiguousarray(np.sin(ang).astype(np.float32).reshape(128, 256))
    return c


def _prep_inputs(inputs):
    f = lambda a: np.ascontiguousarray(np.asarray(a, dtype=np.float32))
    shared = {
        "w_in": f(inputs["w_in"][0]),
        "w_spatial": f(inputs["w_spatial"][0]),
        "w_branch_a": f(inputs["w_branch_a"][0]),
        "w_branch_b": f(inputs["w_branch_b"][0]),
        "w_out": f(inputs["w_out"][0]),
        "w_up": f(inputs["w_up"][0]),
        "w_down": f(inputs["w_down"][0]),
        "c_gpre": f(np.asarray(inputs["norm_mix_pre"][0]).reshape(8, 128).T),
        "c_gpre2": f(np.asarray(inputs["norm_mlp_pre"][0]).reshape(8, 128).T),
        "c_lng": f(np.asarray(inputs["ln_v_gain"][0]).reshape(4, 128).T),
        "c_bgate": f(np.asarray(inputs["b_gate"][0]).reshape(16, 128).T),
        "c_lnb": f(np.asarray(inputs["ln_v_bias"][0]).reshape(1, 512)),
        "c_bsp": f(np.asarray(inputs["b_spatial"][0]).reshape(1, 512)),
        "c_gpost": f(np.broadcast_to(np.asarray(inputs["norm_mix_post"][0])[None, :], (128, D))),
        "c_gpost2": f(np.broadcast_to(np.asarray(inputs["norm_mlp_post"][0])[None, :], (128, D))),
    }
    shared.update(_consts())
    return shared


_CACHE = {}


def kernel(**inputs):
    x = np.asarray(inputs["x"], dtype=np.float32)
    shared = _prep_inputs(inputs)
    if "nc" not in _CACHE:
        _CACHE["nc"] = build(SEQ_PER_CORE)[0]
    nc = _CACHE["nc"]
    in_maps = []
    for c in range(NCORES):
        m = dict(shared)
        m["x"] = np.ascontiguousarray(x[c * SEQ_PER_CORE:(c + 1) * SEQ_PER_CORE])
        in_maps.append(m)
    res = run_bass_kernel_spmd(nc, in_maps, core_ids=list(range(NCORES)))
    outs = [np.asarray(r["out"], dtype=np.float32) for r in res.results]
    return np.concatenate(outs, axis=0)
```

```python
import math
import numpy as np
import concourse.bass as bass
import concourse.mybir as mybir
from concourse.bass_utils import run_bass_kernel_spmd

F32 = mybir.dt.float32
BF16 = mybir.dt.bfloat16
AF = mybir.ActivationFunctionType
ALU = mybir.AluOpType

S = 2048
D = 1024
KC = 8
PAD = 256
SP = S + 2 * PAD
EPS = 1e-6
NCORES = 8
SEQ_PER_CORE = 2
SCALE = 1.0 / math.sqrt(128.0)
SAME_ENGINE_SYNC = True
import os as _os
SKIP = set(_os.environ.get('KSKIP', '').split(','))
POOL_DMA_MAX_OUTSTANDING = 2


class Reg:
    __slots__ = ("name", "w", "r", "dsem", "dcnt", "alias", "excl")

    def __init__(self, name):
        self.name = name
        self.excl = False
        self.w = None
        self.r = []
        self.dsem = None
        self.dcnt = 0
        self.alias = []


class Op:
    __slots__ = ("eng", "fn", "waits", "inc", "count", "dreg", "dval")

    def __init__(self, eng, fn):
        self.eng = eng
        self.fn = fn
        self.waits = []
        self.inc = False
        self.count = None
        self.dreg = None
        self.dval = None


class Prog:
    ENGS = ("pe", "act", "dve", "pool", "sp")

    def __init__(self, nc, stack):
        self.nc = nc
        self.stack = stack
        self.ops = {e: [] for e in self.ENGS}
        self.sems = {e: stack.enter_context(nc.semaphore("s_" + e)) for e in ("pe", "act", "dve", "pool")}
        self.nreg = 0

    def reg(self, name, alias=()):
        r = Reg(name)
        r.alias = list(alias)
        return r

    def _dep(self, o, tok):
        if tok is None:
            return
        if tok[0] == "op":
            p = tok[1]
            if p.eng == o.eng and (p.eng in ("pe", "sp") or not SAME_ENGINE_SYNC):
                return
            p.inc = True
        o.waits.append(tok)

    def _deps(self, o, reads, writes):
        for r in reads:
            for a in r.alias:
                self._dep(o, a.w)
            self._dep(o, r.w)
            if r.excl:
                for t in r.r:
                    if t[0] == "op" and t[1].eng != o.eng:
                        self._dep(o, t)
        for w in writes:
            for a in w.alias:
                self._dep(o, a.w)
                for t in a.r:
                    self._dep(o, t)
            w.alias = []
            self._dep(o, w.w)
            for t in w.r:
                self._dep(o, t)

    def _mark(self, tok, reads, writes):
        for r in reads:
            key = tok[1].eng if tok[0] == "op" else ("d", id(tok[1]))
            r.r = [t for t in r.r if (t[1].eng if t[0] == "op" else ("d", id(t[1]))) != key]
            r.r.append(tok)
        for w in writes:
            w.w = tok
            w.r = []

    def op(self, eng, fn, reads=(), writes=()):
        o = Op(eng, fn)
        self._deps(o, reads, writes)
        self.ops[eng].append(o)
        self._mark(("op", o), reads, writes)
        return o

    def dma(self, eng, out, in_, reads=(), writes=(), semreg=None):
        if semreg.dsem is None:
            self.nreg += 1
            semreg.dsem = self.stack.enter_context(self.nc.semaphore("d%d_%s" % (self.nreg, semreg.name)))
        o = Op(eng, lambda e: e.dma_start(out=out, in_=in_))
        self._deps(o, reads, writes)
        if eng == "pool" and POOL_DMA_MAX_OUTSTANDING:
            hist = self.__dict__.setdefault("pool_dma_hist", [])
            if len(hist) >= POOL_DMA_MAX_OUTSTANDING:
                self._dep(o, hist[-POOL_DMA_MAX_OUTSTANDING])
            hist.append(("dma", semreg, 16 * (semreg.dcnt + 1)))
        semreg.dcnt += 1
        o.dreg = semreg
        o.dval = 16 * semreg.dcnt
        self.ops[eng].append(o)
        self._mark(("dma", semreg, o.dval), reads, writes)
        return o

    def resolve(self):
        for e in self.ENGS:
            c = 0
            for o in self.ops[e]:
                if o.inc:
                    c += 1
                    o.count = c
            self.maxcount = getattr(self, "maxcount", {})
            self.maxcount[e] = c

    def emit_engine(self, e, h):
        waited = {}
        for o in self.ops[e]:
            for tok in o.waits:
                if tok[0] == "op":
                    sem, val = self.sems[tok[1].eng], tok[1].count
                else:
                    sem, val = tok[1].dsem, tok[2]
                k = id(sem)
                if waited.get(k, 0) >= val:
                    continue
                waited[k] = val
                h.wait_ge(sem, val)
            if o.fn is None:
                continue
            ins = o.fn(h)
            if o.dreg is not None:
                ins.then_inc(o.dreg.dsem, 16)
            elif o.inc:
                ins.then_inc(self.sems[e], 1)

    def final_wait(self, eng, regs):
        o = Op(eng, None)
        for r in regs:
            self._dep(o, r.w)
            for t in r.r:
                self._dep(o, t)
        self.ops[eng].append(o)
        return o


def build(nseq=SEQ_PER_CORE, debug=None, phases=(1, 2), upto=99):
    from contextlib import ExitStack

    nc = bass.Bass("TRN2", target_bir_lowering=False)
    stack = ExitStack()

    def din(name, shape, dt=F32):
        return nc.dram_tensor(name, list(shape), dt, kind="ExternalInput").ap()

    x = din("x", [nseq, S, D])
    w_in = din("w_in", [D, 7680])
    w_sp = din("w_spatial", [4, 128, 128])
    w_a = din("w_branch_a", [512, D])
    w_b = din("w_branch_b", [512, D])
    w_out = din("w_out", [D, D])
    w_up = din("w_up", [D, 4096])
    w_down = din("w_down", [4096, D])
    c_ident = din("c_ident", [128, 128])
    c_masks = din("c_masks", [128, 640])
    c_cos = din("c_cos", [128, 256])
    c_sin = din("c_sin", [128, 256])
    c_gpre = din("c_gpre", [128, 8])
    c_gpre2 = din("c_gpre2", [128, 8])
    c_lng = din("c_lng", [128, 4])
    c_bgate = din("c_bgate", [128, 16])
    c_lnb = din("c_lnb", [1, 512])
    c_bsp = din("c_bsp", [1, 512])
    c_gpost = din("c_gpost", [128, D])
    c_gpost2 = din("c_gpost2", [128, D])
    out = nc.dram_tensor("out", [nseq, S, D], F32, kind="ExternalOutput").ap()
    dbg = {}
    if debug:
        for name, shape in debug.items():
            dbg[name] = nc.dram_tensor(name, list(shape), F32, kind="ExternalOutput").ap()

    ARENA_F = 53200
    arena = stack.enter_context(nc.sbuf_tensor("arena", [128, ARENA_F], F32))
    psum = [stack.enter_context(nc.psum_tensor("ps%d" % i, [128, 1024], F32)) for i in range(4)]
    P = Prog(nc, stack)

    class Arena:
        def __init__(self):
            self.top = 0
            self.hist = []
            self.peak = 0

        def alloc(self, name, nwords):
            nwords = (nwords + 7) // 8 * 8
            st = self.top
            self.top += nwords
            assert self.top <= ARENA_F, "SBUF arena overflow at %s: %d" % (name, self.top)
            self.peak = max(self.peak, self.top)
            return st

        def reg(self, name, st, nwords):
            en = st + nwords
            al = [r for (a, b, r) in self.hist if a < en and st < b]
            r = P.reg(name, al)
            self.hist.append((st, en, r))
            return r

        def mark(self):
            return self.top

        def release(self, m):
            self.top = m

    A = Arena()

    class Buf:
        def __init__(self, name, nwords, nregs=1):
            self.st = A.alloc(name, nwords)
            self.n = nwords
            self.f = arena[:, self.st:self.st + nwords]
            self.b = arena[:, self.st:self.st + nwords].bitcast(BF16)
            self.reg = A.reg(name, self.st, nwords)

    def dump(name, src_ap, reg, n, isbf=False):
        if not (debug and name in dbg):
            return
        if isbf:
            stg = Buf("stg_" + name, n)
            P.op("dve", lambda e, stg=stg: e.tensor_copy(out=stg.f, in_=src_ap), reads=[reg], writes=[stg.reg])
            P.dma("sp", dbg[name], stg.f, reads=[stg.reg], writes=[], semreg=stg.reg)
            out_regs.append(stg.reg)
        else:
            P.dma("sp", dbg[name], src_ap, reads=[reg], writes=[], semreg=reg)
            out_regs.append(reg)

    out_regs = []
    bank_f = []
    bank_reg = []
    for i in range(8):
        bank_f.append(psum[i // 2][:, (i % 2) * 512:(i % 2) * 512 + 512])
        bank_reg.append(P.reg("bank%d" % i))
        bank_reg[-1].excl = True
    bank_ctr = {}
    ALLB = (0, 1, 2, 3, 4, 5, 6, 7)
    bank_pool = [ALLB]

    def next_bank(pool=None):
        pool = pool or bank_pool[0]
        c = bank_ctr.get(pool, 0)
        bank_ctr[pool] = c + 1
        return pool[c % len(pool)]

    def next_pair(pool=None):
        pool = pool or (0, 2, 4, 6)
        key = ("pair",) + pool
        c = bank_ctr.get(key, 0)
        bank_ctr[key] = c + 1
        return pool[c % len(pool)]

    identF = Buf("identF", 128)
    gpre = Buf("gpre", 8)
    gpre2 = Buf("gpre2", 8)
    lng = Buf("lng", 8)
    hbg = Buf("hbg", 16)
    gpost2 = Buf("gpost2", 1024)
    neghalf = Buf("neghalf", 8)
    epsT = Buf("epsT", 8)
    gpost = Buf("gpost", 1024)
    mP1 = A.mark()
    identB = Buf("identB", 64)
    masks = Buf("masks", 320)
    onesB = Buf("onesB", 64)
    cosT = Buf("cosT", 256)
    sinT = Buf("sinT", 256)
    Cg = Buf("Cg", 512)
    WsT = Buf("WsT", 256)
    constR = P.reg("constdma")

    def sq(eng):
        return eng

    P.dma("sp", identF.f, c_ident, writes=[identF.reg], semreg=identF.reg)
    P.dma("sp", cosT.f, c_cos, writes=[cosT.reg], semreg=cosT.reg)
    P.dma("sp", sinT.f, c_sin, writes=[sinT.reg], semreg=sinT.reg)
    P.dma("sp", gpre.f, c_gpre, writes=[gpre.reg], semreg=gpre.reg)
    P.dma("sp", gpre2.f, c_gpre2, writes=[gpre2.reg], semreg=gpre2.reg)
    P.dma("sp", lng.f[:, 0:4], c_lng, writes=[lng.reg], semreg=lng.reg)
    P.dma("sp", hbg.f, c_bgate, writes=[hbg.reg], semreg=hbg.reg)
    P.dma("sp", gpost.f, c_gpost, writes=[gpost.reg], semreg=gpost.reg)
    P.dma("sp", gpost2.f, c_gpost2, writes=[gpost2.reg], semreg=gpost2.reg)
    P.dma("pool", masks.b, c_masks, writes=[masks.reg], semreg=masks.reg)

    P.op("dve", lambda e: e.tensor_scalar(out=hbg.f, in0=hbg.f, scalar1=0.5, scalar2=None, op0=ALU.mult),
         reads=[hbg.reg], writes=[hbg.reg])
    P.op("dve", lambda e: e.memset(onesB.b, 1.0), writes=[onesB.reg])
    P.op("dve", lambda e: e.memset(neghalf.f, -0.5), writes=[neghalf.reg])
    P.op("dve", lambda e: e.memset(epsT.f[:, 0:1], EPS), writes=[epsT.reg])
    P.op("dve", lambda e: e.memset(epsT.f[:, 1:2], 4.0 * EPS), writes=[epsT.reg])
    P.op("dve", lambda e: e.tensor_copy(out=identB.b, in_=identF.f), reads=[identF.reg], writes=[identB.reg])

    def rstd_op(dst, src, n, mul, reads, writes, post=1.0, mode="pow"):
        if mode == "pow":
            P.op("dve", lambda e: e.tensor_scalar(out=dst, in0=src, scalar1=mul, scalar2=EPS, op0=ALU.mult, op1=ALU.add),
                 reads=reads, writes=writes)
            P.op("pool", lambda e: e.tensor_tensor(out=dst, in0=dst, in1=neghalf.f[:, 0:n], op=ALU.pow),
                 reads=writes + [neghalf.reg], writes=writes)
            if post != 1.0:
                P.op("dve", lambda e: e.tensor_scalar(out=dst, in0=dst, scalar1=post, scalar2=None, op0=ALU.mult),
                     reads=writes, writes=writes)
        else:
            assert post in (1.0, 0.5)
            bcol = 0 if post == 1.0 else 1
            P.op("act", lambda e: e.activation(out=dst, in_=src, func=AF.Sqrt, scale=mul / (post * post),
                                               bias=epsT.f[:, bcol:bcol + 1]),
                 reads=reads + [epsT.reg], writes=writes)
            P.op("dve", lambda e: e.reciprocal(out=dst, in_=dst), reads=writes, writes=writes)

    m0 = A.mark()
    wsp = Buf("wsp", 512)
    wsTf = Buf("wsTf", 512)
    rows = Buf("rows", 512 * 3 + 128)
    onesF = Buf("onesF", 8)
    P.dma("sp", wsp.f.rearrange("p (g s) -> p g s", g=4), w_sp.rearrange("g t s -> t g s"),
          writes=[wsp.reg], semreg=wsp.reg)
    P.dma("sp", rows.f[0:1, 512:1024], c_lnb, writes=[rows.reg], semreg=rows.reg)
    P.dma("sp", rows.f[0:1, 1024:1536], c_bsp, writes=[rows.reg], semreg=rows.reg)
    P.op("dve", lambda e: e.memset(rows.f[0:1, 1536:1664], 1.0), reads=[], writes=[rows.reg])
    P.op("dve", lambda e: e.memset(onesF.f, 1.0), writes=[onesF.reg])
    bk = next_bank()

    def _tr_ws(e, bk=bk):
        ins = None
        for g in range(4):
            ins = e.transpose(bank_f[bk][:, g * 128:(g + 1) * 128], wsp.f[:, g * 128:(g + 1) * 128], identF.f)
        return ins
    P.op("pe", _tr_ws, reads=[wsp.reg, identF.reg], writes=[bank_reg[bk]])
    P.op("act", lambda e, bk=bk: e.copy(out=wsTf.f, in_=bank_f[bk]), reads=[bank_reg[bk]], writes=[wsTf.reg])
    P.op("dve", lambda e, bk=bk: e.tensor_copy(out=WsT.b, in_=bank_f[bk]), reads=[bank_reg[bk]], writes=[WsT.reg])
    bk = next_bank()

    def _rs(e, bk=bk):
        ins = None
        for g in range(4):
            ins = e.matmul(bank_f[bk][0:1, g * 128:(g + 1) * 128], onesF.f[:, 0:1], wsTf.f[:, g * 128:(g + 1) * 128],
                           start=True, stop=True)
        return ins
    P.op("pe", _rs, reads=[wsTf.reg, onesF.reg], writes=[bank_reg[bk]])
    P.op("act", lambda e, bk=bk: e.copy(out=rows.f[0:1, 0:512], in_=bank_f[bk][0:1, :]),
         reads=[bank_reg[bk]], writes=[rows.reg])
    bk = next_bank()

    def _cg(e, bk=bk):
        ins = None
        for g in range(4):
            sl = slice(g * 128, (g + 1) * 128)
            e.matmul(bank_f[bk][:, sl], rows.f[0:1, 512 + g * 128:512 + (g + 1) * 128], rows.f[0:1, sl],
                     start=True, stop=False)
            ins = e.matmul(bank_f[bk][:, sl], rows.f[0:1, 1536:1664], rows.f[0:1, 1024 + g * 128:1024 + (g + 1) * 128],
                           start=False, stop=True)
        return ins
    P.op("pe", _cg, reads=[rows.reg], writes=[bank_reg[bk]])
    P.op("act", lambda e, bk=bk: e.copy(out=Cg.f, in_=bank_f[bk]), reads=[bank_reg[bk]], writes=[Cg.reg])
    dump("dbg_Cg", Cg.f, Cg.reg, 512)
    dump("dbg_wsTf", wsTf.f, wsTf.reg, 512)
    dump("dbg_rows", rows.f[0:1, :], rows.reg, 1664)
    dump("dbg_WsT", WsT.b, WsT.reg, 512, isbf=True)
    A.release(m0)

    def wload(buf_ap, src_ap, reg, reads=()):
        P.dma("pool", buf_ap, src_ap, reads=list(reads), writes=[reg], semreg=reg)

    nT = Buf("nT", KC * SP // 2)
    nTv = nT.b.rearrange("p (k t) -> p k t", k=KC)
    yaT = Buf("yaT", 4 * S // 2)
    yaTv = yaT.b.rearrange("p (k t) -> p k t", k=4)
    ybT = Buf("ybT", 4 * S // 2)
    ybTv = ybT.b.rearrange("p (k t) -> p k t", k=4)
    junk = Buf("junk", D // 2)
    Wu = Buf("Wu", KC * 512 // 2)
    Wuv = Wu.b.rearrange("p (k n) -> p k n", k=KC)
    Wv = Buf("Wv", KC * 512 // 2)
    Wvv = Wv.b.rearrange("p (k n) -> p k n", k=KC)
    ssb = [Buf("ss%d" % i, 8) for i in range(2)]
    P.op("pool", lambda e: e.memset(nTv[:, :, 0:PAD], 0.0), writes=[nT.reg])
    P.op("pool", lambda e: e.memset(nTv[:, :, PAD + S:SP], 0.0), writes=[nT.reg])
    xq_f = [yaT.f, ybT.f]
    xq_reg = [yaT.reg, ybT.reg]

    def stage1_dma(s, q):
        xv = xq_f[q % 2].rearrange("p (t d) -> p t d", t=4)
        P.dma("sp", xv, x[s, q * 512:(q + 1) * 512, :].rearrange("(t p) d -> p t d", p=128),
              writes=[xq_reg[q % 2]], semreg=xq_reg[q % 2])

    def stage1_front(s, q):
        xf = xq_f[q % 2]
        xreg = xq_reg[q % 2]
        ss = ssb[q % 2]
        xv = xf.rearrange("p (t d) -> p t d", t=4)
        for tt in range(4):
            P.op("act", lambda e, xv=xv, tt=tt, ss=ss: e.activation(out=junk.b, in_=xv[:, tt, :], func=AF.Square,
                                                                     accum_out=ss.f[:, tt:tt + 1]),
                 reads=[xreg], writes=[junk.reg, ss.reg])
        rstd_op(ss.f[:, 0:4], ss.f[:, 0:4], 4, 1.0 / D, [ss.reg], [ss.reg], mode="sqrt")
        for tt in range(4):
            P.op("dve", lambda e, xv=xv, tt=tt, ss=ss: e.tensor_scalar(out=xv[:, tt, :], in0=xv[:, tt, :],
                                                                        scalar1=ss.f[:, tt:tt + 1], scalar2=None,
                                                                        op0=ALU.mult),
                 reads=[xreg, ss.reg], writes=[xreg])

    def stage1_back(s, q):
        xf = xq_f[q % 2]
        xreg = xq_reg[q % 2]
        xv = xf.rearrange("p (t d) -> p t d", t=4)
        for kc in range(KC):
            bk = next_bank((4, 5, 6, 7))

            def _tr(e, bk=bk, xv=xv, kc=kc):
                ins = None
                for tt in range(4):
                    ins = e.transpose(bank_f[bk][:, tt * 128:(tt + 1) * 128], xv[:, tt, kc * 128:(kc + 1) * 128],
                                      identF.f)
                return ins
            P.op("pe", _tr, reads=[xreg, identF.reg], writes=[bank_reg[bk]])
            dst = nTv[:, kc, PAD + q * 512:PAD + (q + 1) * 512]
            if kc % 2 == 0:
                P.op("act", lambda e, bk=bk, dst=dst, kc=kc: e.activation(out=dst, in_=bank_f[bk], func=AF.Identity,
                                                                           scale=gpre.f[:, kc:kc + 1]),
                     reads=[bank_reg[bk], gpre.reg], writes=[nT.reg])
            else:
                P.op("dve", lambda e, bk=bk, dst=dst, kc=kc: e.tensor_scalar(out=dst, in0=bank_f[bk],
                                                                              scalar1=gpre.f[:, kc:kc + 1],
                                                                              scalar2=None, op0=ALU.mult),
                     reads=[bank_reg[bk], gpre.reg], writes=[nT.reg])

    def stage1_quad(s, q, after_dma=None, dma=True):
        if dma:
            stage1_dma(s, q)
        if after_dma is not None:
            after_dma(xq_reg[q % 2])
        stage1_front(s, q)
        stage1_back(s, q)

    def phase1(s):
        mS = A.mark()

        mQ = A.mark()
        Wqk = [Buf("Wqk%d" % i, KC * 768 // 2) for i in range(2)]
        Wv3 = [Buf("Wv3_%d" % i, KC * 384 // 2) for i in range(2)]
        mW = A.mark()

        def slot_w_thunks(j):
            b = j % 2
            th = []
            src = w_in[:, 1024:4096].rearrange("(k p) (m c) -> p k m c", p=128, c=512)
            dstw = Wqk[b].b.rearrange("p (k m c) -> p k m c", k=KC, m=6)
            for m_ in range(6):
                th.append(lambda m_=m_, src=src, dstw=dstw: wload(dstw[:, :, m_, :], src[:, :, m_, j * 128:(j + 1) * 128],
                                                                  Wqk[b].reg))
            src2 = w_in[:, 4096:5632].rearrange("(k p) (m c) -> p k m c", p=128, c=512)
            dstw2 = Wv3[b].b.rearrange("p (k m c) -> p k m c", k=KC, m=3)
            for m_ in range(3):
                th.append(lambda m_=m_, src2=src2, dstw2=dstw2: wload(dstw2[:, :, m_, :],
                                                                      src2[:, :, m_, j * 128:(j + 1) * 128], Wv3[b].reg))
            return th

        def _wuv(xreg):
            wload(Wuv, w_in[:, 0:512].rearrange("(k p) n -> p k n", p=128), Wu.reg, reads=[xreg])
            wload(Wvv, w_in[:, 512:1024].rearrange("(k p) n -> p k n", p=128), Wv.reg)
        if s == 0:
            for q in range(4):
                stage1_quad(0, q, after_dma=_wuv if q == 0 else None)
        m1 = A.mark()
        if debug and "dbg_nT" in dbg and s == 0:
            stg = Buf("dbgstg", KC * SP)
            P.op("dve", lambda e, stg=stg: e.tensor_copy(out=stg.f, in_=nT.b), reads=[nT.reg], writes=[stg.reg])
            P.dma("sp", dbg["dbg_nT"], stg.f, reads=[stg.reg], writes=[], semreg=stg.reg)
            out_regs.append(stg.reg)
            A.release(m1)

        if upto < 2:
            return
        m2 = A.mark()
        uT = Buf("uT", 4 * S // 2)
        uTv = uT.b.rearrange("p (k t) -> p k t", k=4)
        NB2 = 4
        vg = [Buf("vg%d" % i, 512) for i in range(NB2)]
        vn = [Buf("vn%d" % i, 256) for i in range(NB2)]
        tmpg = [Buf("tmpg%d" % i, 512) for i in range(NB2)]
        st6 = [Buf("st6_%d" % i, 8) for i in range(NB2)]
        mv = [Buf("mv%d" % i, 8) for i in range(NB2)]

        def u_group(q, c):
            bk = next_bank((3, 4, 5))

            def _mm(e, bk=bk, q=q, c=c):
                ins = None
                for kc in range(KC):
                    ins = e.matmul(bank_f[bk], Wuv[:, kc, c * 128:(c + 1) * 128],
                                   nTv[:, kc, PAD + q * 512:PAD + (q + 1) * 512], start=(kc == 0), stop=(kc == KC - 1))
                return ins
            P.op("pe", _mm, reads=[Wu.reg, nT.reg], writes=[bank_reg[bk]])
            P.op("act", lambda e, bk=bk, q=q, c=c: e.activation(out=uTv[:, c, q * 512:(q + 1) * 512], in_=bank_f[bk],
                                                                 func=AF.Gelu_apprx_tanh),
                 reads=[bank_reg[bk]], writes=[uT.reg])

        def v_front(t):
            bk = next_bank((0, 1, 2))
            i2 = t % NB2

            def _mm(e, bk=bk, t=t):
                ins = None
                for kc in range(KC):
                    ins = e.matmul(bank_f[bk], nTv[:, kc, PAD + t * 128:PAD + (t + 1) * 128], Wvv[:, kc, :],
                                   start=(kc == 0), stop=(kc == KC - 1))
                return ins
            P.op("pe", _mm, reads=[Wv.reg, nT.reg], writes=[bank_reg[bk]])
            P.op("act", lambda e, bk=bk, i2=i2: e.activation(out=vg[i2].f, in_=bank_f[bk], func=AF.Gelu_apprx_tanh),
                 reads=[bank_reg[bk]], writes=[vg[i2].reg])
            P.op("dve", lambda e, i2=i2: e.bn_stats(out=st6[i2].f[:, 0:6], in_=vg[i2].f), reads=[vg[i2].reg],
                 writes=[st6[i2].reg])
            P.op("dve", lambda e, i2=i2: e.bn_aggr(out=mv[i2].f[:, 0:2], in_=st6[i2].f[:, 0:6]), reads=[st6[i2].reg],
                 writes=[mv[i2].reg])
            rstd_op(mv[i2].f[:, 1:2], mv[i2].f[:, 1:2], 1, 1.0, [mv[i2].reg], [mv[i2].reg])
            P.op("dve", lambda e, i2=i2: e.scalar_tensor_tensor(out=mv[i2].f[:, 2:3], in0=mv[i2].f[:, 0:1], scalar=-1.0,
                                                                 in1=mv[i2].f[:, 1:2], op0=ALU.mult, op1=ALU.mult),
                 reads=[mv[i2].reg], writes=[mv[i2].reg])
            P.op("act", lambda e, i2=i2: e.activation(out=vn[i2].b, in_=vg[i2].f, func=AF.Identity,
                                                       scale=mv[i2].f[:, 1:2], bias=mv[i2].f[:, 2:3]),
                 reads=[vg[i2].reg, mv[i2].reg], writes=[vn[i2].reg])

        def v_back(t):
            i2 = t % NB2
            bk2 = next_bank((6, 7))

            def _sp(e, bk2=bk2, i2=i2):
                ins = None
                for g in range(4):
                    sl = slice(g * 128, (g + 1) * 128)
                    ins = e.matmul(bank_f[bk2][:, sl], vn[i2].b[:, sl], WsT.b[:, sl], start=True, stop=True)
                return ins
            P.op("pe", _sp, reads=[vn[i2].reg, WsT.reg], writes=[bank_reg[bk2]])
            for g in range(4):
                sl = slice(g * 128, (g + 1) * 128)
                P.op("dve", lambda e, bk2=bk2, i2=i2, g=g, sl=sl: e.scalar_tensor_tensor(
                    out=tmpg[i2].f[:, sl], in0=bank_f[bk2][:, sl], scalar=lng.f[:, g:g + 1], in1=Cg.f[:, sl],
                    op0=ALU.mult, op1=ALU.add),
                    reads=[bank_reg[bk2], lng.reg, Cg.reg], writes=[tmpg[i2].reg])
            P.op("pool", lambda e, i2=i2, t=t: e.tensor_tensor(
                out=yaTv[:, :, t * 128:(t + 1) * 128], in0=tmpg[i2].f.rearrange("p (g t) -> p g t", g=4),
                in1=uTv[:, :, t * 128:(t + 1) * 128], op=ALU.mult),
                reads=[tmpg[i2].reg, uT.reg], writes=[yaT.reg])

        SK2 = 3
        sw0 = slot_w_thunks(0)
        for t in range(16 + SK2):
            if t < len(sw0):
                sw0[t]()
            if t < 16:
                if t % 4 == 0:
                    for c in range(4):
                        u_group(t // 4, c)
                v_front(t)
            if t >= SK2:
                v_back(t - SK2)
        dump("dbg_uT", uT.b, uT.reg, 4 * S, isbf=True)
        A.release(m2)
        if debug and "dbg_yaT" in dbg and s == 0:
            stg = Buf("dbgstg2", 4 * S)
            P.op("dve", lambda e, stg=stg: e.tensor_copy(out=stg.f, in_=yaT.b), reads=[yaT.reg], writes=[stg.reg])
            P.dma("sp", dbg["dbg_yaT"], stg.f, reads=[stg.reg], writes=[], semreg=stg.reg)
            out_regs.append(stg.reg)
            A.release(m2)

        if upto < 2.05:
            return
        A.release(mW)
        m3 = A.mark()
        qks = [Buf("qks%d" % i, 768 // 2) for i in range(3)]
        rts = [[Buf("rt%d_%d" % (i, k), 96) for k in range(4)] for i in range(2)]
        qT = Buf("qT", 3 * S // 2)
        qTv = qT.b.rearrange("p (g t) -> p g t", g=3)
        kT = Buf("kT", 3 * SP // 2)
        kTv = kT.b.rearrange("p (g t) -> p g t", g=3)
        VB0 = (0, 17, 37)
        Vt = Buf("Vt", 53 * 128 // 2)
        Vtv = Vt.b.rearrange("p (b d) -> p b d", b=53)
        NPT = 5
        LA = 3
        PT = [Buf("PT%d" % i, 128) for i in range(NPT)]
        accU = Buf("accU", S)
        accD = Buf("accD", S)
        P.op("pool", lambda e: e.memset(kTv[:, :, 0:PAD], 0.0), writes=[kT.reg])
        P.op("pool", lambda e: e.memset(kTv[:, :, PAD + S:SP], 0.0), writes=[kT.reg])

        blocks = []
        for b in range(17):
            blocks.append((0, slice(PAD + 128 * b - 64, PAD + 128 * b + 64)))
        for r in range(4):
            for b in range(5):
                st = PAD + 4 * (128 * b - 64) + r
                blocks.append((1, slice(st, st + 4 * 127 + 1, 4)))
        for r in range(16):
            blocks.append((2, slice(PAD + r, PAD + r + 16 * 127 + 1, 16)))
        vgroups = []
        vb = 0
        while vb < 53:
            g = blocks[vb][0]
            nb = 0
            while vb + nb < 53 and nb < 4 and blocks[vb + nb][0] == g:
                nb += 1
            vgroups.append((vb, nb, g))
            vb += nb

        mAB = masks.b[:, 0:256]
        mA2 = masks.b[:, 256:384]
        mB2 = masks.b[:, 384:512]
        mC = masks.b[:, 512:640]
        tasks = []
        for g in range(3):
            d = (1, 4, 16)[g]
            L = S // d
            if g < 2:
                nqb = L // 128
                for r in range(d):
                    for b in range(nqb + 1):
                        ks = PAD + d * (128 * b - 64) + r
                        ksl = slice(ks, ks + d * 127 + 1, d) if d > 1 else slice(ks, ks + 128)
                        qlo = max(b - 1, 0)
                        qhi = min(b, nqb - 1)
                        mk = mB2 if b == 0 else (mA2 if b == nqb else mAB)
                        tasks.append(dict(g=g, r=r, b=b, nqb=nqb, L=L, ksl=ksl,
                                          qsl=slice(r * L + 128 * qlo, r * L + 128 * (qhi + 1)),
                                          nq=(qhi - qlo + 1) * 128, mk=mk, vbi=VB0[g] + r * (nqb + 1) + b))
            else:
                for r in range(16):
                    tasks.append(dict(g=2, r=r, b=0, nqb=1, L=128, ksl=slice(PAD + r, PAD + r + 16 * 127 + 1, 16),
                                      qsl=slice(r * 128, (r + 1) * 128), nq=128, mk=mC, vbi=VB0[2] + r))

        pt_ctr = [0]

        def finalize_chunk(j, c4):
            sl = slice(c4 * 512, (c4 + 1) * 512)
            P.op("dve", lambda e, sl=sl: e.reciprocal(out=accD.f[:, sl], in_=accD.f[:, sl]), reads=[accD.reg],
                 writes=[accD.reg])
            P.op("dve", lambda e, j=j, sl=sl: e.tensor_tensor(out=ybTv[:, j, sl], in0=accU.f[:, sl], in1=accD.f[:, sl],
                                                             op=ALU.mult),
                 reads=[accU.reg, accD.reg], writes=[ybT.reg])

        for j in range(4):
            swn = slot_w_thunks(j + 1) if j + 1 < 4 else []
            Wq = Wqk[j % 2]
            Wqv = Wq.b.rearrange("p (k n) -> p k n", k=KC)
            Wvj = Wv3[j % 2]
            Wvjv = Wvj.b.rearrange("p (k n) -> p k n", k=KC)

            def qk_front(t, Wqv=Wqv, Wq=Wq):
                pr = next_pair((0, 2))
                qs = qks[t % 3]
                qsv = qs.b.rearrange("p (m d) -> p m d", m=6)
                rt = rts[t % 2]

                def _mm(e, pr=pr, t=t, Wqv=Wqv):
                    ins = None
                    for h in range(2):
                        for kc in range(KC):
                            ins = e.matmul(bank_f[pr + h][:, 0:384], nTv[:, kc, PAD + t * 128:PAD + (t + 1) * 128],
                                           Wqv[:, kc, h * 384:(h + 1) * 384], start=(kc == 0), stop=(kc == KC - 1))
                    return ins
                P.op("pe", _mm, reads=[Wq.reg, nT.reg], writes=[bank_reg[pr], bank_reg[pr + 1]])
                pv = psum[pr // 2].rearrange("p (h n) -> p h n", h=2)[:, :, 0:384].rearrange("p h (m d) -> p h m d", m=3)
                breg = [bank_reg[pr], bank_reg[pr + 1]]
                cosb = cosT.f[:, t * 16:(t + 1) * 16].unsqueeze(1).unsqueeze(1).broadcast_to([128, 2, 3, 16])
                sinb = sinT.f[:, t * 16:(t + 1) * 16].unsqueeze(1).unsqueeze(1).broadcast_to([128, 2, 3, 16])
                x1 = pv[:, :, :, 0:16]
                x2 = pv[:, :, :, 16:32]
                qs4 = qs.b.rearrange("p (h m d) -> p h m d", h=2, m=3)

                def v4(bf):
                    return bf.f.rearrange("p (h m d) -> p h m d", h=2, m=3)
                for k_, (xa, tb_) in enumerate(((x1, cosb), (x2, sinb), (x2, cosb), (x1, sinb))):
                    P.op("dve", lambda e, xa=xa, tb_=tb_, o_=rt[k_]: e.tensor_tensor(out=v4(o_), in0=xa, in1=tb_, op=ALU.mult),
                         reads=breg + [cosT.reg, sinT.reg], writes=[rt[k_].reg])
                for h in range(2):
                    P.op("act", lambda e, pv=pv, h=h, qsv=qsv: e.copy(out=qsv[:, h * 3:(h + 1) * 3, 32:128],
                                                                    in_=pv[:, h, :, 32:128]),
                         reads=[breg[h]], writes=[qs.reg])
                P.op("pool", lambda e, qs4=qs4, rt=rt: e.tensor_tensor(out=qs4[:, :, :, 0:16], in0=v4(rt[0]), in1=v4(rt[1]),
                                                                        op=ALU.subtract),
                     reads=[rt[0].reg, rt[1].reg], writes=[qs.reg])
                P.op("pool", lambda e, qs4=qs4, rt=rt: e.tensor_tensor(out=qs4[:, :, :, 16:32], in0=v4(rt[2]), in1=v4(rt[3]),
                                                                        op=ALU.add),
                     reads=[rt[2].reg, rt[3].reg], writes=[qs.reg])

            def qk_back(t):
                qs = qks[t % 3]
                qsv = qs.b.rearrange("p (m d) -> p m d", m=6)
                bt = next_bank((4, 5))
                btb = bank_f[bt].bitcast(BF16)

                def _tr(e, btb=btb, qsv=qsv):
                    ins = None
                    for m in range(6):
                        ins = e.transpose(btb[:, m * 128:(m + 1) * 128], qsv[:, m, :], identB.b)
                    return ins
                P.op("pe", _tr, reads=[qs.reg, identB.reg], writes=[bank_reg[bt]])
                btv = btb[:, 0:768].rearrange("p (m t) -> p m t", m=6)
                P.op("act", lambda e, btv=btv, t=t: e.copy(out=kTv[:, :, PAD + t * 128:PAD + (t + 1) * 128],
                                                          in_=btv[:, 3:6, :]),
                     reads=[bank_reg[bt]], writes=[kT.reg])
                P.op("act", lambda e, btv=btv, t=t: e.copy(
                    out=qTv[:, 2, :].rearrange("p (r i) -> p r i", r=16)[:, :, 8 * t:8 * t + 8],
                    in_=btv[:, 2, :].rearrange("p (i r) -> p r i", r=16)),
                    reads=[bank_reg[bt]], writes=[qT.reg])
                P.op("dve", lambda e, btv=btv, t=t: e.tensor_copy(out=qTv[:, 0, t * 128:(t + 1) * 128], in_=btv[:, 0, :]),
                     reads=[bank_reg[bt]], writes=[qT.reg])
                P.op("dve", lambda e, btv=btv, t=t: e.tensor_copy(
                    out=qTv[:, 1, :].rearrange("p (r i) -> p r i", r=4)[:, :, 32 * t:32 * t + 32],
                    in_=btv[:, 1, :].rearrange("p (i r) -> p r i", r=4)),
                    reads=[bank_reg[bt]], writes=[qT.reg])

            def v_group(gi, Wvjv=Wvjv, Wvj=Wvj):
                vb, nb, g = vgroups[gi]
                bk = next_bank((6, 7))

                def _mm(e, bk=bk, vb=vb, nb=nb, g=g, Wvjv=Wvjv):
                    ins = None
                    for i in range(nb):
                        tok = blocks[vb + i][1]
                        for kc in range(KC):
                            ins = e.matmul(bank_f[bk][:, i * 128:(i + 1) * 128], nTv[:, kc, tok],
                                           Wvjv[:, kc, g * 128:(g + 1) * 128], start=(kc == 0), stop=(kc == KC - 1))
                    return ins
                P.op("pe", _mm, reads=[nT.reg, Wvj.reg], writes=[bank_reg[bk]])
                dstv = Vt.b[:, vb * 128:(vb + nb) * 128]
                if gi % 2 == 0:
                    P.op("act", lambda e, bk=bk, dstv=dstv, nb=nb: e.copy(out=dstv, in_=bank_f[bk][:, 0:nb * 128]),
                         reads=[bank_reg[bk]], writes=[Vt.reg])
                else:
                    P.op("dve", lambda e, bk=bk, dstv=dstv, nb=nb: e.tensor_copy(out=dstv, in_=bank_f[bk][:, 0:nb * 128]),
                         reads=[bank_reg[bk]], writes=[Vt.reg])

            if upto < 2.15:
                return
            vgi = 0
            QSK = 2
            for t in range(16 + QSK):
                if t < len(swn):
                    swn[t]()
                if j >= 1 and t in (3, 6, 9, 12):
                    finalize_chunk(j - 1, (t - 3) // 3)
                if t < 16:
                    qk_front(t)
                if vgi < len(vgroups):
                    v_group(vgi)
                    vgi += 1
                if t >= QSK:
                    qk_back(t - QSK)
            while vgi < len(vgroups):
                v_group(vgi)
                vgi += 1
            if upto < 2.25:
                return

            def evac_OD(g, pq, bo, bd):
                for (bkx, acc) in ((bo, accU), (bd, accD)):
                    if g == 0:
                        dst = acc.f[:, pq * 512:(pq + 1) * 512]
                        P.op("act", lambda e, bkx=bkx, dst=dst: e.copy(out=dst, in_=bank_f[bkx]),
                             reads=[bank_reg[bkx]], writes=[acc.reg])
                    elif g == 1:
                        dst = acc.f.rearrange("p (i r) -> p r i", r=4)[:, pq, :]
                        P.op("dve", lambda e, bkx=bkx, dst=dst: e.tensor_tensor(out=dst, in0=bank_f[bkx], in1=dst,
                                                                                 op=ALU.add),
                             reads=[bank_reg[bkx], acc.reg], writes=[acc.reg])
                    else:
                        dst = acc.f.rearrange("p (i r) -> p r i", r=16)[:, 4 * pq:4 * pq + 4, :]
                        src = bank_f[bkx].rearrange("p (r i) -> p r i", r=4)
                        P.op("dve", lambda e, src=src, dst=dst: e.tensor_tensor(out=dst, in0=src, in1=dst, op=ALU.add),
                             reads=[bank_reg[bkx], acc.reg], writes=[acc.reg])

            def att_front(T):
                bs = next_bank((0, 1, 2, 3))
                pt = PT[pt_ctr[0] % NPT]
                pt_ctr[0] += 1
                T["pt"] = pt
                nq = T["nq"]
                P.op("pe", lambda e, bs=bs, T=T, nq=nq: e.matmul(bank_f[bs][:, 0:nq], kTv[:, T["g"], T["ksl"]],
                                                               qTv[:, T["g"], T["qsl"]], start=True, stop=True),
                     reads=[kT.reg, qT.reg], writes=[bank_reg[bs]])
                P.op("act", lambda e, bs=bs, pt=pt, nq=nq: e.activation(out=pt.b[:, 0:nq], in_=bank_f[bs][:, 0:nq],
                                                                       func=AF.Exp, scale=SCALE),
                     reads=[bank_reg[bs]], writes=[pt.reg])
                P.op("pool", lambda e, pt=pt, nq=nq, mk=T["mk"]: e.tensor_tensor(out=pt.b[:, 0:nq], in0=pt.b[:, 0:nq],
                                                                                in1=mk, op=ALU.mult),
                     reads=[pt.reg, masks.reg], writes=[pt.reg])

            ost = {"bo": None, "bd": None}

            def att_back(T):
                g, r, b, nqb, L, vbi, pt = T["g"], T["r"], T["b"], T["nqb"], T["L"], T["vbi"], T["pt"]
                if g == 2:
                    if r % 4 == 0:
                        ost["bo"] = next_pair((4, 6))
                        ost["bd"] = ost["bo"] + 1
                    bo, bd = ost["bo"], ost["bd"]
                    cs = slice((r % 4) * 128, (r % 4) * 128 + 128)

                    def _pv(e, bo=bo, bd=bd, cs=cs, vbi=vbi, pt=pt):
                        e.matmul(bank_f[bo][:, cs], Vtv[:, vbi, :], pt.b[:, 0:128], start=True, stop=True)
                        return e.matmul(bank_f[bd][:, cs], onesB.b, pt.b[:, 0:128], start=True, stop=True)
                    P.op("pe", _pv, reads=[Vt.reg, pt.reg, onesB.reg], writes=[bank_reg[bo], bank_reg[bd]])
                    if r % 4 == 3:
                        evac_OD(2, r // 4, bo, bd)
                    return
                col = 0
                if b >= 1:
                    qb = b - 1
                    bo, bd = ost["bo"], ost["bd"]
                    cs = slice((qb % 4) * 128, (qb % 4) * 128 + 128)

                    def _fin(e, bo=bo, bd=bd, cs=cs, vbi=vbi, pt=pt):
                        e.matmul(bank_f[bo][:, cs], Vtv[:, vbi, :], pt.b[:, 0:128], start=False, stop=True)
                        return e.matmul(bank_f[bd][:, cs], onesB.b, pt.b[:, 0:128], start=False, stop=True)
                    P.op("pe", _fin, reads=[Vt.reg, pt.reg, onesB.reg], writes=[bank_reg[bo], bank_reg[bd]])
                    col = 128
                    if qb % 4 == 3 or qb == nqb - 1:
                        evac_OD(g, (r * L + 128 * qb) // 512, bo, bd)
                if b <= nqb - 1:
                    qb = b
                    if qb % 4 == 0:
                        ost["bo"] = next_pair((4, 6))
                        ost["bd"] = ost["bo"] + 1
                    bo, bd = ost["bo"], ost["bd"]
                    cs = slice((qb % 4) * 128, (qb % 4) * 128 + 128)

                    def _sta(e, bo=bo, bd=bd, cs=cs, vbi=vbi, pt=pt, col=col):
                        e.matmul(bank_f[bo][:, cs], Vtv[:, vbi, :], pt.b[:, col:col + 128], start=True, stop=False)
                        return e.matmul(bank_f[bd][:, cs], onesB.b, pt.b[:, col:col + 128], start=True, stop=False)
                    P.op("pe", _sta, reads=[Vt.reg, pt.reg, onesB.reg], writes=[bank_reg[bo], bank_reg[bd]])

            ntk = len(tasks) if upto >= 2.5 else (18 if upto < 2.35 else 38)
            tl = [dict(T) for T in tasks[:ntk]]
            for i in range(ntk + LA):
                if i < ntk:
                    att_front(tl[i])
                if i >= LA:
                    att_back(tl[i - LA])
            if upto < 2.65:
                return
            if j == 3:
                for c4 in range(4):
                    finalize_chunk(3, c4)
        A.release(m3)
        if debug and "dbg_ybT" in dbg and s == 0:
            stg = Buf("dbgstg3", 4 * S)
            P.op("dve", lambda e, stg=stg: e.tensor_copy(out=stg.f, in_=ybT.b), reads=[ybT.reg], writes=[stg.reg])
            P.dma("sp", dbg["dbg_ybT"], stg.f, reads=[stg.reg], writes=[], semreg=stg.reg)
            out_regs.append(stg.reg)
            A.release(m3)

        if upto < 4:
            return
        A.release(mQ)
        m4 = A.mark()
        mT = Buf("mT", KC * S // 2)
        mTv = mT.b.rearrange("p (k t) -> p k t", k=KC)
        Wo = Buf("Wo", KC * D // 2)
        Wov = Wo.b.rearrange("p (k n) -> p k n", k=KC)
        Wg = [Buf("Wg%d" % i, KC * 256 // 2) for i in range(2)]
        Wab = [Buf("Wab%d" % i, 2 * 4 * 128 // 2) for i in range(2)]
        def load_c(c):
            b = c % 2
            src = w_in[:, 5632:7680].rearrange("(k p) (m n) -> p k m n", p=128, n=1024)
            dstw = Wg[b].b.rearrange("p (k m n) -> p k m n", k=KC, m=2)
            for m_ in range(2):
                wload(dstw[:, :, m_, :], src[:, :, m_, c * 128:(c + 1) * 128], Wg[b].reg)
            wab = Wab[b].b.rearrange("p (w k n) -> p w k n", w=2, k=4)
            wload(wab[:, 0, :, :], w_a[:, c * 128:(c + 1) * 128].rearrange("(k p) n -> p k n", p=128), Wab[b].reg)
            wload(wab[:, 1, :, :], w_b[:, c * 128:(c + 1) * 128].rearrange("(k p) n -> p k n", p=128), Wab[b].reg)

        load_c(0)
        ta = [Buf("ta%d" % i, 512) for i in range(2)]
        tb = [Buf("tb%d" % i, 512) for i in range(2)]
        m1b = [Buf("m1b%d" % i, 512) for i in range(2)]
        m2b = [Buf("m2b%d" % i, 512) for i in range(2)]

        it = 0
        for c in range(8):
            if c + 1 < 8:
                load_c(c + 1)
            if c == 1:
                wload(Wov[:, 0:4, :], w_out[0:512, :].rearrange("(k p) n -> p k n", p=128), Wo.reg)
            if c == 2:
                wload(Wov[:, 4:8, :], w_out[512:1024, :].rearrange("(k p) n -> p k n", p=128), Wo.reg)
            wg = Wg[c % 2]
            wgv = wg.b.rearrange("p (k m n) -> p k m n", k=KC, m=2)
            wab = Wab[c % 2]
            wabv = wab.b.rearrange("p (w k n) -> p w k n", w=2, k=4)
            for q in range(4):
                i2 = it % 2
                it += 1
                tsl = slice(PAD + q * 512, PAD + (q + 1) * 512)
                qsl = slice(q * 512, (q + 1) * 512)
                bga, bgb, bA, bB = next_bank(), next_bank(), next_bank(), next_bank()

                def _mm(e, bga=bga, bgb=bgb, bA=bA, bB=bB, wgv=wgv, wabv=wabv, tsl=tsl, qsl=qsl):
                    ins = None
                    for kc in range(KC):
                        e.matmul(bank_f[bga], wgv[:, kc, 0, :], nTv[:, kc, tsl], start=(kc == 0), stop=(kc == KC - 1))
                    for kc in range(KC):
                        e.matmul(bank_f[bgb], wgv[:, kc, 1, :], nTv[:, kc, tsl], start=(kc == 0), stop=(kc == KC - 1))
                    for kc in range(4):
                        e.matmul(bank_f[bA], wabv[:, 0, kc, :], yaTv[:, kc, qsl], start=(kc == 0), stop=(kc == 3))
                    for kc in range(4):
                        ins = e.matmul(bank_f[bB], wabv[:, 1, kc, :], ybTv[:, kc, qsl], start=(kc == 0), stop=(kc == 3))
                    return ins
                P.op("pe", _mm, reads=[wg.reg, wab.reg, nT.reg, yaT.reg, ybT.reg],
                     writes=[bank_reg[bga], bank_reg[bgb], bank_reg[bA], bank_reg[bB]])
                P.op("act", lambda e, bga=bga, i2=i2, c=c: e.activation(out=ta[i2].f, in_=bank_f[bga], func=AF.Tanh,
                                                                       bias=hbg.f[:, c:c + 1], scale=0.5),
                     reads=[bank_reg[bga], hbg.reg], writes=[ta[i2].reg])
                P.op("act", lambda e, bgb=bgb, i2=i2, c=c: e.activation(out=tb[i2].f, in_=bank_f[bgb], func=AF.Tanh,
                                                                       bias=hbg.f[:, 8 + c:9 + c], scale=0.5),
                     reads=[bank_reg[bgb], hbg.reg], writes=[tb[i2].reg])
                P.op("dve", lambda e, bA=bA, i2=i2: e.scalar_tensor_tensor(out=m1b[i2].f, in0=ta[i2].f, scalar=1.0,
                                                                          in1=bank_f[bA], op0=ALU.add, op1=ALU.mult),
                     reads=[ta[i2].reg, bank_reg[bA]], writes=[m1b[i2].reg])
                P.op("dve", lambda e, bB=bB, i2=i2: e.scalar_tensor_tensor(out=m2b[i2].f, in0=tb[i2].f, scalar=1.0,
                                                                          in1=bank_f[bB], op0=ALU.add, op1=ALU.mult),
                     reads=[tb[i2].reg, bank_reg[bB]], writes=[m2b[i2].reg])
                P.op("pool", lambda e, i2=i2, c=c, qsl=qsl: e.tensor_tensor(out=mTv[:, c, qsl], in0=m1b[i2].f,
                                                                           in1=m2b[i2].f, op=ALU.add),
                     reads=[m1b[i2].reg, m2b[i2].reg], writes=[mT.reg])
        if s == nseq - 1 and 2 in phases:
            top_save = A.top
            A.top = mP1
            p2w["Wup"] = Buf("Wup", KC * 4096 // 2)
            p2w["wup_r"] = [A.reg("Wup_c%d" % i, p2w["Wup"].st, p2w["Wup"].n) for i in range(4)]
            wupv_ = p2w["Wup"].b.rearrange("p (k n) -> p k n", k=KC)
            A.top = top_save
        xt = [Buf("xt%d" % i, D) for i in range(2)]
        h1 = [Buf("h1_%d" % i, D) for i in range(2)]
        junk2 = Buf("junk2", D // 2)
        ss2 = [Buf("ss2_%d" % i, 8) for i in range(2)]
        for t in range(16):
            i2 = t % 2
            P.dma("sp", xt[i2].f, x[s, t * 128:(t + 1) * 128, :], writes=[xt[i2].reg], semreg=xt[i2].reg)
            if s + 1 < nseq:
                if t == 0:
                    stage1_dma(s + 1, 0)
                    stage1_dma(s + 1, 1)
                if t in (9, 13):
                    stage1_dma(s + 1, 2 + (t - 9) // 4)
                if t % 4 == 3:
                    stage1_front(s + 1, t // 4)
                if t >= 6 and t % 4 == 2:
                    stage1_back(s + 1, (t - 6) // 4)
            if "Wup" in p2w and s == nseq - 1 and t % 4 == 1:
                i_ = t // 4
                wload(p2w["Wup"].b.rearrange("p (k n) -> p k n", k=KC)[:, :, i_ * 1024:(i_ + 1) * 1024],
                      w_up[:, i_ * 1024:(i_ + 1) * 1024].rearrange("(k p) n -> p k n", p=128), p2w["wup_r"][i_])
            pr = next_pair((0, 2))

            def _mm(e, pr=pr, t=t):
                ins = None
                for h in range(2):
                    for kc in range(KC):
                        ins = e.matmul(bank_f[pr + h], mTv[:, kc, t * 128:(t + 1) * 128], Wov[:, kc, h * 512:(h + 1) * 512],
                                       start=(kc == 0), stop=(kc == KC - 1))
                return ins
            P.op("pe", _mm, reads=[mT.reg, Wo.reg], writes=[bank_reg[pr], bank_reg[pr + 1]])
            yv = psum[pr // 2]
            breg = [bank_reg[pr], bank_reg[pr + 1]]
            P.op("act", lambda e, yv=yv, i2=i2: e.activation(out=junk2.b,
                                                            in_=yv[:, :], func=AF.Square, scale=0.5,
                                                            accum_out=ss2[i2].f[:, 0:1]),
                 reads=breg, writes=[junk2.reg, ss2[i2].reg])
            rstd_op(ss2[i2].f[:, 0:1], ss2[i2].f[:, 0:1], 1, 1.0 / D, [ss2[i2].reg], [ss2[i2].reg], post=0.5, mode="sqrt")
            P.op("dve", lambda e, yv=yv, i2=i2: e.scalar_tensor_tensor(out=h1[i2].f, in0=yv[:, :], scalar=ss2[i2].f[:, 0:1],
                                                                      in1=gpost.f, op0=ALU.mult, op1=ALU.mult),
                 reads=breg + [ss2[i2].reg, gpost.reg], writes=[h1[i2].reg])
            P.op("dve" if s == nseq - 1 else "pool",
                 lambda e, i2=i2: e.tensor_tensor(out=h1[i2].f, in0=h1[i2].f, in1=xt[i2].f, op=ALU.add),
                 reads=[h1[i2].reg, xt[i2].reg], writes=[h1[i2].reg])
            P.dma("sp", out[s, t * 128:(t + 1) * 128, :], h1[i2].f, reads=[h1[i2].reg], writes=[h1dram[s][t]],
                  semreg=h1[i2].reg)
            out_regs.append(h1[i2].reg)
        if s + 1 < nseq:
            stage1_back(s + 1, 3)
        A.release(mS)

    p2w = {}
    h1dram = [[P.reg("h1d_%d_%d" % (s, t)) for t in range(16)] for s in range(nseq)]

    if 1 in phases:
        for s in range(nseq):
            phase1(s)

    def phase2():
        A.release(mP1)
        if "Wup" in p2w:
            Wup = p2w["Wup"]
            wup_r = p2w["wup_r"]
            A.top = Wup.st + Wup.n
            Wupv = Wup.b.rearrange("p (k n) -> p k n", k=KC)
        else:
            Wup = Buf("Wup", KC * 4096 // 2)
            Wupv = Wup.b.rearrange("p (k n) -> p k n", k=KC)
            wup_r = [A.reg("Wup_c%d" % i, Wup.st, Wup.n) for i in range(4)]
            for i in range(4):
                wload(Wupv[:, :, i * 1024:(i + 1) * 1024],
                      w_up[:, i * 1024:(i + 1) * 1024].rearrange("(k p) n -> p k n", p=128), wup_r[i])
        Wdn = Buf("Wdn", 32 * D // 2)
        Wdnv = Wdn.b.rearrange("p (k n) -> p k n", k=32)
        wdn_r = [A.reg("Wdn_c%d" % i, Wdn.st, Wdn.n) for i in range(4)]
        hin = [Buf("hin%d" % i, D) for i in range(2)]
        hres = [Buf("hres%d" % i, D) for i in range(2)]
        ot = [Buf("ot%d" % i, D) for i in range(2)]
        n2T = Buf("n2T", KC * 512 // 2)
        n2Tv = n2T.b.rearrange("p (k t) -> p k t", k=KC)
        hidT = Buf("hidT", 32 * 512 // 2)
        hidTv = hidT.b.rearrange("p (k t) -> p k t", k=32)
        rl = [Buf("rl%d" % i, 256) for i in range(2)]
        junk3 = Buf("junk3", D // 2)
        ss3 = [Buf("ss3_%d" % i, 8) for i in range(2)]
        ss4 = [Buf("ss4_%d" % i, 8) for i in range(2)]
        quads = [(s, q) for s in range(nseq) for q in range(4)]

        def pro_front(s, q, tt):
            hb = hin[tt % 2]
            sb = ss3[tt % 2]
            t = 4 * q + tt
            P.dma("sp", hb.f, out[s, t * 128:(t + 1) * 128, :], reads=[h1dram[s][t]], writes=[hb.reg], semreg=hb.reg)
            P.op("act", lambda e, hb=hb, sb=sb: e.activation(out=junk3.b, in_=hb.f, func=AF.Square,
                                                            accum_out=sb.f[:, 0:1]),
                 reads=[hb.reg], writes=[junk3.reg, sb.reg])
            rstd_op(sb.f[:, 0:1], sb.f[:, 0:1], 1, 1.0 / D, [sb.reg], [sb.reg], mode="sqrt")
            P.op("dve", lambda e, hb=hb, sb=sb: e.tensor_scalar(out=hb.f, in0=hb.f, scalar1=sb.f[:, 0:1], scalar2=None,
                                                               op0=ALU.mult),
                 reads=[hb.reg, sb.reg], writes=[hb.reg])

        def pro_back(s, q, tt):
            hb = hin[tt % 2]
            for half in range(2):
                bk = next_bank((0, 1, 2, 3))

                def _tr(e, bk=bk, hb=hb, half=half):
                    ins = None
                    for k4 in range(4):
                        kc = half * 4 + k4
                        ins = e.transpose(bank_f[bk][:, k4 * 128:(k4 + 1) * 128], hb.f[:, kc * 128:(kc + 1) * 128],
                                          identF.f)
                    return ins
                P.op("pe", _tr, reads=[hb.reg, identF.reg], writes=[bank_reg[bk]])
                for k4 in range(4):
                    kc = half * 4 + k4
                    dst = n2Tv[:, kc, tt * 128:(tt + 1) * 128]
                    if k4 % 2 == 0:
                        P.op("act", lambda e, bk=bk, k4=k4, kc=kc, dst=dst: e.activation(
                            out=dst, in_=bank_f[bk][:, k4 * 128:(k4 + 1) * 128], func=AF.Identity,
                            scale=gpre2.f[:, kc:kc + 1]),
                            reads=[bank_reg[bk], gpre2.reg], writes=[n2T.reg])
                    else:
                        P.op("dve", lambda e, bk=bk, k4=k4, kc=kc, dst=dst: e.tensor_scalar(
                            out=dst, in0=bank_f[bk][:, k4 * 128:(k4 + 1) * 128], scalar1=gpre2.f[:, kc:kc + 1],
                            scalar2=None, op0=ALU.mult),
                            reads=[bank_reg[bk], gpre2.reg], writes=[n2T.reg])

        def up(s, q, nxt=None):
            for fc in range(32):
                if nxt is not None and fc in (10, 22):
                    pro_front(nxt[0], nxt[1], 0 if fc == 10 else 1)
                bk = next_bank((0, 1, 2, 3))
                rb = rl[fc % 2]

                def _mm(e, bk=bk, fc=fc):
                    ins = None
                    for kc in range(KC):
                        ins = e.matmul(bank_f[bk], Wupv[:, kc, fc * 128:(fc + 1) * 128], n2Tv[:, kc, :],
                                       start=(kc == 0), stop=(kc == KC - 1))
                    return ins
                P.op("pe", _mm, reads=[wup_r[fc // 8], n2T.reg], writes=[bank_reg[bk]])
                P.op("act", lambda e, bk=bk, rb=rb: e.activation(out=rb.b, in_=bank_f[bk], func=AF.Relu),
                     reads=[bank_reg[bk]], writes=[rb.reg])
                P.op("dve", lambda e, bk=bk, rb=rb, fc=fc: e.tensor_tensor(out=hidTv[:, fc, :], in0=bank_f[bk], in1=rb.b,
                                                                          op=ALU.mult),
                     reads=[bank_reg[bk], rb.reg], writes=[hidT.reg])

        def down(s, q, mid=None):
            for tt in range(4):
                if tt == 2 and mid is not None:
                    mid()
                i2 = tt % 2
                t = 4 * q + tt
                pr = next_pair((4, 6))
                P.dma("sp", hres[i2].f, out[s, t * 128:(t + 1) * 128, :], reads=[h1dram[s][t]], writes=[hres[i2].reg],
                      semreg=hres[i2].reg)

                def _mm(e, pr=pr, tt=tt):
                    ins = None
                    for h in range(2):
                        for fc in range(32):
                            ins = e.matmul(bank_f[pr + h], hidTv[:, fc, tt * 128:(tt + 1) * 128],
                                           Wdnv[:, fc, h * 512:(h + 1) * 512], start=(fc == 0), stop=(fc == 31))
                    return ins
                P.op("pe", _mm, reads=[hidT.reg] + wdn_r, writes=[bank_reg[pr], bank_reg[pr + 1]])
                yv = psum[pr // 2]
                breg = [bank_reg[pr], bank_reg[pr + 1]]
                P.op("act", lambda e, yv=yv, i2=i2: e.activation(out=junk3.b, in_=yv[:, :], func=AF.Square,
                                                                accum_out=ss4[i2].f[:, 0:1]),
                     reads=breg, writes=[junk3.reg, ss4[i2].reg])
                rstd_op(ss4[i2].f[:, 0:1], ss4[i2].f[:, 0:1], 1, 1.0 / D, [ss4[i2].reg], [ss4[i2].reg], mode="sqrt")
                P.op("dve", lambda e, yv=yv, i2=i2: e.scalar_tensor_tensor(out=ot[i2].f, in0=yv[:, :],
                                                                          scalar=ss4[i2].f[:, 0:1], in1=gpost2.f,
                                                                          op0=ALU.mult, op1=ALU.mult),
                     reads=breg + [ss4[i2].reg, gpost2.reg], writes=[ot[i2].reg])
                P.op("pool", lambda e, i2=i2: e.tensor_tensor(out=ot[i2].f, in0=ot[i2].f, in1=hres[i2].f, op=ALU.add),
                     reads=[ot[i2].reg, hres[i2].reg], writes=[ot[i2].reg])
                P.dma("sp", out[s, t * 128:(t + 1) * 128, :], ot[i2].f, reads=[ot[i2].reg, hres[i2].reg],
                      writes=[h1dram[s][t]], semreg=ot[i2].reg)
                out_regs.append(ot[i2].reg)

        for tt in range(4):
            pro_front(quads[0][0], quads[0][1], tt) if tt < 2 else None
        pro_back(quads[0][0], quads[0][1], 0)
        pro_back(quads[0][0], quads[0][1], 1)
        for tt in (2, 3):
            pro_front(quads[0][0], quads[0][1], tt)
        for tt in (2, 3):
            pro_back(quads[0][0], quads[0][1], tt)
        for i in range(4):
            wload(Wdnv[:, i * 8:(i + 1) * 8, :],
                  w_down[i * 1024:(i + 1) * 1024, :].rearrange("(k p) n -> p k n", p=128), wdn_r[i])
        for i, (s, q) in enumerate(quads):
            nxt = quads[i + 1] if i + 1 < len(quads) else None
            up(s, q, nxt)
            mid = None
            if nxt is not None:
                pro_back(nxt[0], nxt[1], 0)
                pro_back(nxt[0], nxt[1], 1)
                pro_front(nxt[0], nxt[1], 2)
                pro_front(nxt[0], nxt[1], 3)

                def mid(nxt=nxt):
                    pro_back(nxt[0], nxt[1], 2)
                    pro_back(nxt[0], nxt[1], 3)
            down(s, q, mid)

    if 2 in phases:
        phase2()

    seen = []
    for r in out_regs:
        if r not in seen:
            seen.append(r)
    P.final_wait("sp", seen)

    P.resolve()
    with stack:
        with nc.Block() as block:
            @block.tensor
            def _(e):
                P.emit_engine("pe", e)

            @block.scalar
            def _(e):
                P.emit_engine("act", e)

            @block.vector
            def _(e):
                P.emit_engine("dve", e)

            @block.gpsimd
            def _(e):
                P.emit_engine("pool", e)

            @block.sync
            def _(e):
                P.emit_engine("sp", e)
    return nc, A.peak


def _consts():
    c = {}
    c["c_ident"] = np.eye(128, dtype=np.float32)
    p = np.arange(128)[:, None]
    q = np.arange(128)[None, :]
    mA = (q >= p)
    mB = (q <= p)
    mA2 = mA & (p < 64)
    mB2 = mB & (p >= 64)
    mC = np.abs(q - p) <= 64
    c["c_masks"] = np.ascontiguousarray(np.concatenate([mA, mB, mA2, mB2, mC], axis=1).astype(np.float32))
    inv_freq = 500000.0 ** (-np.arange(0, 32, 2, dtype=np.float32) / 32.0)
    pos = (np.arange(16)[None, :, None] * 128 + np.arange(128)[:, None, None]).astype(np.float32)
    ang = pos * inv_freq[None, None, :].astype(np.float32)
    c["c_cos"] = np.ascontiguousarray(np.cos(ang).astype(np.float32).reshape(128, 256))
    c["c_sin"] = np.ascontiguousarray(np.sin(ang).astype(np.float32).reshape(128, 256))
    return c


def _prep_inputs(inputs):
    f = lambda a: np.ascontiguousarray(np.asarray(a, dtype=np.float32))
    shared = {
        "w_in": f(inputs["w_in"][0]),
        "w_spatial": f(inputs["w_spatial"][0]),
        "w_branch_a": f(inputs["w_branch_a"][0]),
        "w_branch_b": f(inputs["w_branch_b"][0]),
        "w_out": f(inputs["w_out"][0]),
        "w_up": f(inputs["w_up"][0]),
        "w_down": f(inputs["w_down"][0]),
        "c_gpre": f(np.asarray(inputs["norm_mix_pre"][0]).reshape(8, 128).T),
        "c_gpre2": f(np.asarray(inputs["norm_mlp_pre"][0]).reshape(8, 128).T),
        "c_lng": f(np.asarray(inputs["ln_v_gain"][0]).reshape(4, 128).T),
        "c_bgate": f(np.asarray(inputs["b_gate"][0]).reshape(16, 128).T),
        "c_lnb": f(np.asarray(inputs["ln_v_bias"][0]).reshape(1, 512)),
        "c_bsp": f(np.asarray(inputs["b_spatial"][0]).reshape(1, 512)),
        "c_gpost": f(np.broadcast_to(np.asarray(inputs["norm_mix_post"][0])[None, :], (128, D))),
        "c_gpost2": f(np.broadcast_to(np.asarray(inputs["norm_mlp_post"][0])[None, :], (128, D))),
    }
    shared.update(_consts())
    return shared


_CACHE = {}


def kernel(**inputs):
    x = np.asarray(inputs["x"], dtype=np.float32)
    shared = _prep_inputs(inputs)
    if "nc" not in _CACHE:
        _CACHE["nc"] = build(SEQ_PER_CORE)[0]
    nc = _CACHE["nc"]
    in_maps = []
    for c in range(NCORES):
        m = dict(shared)
        m["x"] = np.ascontiguousarray(x[c * SEQ_PER_CORE:(c + 1) * SEQ_PER_CORE])
        in_maps.append(m)
    res = run_bass_kernel_spmd(nc, in_maps, core_ids=list(range(NCORES)))
    outs = [np.asarray(r["out"], dtype=np.float32) for r in res.results]
    return np.concatenate(outs, axis=0)
```
